# Optimizing a Trainium2 kernel written in Bass

```python
import math
import jax, jax.numpy as jnp
from jax import lax
import numpy as np

D_MODEL = 1024
BATCH = 2
SEQ = 8192
DEPTH = 1

GRID_W = 64
PLE_DIM = 256
D_FF = 2816
HEAD_DIM = 64
N_Q_HEADS = 8
N_KV_HEADS = 2
GQA_GROUP = N_Q_HEADS // N_KV_HEADS
ATTN_WIDTH = N_Q_HEADS * HEAD_DIM
KV_WIDTH = N_KV_HEADS * HEAD_DIM
Q_BLOCK = 128
ROPE_THETA = 10000.0
ROPE_AXIS_DIM = HEAD_DIM // 2
SSM_WIDTH = D_MODEL // 2
SSM_GROUP = 16
SSM_N_GROUPS = SSM_WIDTH // SSM_GROUP
SSM_STATE = 64
N_DIRS = 2
IN_PROJ_WIDTH = ATTN_WIDTH + 2 * KV_WIDTH + SSM_WIDTH + 2 * D_MODEL
IN_SPLITS = [ATTN_WIDTH,
             ATTN_WIDTH + KV_WIDTH,
             ATTN_WIDTH + 2 * KV_WIDTH,
             ATTN_WIDTH + 2 * KV_WIDTH + SSM_WIDTH,
             ATTN_WIDTH + 2 * KV_WIDTH + SSM_WIDTH + D_MODEL]
NORM_EPS = 1e-6

kernel_name = "hybrid_gqa_s5_macaron_encoder_layer"


def rms_norm(x, g):
    xf = x.astype(jnp.float32)
    y = xf * lax.rsqrt(jnp.mean(xf * xf, axis=-1, keepdims=True) + NORM_EPS)
    return (y * g.astype(jnp.float32)).astype(x.dtype)


def swiglu(x, w_gate, w_up, w_down):
    return (jax.nn.silu(x @ w_gate) * (x @ w_up)) @ w_down


def axial_rope_tables(seq_len):
    rows = seq_len // GRID_W
    row = jnp.repeat(jnp.arange(rows, dtype=jnp.int32), GRID_W).astype(jnp.float32)
    col = jnp.tile(jnp.arange(GRID_W, dtype=jnp.int32), rows).astype(jnp.float32)
    freqs = ROPE_THETA ** (-jnp.arange(0, ROPE_AXIS_DIM, 2, dtype=jnp.float32) / ROPE_AXIS_DIM)
    ang = jnp.concatenate([row[:, None] * freqs, col[:, None] * freqs], axis=-1)
    ang = jnp.concatenate([ang, ang], axis=-1)
    return jnp.cos(ang), jnp.sin(ang)


def apply_rope(x, cos, sin):
    x1, x2 = jnp.split(x, 2, axis=-1)
    rot = jnp.concatenate([-x2, x1], axis=-1)
    return x * cos[None, :, None, :] + rot * sin[None, :, None, :]


def block_gqa_attention(q, k, v):
    b, s = q.shape[0], q.shape[1]
    nb = s // Q_BLOCK
    qb = q.reshape(b, nb, Q_BLOCK, N_KV_HEADS, GQA_GROUP, HEAD_DIM).transpose(1, 0, 2, 3, 4, 5)
    scale = HEAD_DIM ** -0.5

    def one_block(q_blk):
        scores = jnp.einsum('bqkgd,bskd->bkgqs', q_blk, k) * scale
        probs = jax.nn.softmax(scores, axis=-1)
        return jnp.einsum('bkgqs,bskd->bqkgd', probs, v)

    out = lax.map(one_block, qb)
    return out.transpose(1, 0, 2, 3, 4, 5).reshape(b, s, ATTN_WIDTH)


def s5_scan(u_g, lam_re, lam_im, log_step, b_re, b_im, c_re, c_im, reverse):
    f32 = jnp.float32
    lam = lax.complex(lam_re.astype(f32), lam_im.astype(f32))
    delta = jnp.exp(log_step.astype(f32))[:, None]
    lam_bar = jnp.exp(lam * delta)
    b_mat = lax.complex(b_re.astype(f32), b_im.astype(f32))
    b_bar = ((lam_bar - 1.0) / lam)[..., None] * b_mat
    bu = jnp.einsum('gph,bsgh->bsgp', b_bar, u_g.astype(jnp.complex64))
    a = jnp.broadcast_to(lam_bar, bu.shape)

    def combine(e1, e2):
        a1, x1 = e1
        a2, x2 = e2
        return a1 * a2, a2 * x1 + x2

    _, states = lax.associative_scan(combine, (a, bu), reverse=reverse, axis=1)
    c_mat = lax.complex(c_re.astype(f32), c_im.astype(f32))
    return jnp.einsum('ghp,bsgp->bsgh', c_mat, states).real


def setup_inputs(seed: int = 0) -> dict:
    key = jax.random.key(seed)
    ks = iter(jax.random.split(key, 40))
    f32 = jnp.float32

    def w(shape, fan_in):
        return jax.random.normal(next(ks), shape, f32) * (fan_in ** -0.5)

    def gain(shape):
        return 1.0 + 0.02 * jax.random.normal(next(ks), shape, f32)

    L = DEPTH
    G, P, H = SSM_N_GROUPS, SSM_STATE, SSM_GROUP
    n_idx = jnp.arange(P, dtype=f32)
    lam_re = -0.5 + 0.01 * jax.random.normal(next(ks), (L, N_DIRS, G, P), f32)
    lam_im = math.pi * n_idx + 0.01 * jax.random.normal(next(ks), (L, N_DIRS, G, P), f32)
    log_step = jax.random.uniform(next(ks), (L, N_DIRS, G), f32,
                                  minval=math.log(0.001), maxval=math.log(0.1))
    inv_sqrt2 = 0.5 ** 0.5
    return {
        "x": jax.random.normal(next(ks), (BATCH, SEQ, D_MODEL), f32),
        "p": jax.random.normal(next(ks), (DEPTH, BATCH, SEQ, PLE_DIM), f32),
        "ffn1_norm": gain((L, D_MODEL)),
        "ffn1_w_gate": w((L, D_MODEL, D_FF), D_MODEL),
        "ffn1_w_up": w((L, D_MODEL, D_FF), D_MODEL),
        "ffn1_w_down": w((L, D_FF, D_MODEL), D_FF),
        "mix_norm": gain((L, D_MODEL)),
        "w_in": w((L, D_MODEL, IN_PROJ_WIDTH), D_MODEL),
        "q_norm": gain((L, HEAD_DIM)),
        "k_norm": gain((L, HEAD_DIM)),
        "ssm_lambda_re": lam_re,
        "ssm_lambda_im": lam_im,
        "ssm_log_step": log_step,
        "ssm_b_re": w((L, N_DIRS, G, P, H), H) * inv_sqrt2,
        "ssm_b_im": w((L, N_DIRS, G, P, H), H) * inv_sqrt2,
        "ssm_c_re": w((L, N_DIRS, G, H, P), P) * inv_sqrt2,
        "ssm_c_im": w((L, N_DIRS, G, H, P), P) * inv_sqrt2,
        "ssm_d": 0.5 * jax.random.normal(next(ks), (L, SSM_WIDTH), f32),
        "ssm_glu_w": w((L, SSM_WIDTH, SSM_WIDTH), SSM_WIDTH),
        "ssm_glu_b": 0.01 * jax.random.normal(next(ks), (L, SSM_WIDTH), f32),
        "w_attn_branch": w((L, ATTN_WIDTH, D_MODEL), ATTN_WIDTH),
        "w_ssm_branch": w((L, SSM_WIDTH, D_MODEL), SSM_WIDTH),
        "w_out": w((L, D_MODEL, D_MODEL), D_MODEL),
        "ffn2_norm": gain((L, D_MODEL)),
        "ffn2_w_gate": w((L, D_MODEL, D_FF), D_MODEL),
        "ffn2_w_up": w((L, D_MODEL, D_FF), D_MODEL),
        "ffn2_w_down": w((L, D_FF, D_MODEL), D_FF),
        "ple_norm": gain((L, D_MODEL)),
        "ple_w_gate": w((L, D_MODEL, D_MODEL), D_MODEL),
        "ple_w_proj": w((L, PLE_DIM, D_MODEL), PLE_DIM),
    }


def reference(x, p, ffn1_norm, ffn1_w_gate, ffn1_w_up, ffn1_w_down, mix_norm, w_in,
              q_norm, k_norm, ssm_lambda_re, ssm_lambda_im, ssm_log_step, ssm_b_re,
              ssm_b_im, ssm_c_re, ssm_c_im, ssm_d, ssm_glu_w, ssm_glu_b, w_attn_branch,
              w_ssm_branch, w_out, ffn2_norm, ffn2_w_gate, ffn2_w_up, ffn2_w_down,
              ple_norm, ple_w_gate, ple_w_proj):
    f32 = jnp.float32
    b, s, _ = x.shape
    cos, sin = axial_rope_tables(s)
    h = x
    for i in range(DEPTH):
        h = h + 0.5 * swiglu(rms_norm(h, ffn1_norm[i]), ffn1_w_gate[i], ffn1_w_up[i], ffn1_w_down[i])

        u = rms_norm(h, mix_norm[i])
        proj = u @ w_in[i]
        q, k, v, x_ssm, g_attn, g_ssm = jnp.split(proj, IN_SPLITS, axis=-1)

        q = apply_rope(rms_norm(q.reshape(b, s, N_Q_HEADS, HEAD_DIM).astype(f32), q_norm[i]), cos, sin)
        k = apply_rope(rms_norm(k.reshape(b, s, N_KV_HEADS, HEAD_DIM).astype(f32), k_norm[i]), cos, sin)
        v = v.reshape(b, s, N_KV_HEADS, HEAD_DIM).astype(f32)
        y_attn = block_gqa_attention(q, k, v).astype(x.dtype) @ w_attn_branch[i]

        u_g = x_ssm.reshape(b, s, SSM_N_GROUPS, SSM_GROUP)
        y_fwd = s5_scan(u_g, ssm_lambda_re[i, 0], ssm_lambda_im[i, 0], ssm_log_step[i, 0],
                        ssm_b_re[i, 0], ssm_b_im[i, 0], ssm_c_re[i, 0], ssm_c_im[i, 0], reverse=False)
        y_bwd = s5_scan(u_g, ssm_lambda_re[i, 1], ssm_lambda_im[i, 1], ssm_log_step[i, 1],
                        ssm_b_re[i, 1], ssm_b_im[i, 1], ssm_c_re[i, 1], ssm_c_im[i, 1], reverse=True)
        y_s5 = (y_fwd + y_bwd).reshape(b, s, SSM_WIDTH) + ssm_d[i].astype(f32) * x_ssm.astype(f32)
        z = jax.nn.gelu(y_s5).astype(x.dtype)
        z = z * jax.nn.sigmoid(z @ ssm_glu_w[i] + ssm_glu_b[i])
        y_ssm = z @ w_ssm_branch[i]

        merged = jax.nn.sigmoid(g_attn) * y_attn + jax.nn.sigmoid(g_ssm) * y_ssm
        h = h + merged @ w_out[i]

        h = h + 0.5 * swiglu(rms_norm(h, ffn2_norm[i]), ffn2_w_gate[i], ffn2_w_up[i], ffn2_w_down[i])

        ple = p[i] @ ple_w_proj[i]
        h = h + jax.nn.sigmoid(rms_norm(h, ple_norm[i]) @ ple_w_gate[i]) * ple
    return h
```

```python
import numpy as np
import ml_dtypes
from contextlib import ExitStack
import concourse.bass as bass
import concourse.mybir as mybir
from concourse.bass_utils import run_bass_kernel_spmd

F32 = mybir.dt.float32
BF16 = mybir.dt.bfloat16
I32 = mybir.dt.int32
AF = mybir.ActivationFunctionType
ALU = mybir.AluOpType


class Res:
    __slots__ = ("name", "last_w", "readers")

    def __init__(self, name=""):
        self.name = name
        self.last_w = None
        self.readers = {}


class Sched:
    ENGS = ("tensor", "vector", "scalar", "gpsimd", "sync")

    def __init__(self, nc, stack):
        self.nc = nc
        self.stack = stack
        self.sems = {}
        self.count = {}
        self.prog = {e: [] for e in self.ENGS}
        self.seen = {e: {} for e in self.ENGS}
        for e in self.ENGS:
            self.sems[e] = stack.enter_context(nc.semaphore("s_" + e))
            self.count[e] = 0
        self.ndma = 0
        self.enabled = True

    def dma_sem(self, name=None):
        key = "dma%d" % self.ndma
        self.ndma += 1
        self.sems[key] = self.stack.enter_context(self.nc.semaphore("s_" + key))
        self.count[key] = 0
        return key

    def _wait(self, eng, key, val):
        if val <= 0:
            return
        if self.seen[eng].get(key, 0) >= val:
            return
        self.seen[eng][key] = val
        sem = self.sems[key]
        self.prog[eng].append(lambda e, sem=sem, val=val: e.wait_ge(sem, val))

    def _deps(self, eng, reads, writes):
        deps = []
        for r in reads:
            if r.last_w is not None:
                deps.append(r.last_w)
        for w in writes:
            if w.last_w is not None:
                deps.append(w.last_w)
            deps.extend(w.readers.items())
        for key, val in deps:
            if key == eng and eng == "tensor":
                continue
            self._wait(eng, key, val)

    def op(self, eng, fn, reads=(), writes=()):
        if not self.enabled:
            return 0
        self._deps(eng, reads, writes)
        self.count[eng] += 1
        v = self.count[eng]
        sem = self.sems[eng]
        self.prog[eng].append(lambda e, fn=fn, sem=sem: fn(e).then_inc(sem, 1))
        for r in reads:
            r.readers[eng] = v
        for w in writes:
            w.last_w = (eng, v)
            w.readers = {}
        return v

    def dma(self, eng, dkey, out, in_, reads=(), writes=(), **kw):
        if not self.enabled:
            return 0
        self._deps(eng, reads, writes)
        self.count[dkey] += 16
        v = self.count[dkey]
        sem = self.sems[dkey]
        self.prog[eng].append(
            lambda e, out=out, in_=in_, sem=sem, kw=kw: e.dma_start(out=out, in_=in_, **kw).then_inc(sem, 16))
        for r in reads:
            r.readers[dkey] = v
        for w in writes:
            w.last_w = (dkey, v)
            w.readers = {}
        return v

    def coll(self, dkey, fn, reads=(), writes=()):
        eng = "gpsimd"
        if not self.enabled:
            return 0
        self._deps(eng, reads, writes)
        self.count[dkey] += 1
        v = self.count[dkey]
        sem = self.sems[dkey]
        self.prog[eng].append(lambda e, fn=fn, sem=sem: fn(e).then_inc(sem, 1))
        for r in reads:
            r.readers[dkey] = v
        for w in writes:
            w.last_w = (dkey, v)
            w.readers = {}
        return v

    def raw(self, eng, fn):
        self.prog[eng].append(fn)

    def wait_all(self, eng):
        for key, val in self.count.items():
            if key != eng or True:
                self._wait(eng, key, val)

    def barrier(self):
        snap = dict(self.count)
        for e in self.ENGS:
            for key, val in snap.items():
                self._wait(e, key, val)

    def emit(self):
        nc = self.nc
        with nc.Block() as block:
            @block.tensor
            def _(eng):
                for f in self.prog["tensor"]:
                    f(eng)

            @block.vector
            def _(eng):
                for f in self.prog["vector"]:
                    f(eng)

            @block.scalar
            def _(eng):
                for f in self.prog["scalar"]:
                    f(eng)

            @block.gpsimd
            def _(eng):
                for f in self.prog["gpsimd"]:
                    f(eng)

            @block.sync
            def _(eng):
                for f in self.prog["sync"]:
                    f(eng)

NT = 2048
DM = 1024
DFF = 2816
NJ = 22
TWO_PI = 2.0 * np.pi
DEBUG = {}
SS = dict(A=1, B=1, C=1, D=1)
PH = dict(ffn1=1, mix1=1, mix2=1, ssm=1, attn=1, merge=1, ffn2=1, ple=1)


def build(dbg=False):
    nc = bass.Bass("TRN2", target_bir_lowering=False)
    di = lambda name, shape, dt=F32: nc.dram_tensor(name, shape, dt, kind="ExternalInput").ap()
    xT = di("xT", [DM, NT]); pT = di("pT", [256, NT]); pos = di("pos", [2, NT])
    vecs_d = di("vecs", [128, 72]); cf32_d = di("cf32", [128, 5 * 128 + 26 + 12])
    sel_d = di("sel", [128, 64 * 128]); selT_d = di("selT", [128, 64 * 128])
    ssmp_d = di("ssmp", [128, 96 + 2048])
    w1g = di("w1g", [DM, DFF]); w1u = di("w1u", [DM, DFF]); w1d = di("w1d", [DFF, DM])
    w2g = di("w2g", [DM, DFF]); w2u = di("w2u", [DM, DFF]); w2d = di("w2d", [DFF, DM])
    w_in = di("w_in", [DM, 3328]); wglu = di("wglu", [512, 512]); wab = di("wab", [512, DM])
    wsb = di("wsb", [512, DM]); wout = di("wout", [DM, DM]); wpg = di("wpg", [DM, DM]); wpp = di("wpp", [256, DM])
    outT = nc.dram_tensor("outT", [DM, NT], F32, kind="ExternalOutput").ap()
    k_loc = nc.dram_tensor("k_loc", [128, 2048], BF16).ap()
    k_all = nc.dram_tensor("k_all", [512, 2048], BF16).ap()
    v_loc = nc.dram_tensor("v_loc", [128, 16 * 194], BF16).ap()
    v_all = nc.dram_tensor("v_all", [512, 16 * 194], BF16).ap()
    st_loc = nc.dram_tensor("st_loc", [128, 64], F32).ap()
    st_all = nc.dram_tensor("st_all", [512, 64], F32).ap()
    h_spill = nc.dram_tensor("h_spill", [DM, NT], F32).ap()
    dbg_out = {}

    with ExitStack() as st:
        S = Sched(nc, st)
        _uid = [0]
        def sbt(stack, name, shape, dt):
            _uid[0] += 1
            return stack.enter_context(nc.sbuf_tensor("sb%d_%s" % (_uid[0], name), shape, dt))
        V = lambda fn, r=(), w=(): S.op("vector", fn, r, w)
        A = lambda fn, r=(), w=(): S.op("scalar", fn, r, w)
        G = lambda fn, r=(), w=(): S.op("gpsimd", fn, r, w)
        T = lambda fn, r=(), w=(): S.op("tensor", fn, r, w)
        def mm(ps, lhsT, rhs, start, stop, r=(), w=()):
            T(lambda e: e.matmul(ps, lhsT=lhsT, rhs=rhs, start=start, stop=stop), r, w)

        d_dbg = S.dma_sem()
        def dump(name, src, shape, reads):
            if not DEBUG.get(name):
                return
            o = nc.dram_tensor("dbg_" + name, list(shape), F32, kind="ExternalOutput").ap()
            en = S.enabled; S.enabled = True
            S.dma("gpsimd", d_dbg, o, src, reads=reads)
            S.enabled = en
        hT = sbt(st, "hT", [128, 8, NT], F32); r_h = Res()
        vecs = sbt(st, "vecs", [128, 72], F32); r_vecs = Res()
        cf32 = sbt(st, "cf32", [128, 5 * 128 + 38], F32); r_cf = Res()
        cbf = sbt(st, "cbf", [128, 3 * 128], BF16); r_cbf = Res()
        identf = cf32[:, 0:128]; rotm = cf32[:, 128:256]; onesf = cf32[:, 256:384]
        maskf = cf32[:, 384:512]; maskb = cf32[:, 512:640]; Etab = cf32[:, 640:666]; selc = cf32[:, 666:678]
        identb = cbf[:, 0:128]; onesb = cbf[:, 128:256]; blkones = cbf[:, 256:384]
        d_in = S.dma_sem()
        S.dma("sync", d_in, hT[:], xT.rearrange("(c p) t -> p c t", p=128), writes=[r_h])
        S.dma("sync", d_in, vecs[:], vecs_d, writes=[r_vecs])
        S.dma("sync", d_in, cf32[:], cf32_d, writes=[r_cf])
        V(lambda e: e.tensor_copy(out=cbf[:, 0:128], in_=cf32[:, 0:128]), [r_cf], [r_cbf])
        V(lambda e: e.tensor_copy(out=cbf[:, 128:256], in_=cf32[:, 256:384]), [r_cf], [r_cbf])
        V(lambda e: e.memset(cbf[:, 256:384], 0.0), [], [r_cbf])
        V(lambda e: e.memset(cbf[0:64, 256:320], 1.0), [], [r_cbf])
        V(lambda e: e.memset(cbf[64:128, 320:384], 1.0), [], [r_cbf])

        pbig = []
        pbanks = []
        for i in range(3):
            t_ = st.enter_context(nc.psum_tensor("pbig%d" % i, [128, 1024], F32))
            pbig.append((t_, Res()))
            pbanks.append((t_[:, 0:512], Res())); pbanks.append((t_[:, 512:1024], Res()))
        pbanks.append((st.enter_context(nc.psum_tensor("pb6", [128, 512], F32)), Res()))
        pbf = st.enter_context(nc.psum_tensor("pbf", [128, 1024], BF16)); r_pbf = Res()
        pctr = [0]
        def nps():
            p = pbanks[pctr[0] % 7]; pctr[0] += 1
            return p

        class WStream:
            def __init__(self, stack, name, shape, n=2, dt=BF16):
                self.slots = [(sbt(stack, "%s%d" % (name, i), shape, dt), Res(), S.dma_sem()) for i in range(n)]
                self.i = 0
            def next(self):
                s = self.slots[self.i % len(self.slots)]; self.i += 1
                return s
        def wload(slot, dst, src):
            tile, res, dk = slot
            S.dma("gpsimd", dk, dst, src, writes=[res])

        def rmsnorm(stack_tmp, xn, r_xn, t0, gcol, tmp):
            sq, r_sq, rs, r_rs = tmp["sq"], tmp["r_sq"], tmp["rs"], tmp["r_rs"]
            A(lambda e: e.activation(out=sq[:], in_=hT[:, :, t0:t0 + 512], func=AF.Square), [r_h], [r_sq])
            ps, r_ps = nps()
            for c in range(8):
                mm(ps[:, :], onesb, sq[:, c, :], c == 0, c == 7, [r_sq, r_cbf], [r_ps])
            A(lambda e: e.activation(out=rs[:], in_=ps[:, :], func=AF.Sqrt, bias=1e-6, scale=1.0 / DM), [r_ps], [r_rs])
            V(lambda e: e.reciprocal(out=rs[:], in_=rs[:]), [r_rs], [r_rs])
            for c in range(8):
                V(lambda e, c=c: e.scalar_tensor_tensor(out=xn[:, c, :], in0=hT[:, c, t0:t0 + 512],
                                                       scalar=vecs[:, gcol + c:gcol + c + 1], in1=rs[:],
                                                       op0=ALU.mult, op1=ALU.mult), [r_h, r_rs, r_vecs], [r_xn])

        def mk_norm_tmp(stack):
            return dict(sq=sbt(stack, "sq", [128, 8, 512], BF16), r_sq=Res(),
                        rs=sbt(stack, "rs", [128, 512], F32), r_rs=Res())

        def ffn(wg, wu, wd, gcol, tag):
            with ExitStack() as ph:
                tmp = mk_norm_tmp(ph)
                xn = sbt(ph, "xn" + tag, [128, 8, 1024], BF16); r_xn = Res()
                hid = sbt(ph, "hid" + tag, [128, NJ, 1024], BF16); r_hid = Res()
                sg = [(sbt(ph, "sg%s%d" % (tag, i), [128, 512], F32), Res()) for i in range(2)]
                wgs = WStream(ph, "wg" + tag, [128, 8, 256]); wus = WStream(ph, "wu" + tag, [128, 8, 256])
                wds = WStream(ph, "wd" + tag, [128, NJ, 256])
                wg_v = wg.rearrange("(c p) n -> p c n", p=128); wu_v = wu.rearrange("(c p) n -> p c n", p=128)
                wd_v = wd.rearrange("(j p) n -> p j n", p=128)
                for tt in range(2):
                    for sub in range(2):
                        rmsnorm(ph, xn[:, :, sub * 512:(sub + 1) * 512], r_xn, tt * 1024 + sub * 512, gcol, tmp)
                    k = 0
                    for jb in range(11):
                        sg_, su_ = wgs.next(), wus.next()
                        wload(sg_, sg_[0][:], wg_v[:, :, jb * 256:(jb + 1) * 256])
                        wload(su_, su_[0][:], wu_v[:, :, jb * 256:(jb + 1) * 256])
                        for jj in range(2):
                            j = jb * 2 + jj
                            for sub in range(2):
                                pg, r_pg = nps(); pu, r_pu = nps()
                                xs = xn[:, :, sub * 512:(sub + 1) * 512]
                                for c in range(8):
                                    mm(pg[:, :], sg_[0][:, c, jj * 128:(jj + 1) * 128], xs[:, c, :], c == 0, c == 7, [sg_[1], r_xn], [r_pg])
                                for c in range(8):
                                    mm(pu[:, :], su_[0][:, c, jj * 128:(jj + 1) * 128], xs[:, c, :], c == 0, c == 7, [su_[1], r_xn], [r_pu])
                                sgt, r_sgt = sg[k % 2]; k += 1
                                A(lambda e, sgt=sgt, pg=pg: e.activation(out=sgt[:], in_=pg[:, :], func=AF.Silu), [r_pg], [r_sgt])
                                V(lambda e, sgt=sgt, pu=pu, j=j, sub=sub: e.tensor_tensor(
                                    out=hid[:, j, sub * 512:(sub + 1) * 512], in0=sgt[:], in1=pu[:, :], op=ALU.mult),
                                    [r_sgt, r_pu], [r_hid])
                    for ob in range(4):
                        sd_ = wds.next()
                        wload(sd_, sd_[0][:], wd_v[:, :, ob * 256:(ob + 1) * 256])
                        for oo in range(2):
                            o = ob * 2 + oo
                            for sub in range(2):
                                py, r_py = nps()
                                for j in range(NJ):
                                    mm(py[:, :], sd_[0][:, j, oo * 128:(oo + 1) * 128], hid[:, j, sub * 512:(sub + 1) * 512],
                                       j == 0, j == NJ - 1, [sd_[1], r_hid], [r_py])
                                t0 = tt * 1024 + sub * 512
                                V(lambda e, py=py, o=o, t0=t0: e.scalar_tensor_tensor(
                                    out=hT[:, o, t0:t0 + 512], in0=py[:, :], scalar=0.5, in1=hT[:, o, t0:t0 + 512],
                                    op0=ALU.mult, op1=ALU.add), [r_py, r_h], [r_h])
                S.barrier()

        def sin_of(stack, out, ang, shape, r_in, r_out, shift, tag):
            u = sbt(stack, "rr_u" + tag, shape, F32); ki = sbt(stack, "rr_k" + tag, shape, I32)
            kf = sbt(stack, "rr_f" + tag, shape, F32); r_t = Res()
            V(lambda e: e.tensor_scalar(out=u[:], in0=ang, scalar1=1.0 / TWO_PI, scalar2=shift / TWO_PI,
                                        op0=ALU.mult, op1=ALU.add), r_in, [r_t])
            V(lambda e: e.tensor_copy(out=ki[:], in_=u[:]), [r_t], [r_t])
            V(lambda e: e.tensor_copy(out=kf[:], in_=ki[:]), [r_t], [r_t])
            V(lambda e: e.tensor_tensor(out=u[:], in0=u[:], in1=kf[:], op=ALU.subtract), [r_t], [r_t])
            V(lambda e: e.tensor_scalar(out=kf[:], in0=u[:], scalar1=0.5, scalar2=-1.0, op0=ALU.is_gt, op1=ALU.mult), [r_t], [r_t])
            V(lambda e: e.tensor_tensor(out=u[:], in0=u[:], in1=kf[:], op=ALU.add), [r_t], [r_t])
            V(lambda e: e.tensor_scalar(out=kf[:], in0=u[:], scalar1=-0.5, scalar2=1.0, op0=ALU.is_lt, op1=ALU.mult), [r_t], [r_t])
            V(lambda e: e.tensor_tensor(out=u[:], in0=u[:], in1=kf[:], op=ALU.add), [r_t], [r_t])
            V(lambda e: e.tensor_scalar(out=u[:], in0=u[:], scalar1=0.5, scalar2=-0.5, op0=ALU.min, op1=ALU.max), [r_t], [r_t])
            A(lambda e: e.activation(out=out, in_=u[:], func=AF.Sin, scale=TWO_PI), [r_t], r_out)

        def qk_finish(ps, r_ps, gcol, cosT, sinT, r_cs, out_bf, r_out, tmp):
            sq, r_sq, rs, r_rs, qn, r_qn, t1, r_t1 = tmp
            A(lambda e: e.activation(out=sq[:], in_=ps[:, :], func=AF.Square), [r_ps], [r_sq])
            p2, r_p2 = nps()
            mm(p2[:, :], blkones, sq[:], True, True, [r_sq, r_cbf], [r_p2])
            A(lambda e: e.activation(out=rs[:], in_=p2[:, :], func=AF.Sqrt, bias=1e-6, scale=1.0 / 64), [r_p2], [r_rs])
            V(lambda e: e.reciprocal(out=rs[:], in_=rs[:]), [r_rs], [r_rs])
            V(lambda e: e.scalar_tensor_tensor(out=qn[:], in0=ps[:, :], scalar=vecs[:, gcol:gcol + 1], in1=rs[:],
                                               op0=ALU.mult, op1=ALU.mult), [r_ps, r_rs, r_vecs], [r_qn])
            p3, r_p3 = nps()
            mm(p3[:, :], rotm, qn[:], True, True, [r_qn, r_cf], [r_p3])
            V(lambda e: e.tensor_tensor(out=t1[:], in0=p3[:, :], in1=sinT, op=ALU.mult), [r_p3, r_cs], [r_t1])
            V(lambda e: e.tensor_tensor(out=qn[:], in0=qn[:], in1=cosT, op=ALU.mult), [r_qn, r_cs], [r_qn])
            V(lambda e: e.tensor_tensor(out=out_bf, in0=qn[:], in1=t1[:], op=ALU.add), [r_qn, r_t1], [r_out])

        def rope_tables(stack, t0, cosT, sinT, r_cs, posb, r_posb, ang, r_ang, d_pos, tag):
            S.dma("sync", d_pos, posb[:, 0, :], pos[0:1, t0:t0 + 512].partition_broadcast(128)[:, 0, :], writes=[r_posb])
            S.dma("sync", d_pos, posb[:, 1, :], pos[1:2, t0:t0 + 512].partition_broadcast(128)[:, 0, :], writes=[r_posb])
            V(lambda e: e.tensor_scalar(out=ang[:], in0=posb[:, 0, :], scalar1=vecs[:, 38:39], scalar2=None, op0=ALU.mult),
              [r_posb, r_vecs], [r_ang])
            V(lambda e: e.scalar_tensor_tensor(out=ang[:], in0=posb[:, 1, :], scalar=vecs[:, 39:40], in1=ang[:],
                                               op0=ALU.mult, op1=ALU.add), [r_posb, r_vecs, r_ang], [r_ang])
            with ExitStack() as tmp:
                sin_of(tmp, sinT[:], ang[:], [128, 512], [r_ang], [r_cs], 0.0, tag + "s")
                sin_of(tmp, cosT[:], ang[:], [128, 512], [r_ang], [r_cs], np.pi / 2, tag + "c")
                S.barrier()

        win_v = w_in.rearrange("(c p) n -> p c n", p=128)

        S.enabled = bool(PH['ffn1'])
        ffn(w1g, w1u, w1d, 0, "a")

        mix = ExitStack()
        zbuf = sbt(mix, "zbuf", [128, 4, NT], BF16); r_z = Res()
        S.enabled = bool(PH['mix1'])
        qT = sbt(mix, "qT", [128, 4, NT], BF16); r_q = Res()
        with ExitStack() as ph:
            cosT = sbt(ph, "cosT", [128, NT], F32); sinT = sbt(ph, "sinT", [128, NT], F32); r_cs = Res()
            with ExitStack() as tb0:
                posb = sbt(tb0, "posb", [128, 2, NT], F32); r_posb = Res(); ang = sbt(tb0, "ang", [128, NT], F32); r_ang = Res()
                d_pos = S.dma_sem()
                S.dma("sync", d_pos, posb[:, 0, :], pos[0:1, :].partition_broadcast(128)[:, 0, :], writes=[r_posb])
                S.dma("sync", d_pos, posb[:, 1, :], pos[1:2, :].partition_broadcast(128)[:, 0, :], writes=[r_posb])
                V(lambda e: e.tensor_scalar(out=ang[:], in0=posb[:, 0, :], scalar1=vecs[:, 38:39], scalar2=None, op0=ALU.mult),
                  [r_posb, r_vecs], [r_ang])
                V(lambda e: e.scalar_tensor_tensor(out=ang[:], in0=posb[:, 1, :], scalar=vecs[:, 39:40], in1=ang[:],
                                                   op0=ALU.mult, op1=ALU.add), [r_posb, r_vecs, r_ang], [r_ang])
                with ExitStack() as tb_:
                    sin_of(tb_, sinT[:], ang[:], [128, NT], [r_ang], [r_cs], 0.0, "rs")
                    S.barrier()
                with ExitStack() as tb_:
                    sin_of(tb_, cosT[:], ang[:], [128, NT], [r_ang], [r_cs], np.pi / 2, "rc")
                    S.barrier()
            tmpn = mk_norm_tmp(ph)
            xn = sbt(ph, "xn1", [128, 8, 512], BF16); r_xn = Res()
            wk = sbt(ph, "wkvx", [128, 8, 768], BF16); r_wk = Res(); d_wk = S.dma_sem()
            S.dma("gpsimd", d_wk, wk[:, :, 0:384], win_v[:, :, 512:896], writes=[r_wk])
            S.dma("gpsimd", d_wk, wk[:, :, 384:768], win_v[:, :, 896:1280], writes=[r_wk])
            wq = sbt(ph, "wq", [128, 8, 4, 128], BF16); r_wq = Res(); d_wq = S.dma_sem()
            for c in range(4):
                for hh in range(2):
                    col = (hh * 4 + c) * 64
                    S.dma("gpsimd", d_wq, wq[:, :, c, hh * 64:(hh + 1) * 64], win_v[:, :, col:col + 64], writes=[r_wq])
            kloc = sbt(ph, "kloc", [128, NT], BF16); r_kloc = Res()
            vloc = sbt(ph, "vloc", [128, 16, 194], BF16); r_vloc = Res()
            V(lambda e: e.memset(vloc[:], 0.0), [], [r_vloc])
            V(lambda e: e.memset(vloc[:, :, 64:65], 1.0), [], [r_vloc])
            V(lambda e: e.memset(vloc[:, :, 66:67], 1.0), [], [r_vloc])
            qkts = [(sbt(ph, "qk_sq%d" % i, [128, 512], BF16), Res(), sbt(ph, "qk_rs%d" % i, [128, 512], F32), Res(),
                     sbt(ph, "qk_qn%d" % i, [128, 512], F32), Res(), sbt(ph, "qk_t1%d" % i, [128, 512], F32), Res()) for i in range(2)]
            nq = 0
            for s4 in range(4):
                t0 = s4 * 512
                rmsnorm(ph, xn, r_xn, t0, 8, tmpn)
                ps, r_ps = nps()
                for c in range(8):
                    mm(ps[:, :], wk[:, c, 0:128], xn[:, c, :], c == 0, c == 7, [r_wk, r_xn], [r_ps])
                qk_finish(ps, r_ps, 33, cosT[:, t0:t0 + 512], sinT[:, t0:t0 + 512], r_cs, kloc[:, t0:t0 + 512], r_kloc, qkts[nq % 2]); nq += 1
                for tc in range(4):
                    pv, r_pv = nps()
                    for c in range(8):
                        mm(pv[:, 0:128], xn[:, c, tc * 128:(tc + 1) * 128], wk[:, c, 128:256], c == 0, c == 7, [r_wk, r_xn], [r_pv])
                    tcg = s4 * 4 + tc
                    V(lambda e, pv=pv, tcg=tcg: e.tensor_copy(out=vloc[:, tcg, 0:64], in_=pv[:, 0:64]), [r_pv], [r_vloc])
                    A(lambda e, pv=pv, tcg=tcg: e.activation(out=vloc[:, tcg, 130:194], in_=pv[:, 64:128], func=AF.Copy), [r_pv], [r_vloc])
                for cj in range(4):
                    px, r_px = nps()
                    for c in range(8):
                        mm(px[:, :], wk[:, c, 256 + cj * 128:256 + (cj + 1) * 128], xn[:, c, :], c == 0, c == 7, [r_wk, r_xn], [r_px])
                    A(lambda e, px=px, cj=cj, t0=t0: e.activation(out=zbuf[:, cj, t0:t0 + 512], in_=px[:, :], func=AF.Copy), [r_px], [r_z])
                for c4 in range(4):
                    ps, r_ps = nps()
                    for c in range(8):
                        mm(ps[:, :], wq[:, c, c4, :], xn[:, c, :], c == 0, c == 7, [r_wq, r_xn], [r_ps])
                    qk_finish(ps, r_ps, 32, cosT[:, t0:t0 + 512], sinT[:, t0:t0 + 512], r_cs, qT[:, c4, t0:t0 + 512], r_q, qkts[nq % 2]); nq += 1
                if s4 == 3:
                    pass
            d_kv = S.dma_sem(); d_kv2 = S.dma_sem(); r_kloc2 = Res(); r_vloc2 = Res(); r_kall = Res(); r_vall = Res()
            S.dma("sync", d_kv, k_loc, kloc[:], reads=[r_kloc], writes=[r_kloc2])
            S.dma("sync", d_kv2, v_loc, vloc[:].rearrange("p a b -> p (a b)"), reads=[r_vloc], writes=[r_vloc2])
            d_ag = S.dma_sem(); d_ag2 = S.dma_sem()
            S.coll(d_ag, lambda e: e.collective_compute("AllGather", ALU.bypass, replica_groups=[[0, 1, 2, 3], [4, 5, 6, 7]],
                                                        ins=[k_loc.opt()], outs=[k_all.opt()]), reads=[r_kloc2], writes=[r_kall])
            S.coll(d_ag2, lambda e: e.collective_compute("AllGather", ALU.bypass, replica_groups=[[0, 1, 2, 3], [4, 5, 6, 7]],
                                                         ins=[v_loc.opt()], outs=[v_all.opt()]), reads=[r_vloc2], writes=[r_vall])
            S.barrier()

        dump("kall", k_all, [512, 2048], [r_kall])
        dump("vall", v_all, [512, 16 * 194], [r_vall])
        dump("xssm", zbuf[:], [128, 4, NT], [r_z])
        dump("q", qT[:], [128, 4, NT], [r_q])
        d_sp0 = S.dma_sem(); r_hsp = Res()
        S.dma("sync", d_sp0, h_spill.rearrange("(c p) t -> p c t", p=128), hT[:], reads=[r_h], writes=[r_hsp])
        S.barrier()
        hflat = hT[:].rearrange("p c t -> p (c t)")
        with ExitStack() as ssm:
            r_s = Res()
            Cst = sbt(ssm, "Cst", [128, 32, 2, 128], BF16)
            Mop = sbt(ssm, "Mop", [128, 32, 128], BF16)
            lam8 = sbt(ssm, "lam8", [128, 2, 32], F32)
            ab = ExitStack()
            Bst = sbt(ab, "Bst", [128, 32, 2, 128], BF16)
            Bf = hflat[:, 0:4096].bitcast(BF16).rearrange("p (g r k) -> p g r k", g=32, r=2)
            CM = hflat[:, 4096:8192].bitcast(BF16).rearrange("p (g r k) -> p g r k", g=32, r=2)
            GH = hflat[:, 0:8256].bitcast(BF16).rearrange("p (r g c) -> p r g c", r=2, g=32)
            U = hflat[:, 8256:12352].bitcast(BF16).rearrange("p (g c) -> p g c", g=32)
            NS = 26
            S.enabled = bool(PH['ssm'] and SS['A'])
            with ExitStack() as sa:
                sp = sbt(sa, "ssmp", [128, 2144], F32); d_sp = S.dma_sem()
                S.dma("sync", d_sp, sp[:], ssmp_d, writes=[r_s])
                lamre = sp[:, 0:32]; lamim = sp[:, 32:64]; lstep = sp[:, 64:96]
                b3 = lambda a: a.rearrange("p (g h) -> p g h", h=16)
                bre = b3(sp[:, 96:608]); bim = b3(sp[:, 608:1120]); cre = b3(sp[:, 1120:1632]); cim = b3(sp[:, 1632:2144])
                def sm(name, n):
                    return sbt(sa, name, [128, n], F32)
                dl = sm("dl", 32); are = sm("are", 32); aim = sm("aim", 32)
                magE = sm("magE", 32 * NS); angE = sm("angE", 32 * NS); sinE = sm("sinE", 32 * NS); cosE = sm("cosE", 32 * NS)
                g3 = lambda a: a.rearrange("p (g s) -> p g s", s=NS)
                VV = lambda out, a, b, op: V(lambda e: e.tensor_tensor(out=out, in0=a, in1=b, op=op), [r_s, r_cf], [r_s])
                A(lambda e: e.activation(out=dl[:], in_=lstep, func=AF.Exp), [r_s], [r_s])
                VV(are[:], dl[:], lamre, ALU.mult); VV(aim[:], dl[:], lamim, ALU.mult)
                Eb = Etab.unsqueeze(1).broadcast_to([128, 32, NS])
                VV(g3(magE[:]), are[:].unsqueeze(2).broadcast_to([128, 32, NS]), Eb, ALU.mult)
                VV(g3(angE[:]), aim[:].unsqueeze(2).broadcast_to([128, 32, NS]), Eb, ALU.mult)
                A(lambda e: e.activation(out=magE[:], in_=magE[:], func=AF.Exp), [r_s], [r_s])
                with ExitStack() as tmp:
                    sin_of(tmp, sinE[:], angE[:], [128, 32 * NS], [r_s], [r_s], 0.0, "ss")
                    S.barrier()
                with ExitStack() as tmp:
                    sin_of(tmp, cosE[:], angE[:], [128, 32 * NS], [r_s], [r_s], np.pi / 2, "sc")
                    S.barrier()
                VV(cosE[:], cosE[:], magE[:], ALU.mult); VV(sinE[:], sinE[:], magE[:], ALU.mult)
                PWr = g3(cosE[:]); PWi = g3(sinE[:])
                V(lambda e: e.tensor_copy(out=lam8[:, 0, :], in_=PWr[:, :, 24]), [r_s], [r_s])
                V(lambda e: e.tensor_copy(out=lam8[:, 1, :], in_=PWi[:, :, 24]), [r_s], [r_s])
                nr = sm("nr", 32); ni = sm("ni", 32); den_ = sm("den_", 32); cr = sm("cr", 32); ci = sm("ci", 32); tq = sm("tq", 32)
                V(lambda e: e.tensor_scalar(out=nr[:], in0=PWr[:, :, 25], scalar1=-1.0, scalar2=None, op0=ALU.add), [r_s], [r_s])
                V(lambda e: e.tensor_copy(out=ni[:], in_=PWi[:, :, 25]), [r_s], [r_s])
                VV(den_[:], lamre, lamre, ALU.mult); VV(tq[:], lamim, lamim, ALU.mult); VV(den_[:], den_[:], tq[:], ALU.add)
                V(lambda e: e.reciprocal(out=den_[:], in_=den_[:]), [r_s], [r_s])
                VV(cr[:], nr[:], lamre, ALU.mult); VV(tq[:], ni[:], lamim, ALU.mult); VV(cr[:], cr[:], tq[:], ALU.add); VV(cr[:], cr[:], den_[:], ALU.mult)
                VV(ci[:], ni[:], lamre, ALU.mult); VV(tq[:], nr[:], lamim, ALU.mult); VV(ci[:], ci[:], tq[:], ALU.subtract); VV(ci[:], ci[:], den_[:], ALU.mult)
                bbr = sbt(sa, "bbr", [128, 32, 16], F32); bbi = sbt(sa, "bbi", [128, 32, 16], F32); tb = sbt(sa, "tb", [128, 32, 16], F32)
                crb = cr[:].unsqueeze(2).broadcast_to([128, 32, 16]); cib = ci[:].unsqueeze(2).broadcast_to([128, 32, 16])
                VV(bbr[:], bre, crb, ALU.mult); VV(tb[:], bim, cib, ALU.mult); VV(bbr[:], bbr[:], tb[:], ALU.subtract)
                VV(bbi[:], bim, crb, ALU.mult); VV(tb[:], bre, cib, ALU.mult); VV(bbi[:], bbi[:], tb[:], ALU.add)
                t1 = sbt(sa, "t1", [128, 8, 8, 16], F32); t2 = sbt(sa, "t2", [128, 8, 8, 16], F32)
                def table(dst, xr, xi, s0, negi):
                    for gq in range(4):
                        gs = slice(gq * 8, gq * 8 + 8)
                        xrb = xr[:, gs, :].unsqueeze(2).broadcast_to([128, 8, 8, 16]); xib = xi[:, gs, :].unsqueeze(2).broadcast_to([128, 8, 8, 16])
                        prb = PWr[:, gs, s0:s0 + 8].unsqueeze(3).broadcast_to([128, 8, 8, 16]); pib = PWi[:, gs, s0:s0 + 8].unsqueeze(3).broadcast_to([128, 8, 8, 16])
                        dr = dst[:, gs, 0, :].rearrange("p g (s h) -> p g s h", h=16); dim_ = dst[:, gs, 1, :].rearrange("p g (s h) -> p g s h", h=16)
                        VV(t1[:], xrb, prb, ALU.mult); VV(t2[:], xib, pib, ALU.mult); VV(dr, t1[:], t2[:], ALU.subtract)
                        VV(t1[:], xrb, pib, ALU.mult); VV(t2[:], xib, prb, ALU.mult)
                        if negi:
                            VV(t1[:], t1[:], t2[:], ALU.add)
                            V(lambda e, dim_=dim_: e.tensor_scalar(out=dim_, in0=t1[:], scalar1=-1.0, scalar2=None, op0=ALU.mult), [r_s], [r_s])
                        else:
                            VV(dim_, t1[:], t2[:], ALU.add)
                table(Bf, bbr[:], bbi[:], 0, False)
                table(CM, cre, cim, 8, True)
                table(Cst, cre, cim, 16, True)
                mt = sbt(sa, "mt", [128, 128], F32); mt2 = sbt(sa, "mt2", [128, 128], F32)
                for g in range(32):
                    pa, r_pa = nps(); pb_, r_pb_ = nps()
                    for ri in range(2):
                        mm(pa[:, 0:128], Bf[0:64, g, ri, :], CM[0:64, g, ri, :], ri == 0, ri == 1, [r_s], [r_pa])
                    for ri in range(2):
                        mm(pb_[:, 0:128], Bf[64:128, g, ri, :], CM[64:128, g, ri, :], ri == 0, ri == 1, [r_s], [r_pb_])
                    V(lambda e, pa=pa: e.tensor_tensor(out=mt[:], in0=pa[:, 0:128], in1=maskf, op=ALU.mult), [r_pa, r_cf, r_s], [r_s])
                    V(lambda e, pb_=pb_: e.tensor_tensor(out=mt2[:], in0=pb_[:, 0:128], in1=maskb, op=ALU.mult), [r_pb_, r_cf, r_s], [r_s])
                    VV(mt[:], mt[:], mt2[:], ALU.add)
                    V(lambda e, g=g: e.scalar_tensor_tensor(out=Mop[:, g, :], in0=identf, scalar=vecs[:, 40 + g:41 + g], in1=mt[:],
                                                          op0=ALU.mult, op1=ALU.add), [r_s, r_cf, r_vecs], [r_s])
                    for ri in range(2):
                        T(lambda e, g=g, ri=ri: e.transpose(out=pbf[:, ri * 128:(ri + 1) * 128], in_=Bf[:, g, ri, :], identity=identb), [r_s, r_cbf], [r_pbf])
                    A(lambda e, g=g: e.activation(out=Bst[:, g, :, :].rearrange("p r k -> p (r k)"), in_=pbf[:, 0:256], func=AF.Copy), [r_pbf], [r_s])
                S.barrier()
            S.enabled = bool(PH['ssm'])
            dump('Mop', Mop[:], [128, 32, 128], [r_s]); dump('Bst', Bst[:], [128, 32, 2, 128], [r_s]); dump('Cst', Cst[:], [128, 32, 2, 128], [r_s]); dump('lam8', lam8[:], [128, 2, 32], [r_s])
            S.enabled = bool(PH['ssm'] and SS['B'])
            with ExitStack() as sb_:
                selb = sbt(sb_, "selb", [128, 64, 128], BF16); d_sel = S.dma_sem()
                for q4 in range(4):
                    S.dma("gpsimd", d_sel, selb[:, q4 * 16:(q4 + 1) * 16, :].rearrange("p a b -> p (a b)"), sel_d[:, q4 * 2048:(q4 + 1) * 2048], writes=[r_s])
                for g in range(32):
                    kt, gl = g // 8, g % 8
                    pu_, r_pu_ = nps()
                    xv = zbuf[:, kt, :].rearrange("p (c i) -> p i c", i=8)
                    for i in range(8):
                        mm(pu_[:, 0:256], selb[:, gl * 8 + i, :], xv[:, i, :], i == 0, i == 7, [r_s, r_z], [r_pu_])
                    V(lambda e, pu_=pu_, g=g: e.tensor_copy(out=U[:, g, :], in_=pu_[:, 0:256]), [r_pu_, r_s], [r_s])
                for g in range(32):
                    for ri in range(2):
                        pg_, r_pg_ = nps()
                        mm(pg_[:, 0:256], Bst[:, g, ri, :], U[:, g, :], True, True, [r_s], [r_pg_])
                        V(lambda e, pg_=pg_, g=g, ri=ri: e.tensor_copy(out=GH[0:64, ri, g, 2:258], in_=pg_[0:64, 0:256]), [r_pg_, r_s], [r_s])
                        A(lambda e, pg_=pg_, g=g, ri=ri: e.activation(out=GH[64:128, ri, g, 0:256], in_=pg_[64:128, 0:256], func=AF.Copy), [r_pg_, r_s], [r_s])
                S.barrier()
            ab.close()
            S.enabled = bool(PH['ssm'])
            dump('U', U, [128, 32, 256], [r_s]); dump('GH0', GH, [128, 2, 32, 258], [r_s])
            S.enabled = bool(PH['ssm'] and SS['C'])
            kvst = ExitStack()
            S.enabled = bool(PH['attn'])
            Kall = sbt(kvst, "Kall", [128, 4 * NT], BF16); r_K = Res()
            Vall = sbt(kvst, "Vall", [128, 64, 194], BF16); r_V = Res()
            d_kl = S.dma_sem()
            for r in range(4):
                S.dma("sync", d_kl, Kall[:, r * NT:(r + 1) * NT], k_all[r * 128:(r + 1) * 128, :], reads=[r_kall], writes=[r_K])
                S.dma("sync", d_kl, Vall[:, r * 16:(r + 1) * 16, :].rearrange("p a b -> p (a b)"),
                      v_all[r * 128:(r + 1) * 128, :], reads=[r_vall], writes=[r_V])
            S.enabled = bool(PH['ssm'] and SS['C'])
            sc_ = ExitStack()
            if True:
                ENG = ['gpsimd']
                EO = lambda fn, r=(), w=(): S.op(ENG[0], fn, r, w)
                GG = lambda out, a, b, op: EO(lambda e: e.tensor_tensor(out=out, in0=a, in1=b, op=op), [r_s, r_cf], [r_s])
                St = [sbt(sc_, "St%d" % i, [128, 2, 32], F32) for i in range(2)]
                AR = sbt(sc_, "AR", [128, 2, 32], F32); AI = sbt(sc_, "AI", [128, 2, 32], F32)
                ta = sbt(sc_, "ta", [128, 2, 32], F32); tb2 = sbt(sc_, "tb2", [128, 2, 32], F32)
                EO(lambda e: e.tensor_copy(out=AR[:, 0, :], in_=lam8[:, 0, :]), [r_s], [r_s])
                EO(lambda e: e.tensor_copy(out=AR[:, 1, :], in_=lam8[:, 0, :]), [r_s], [r_s])
                EO(lambda e: e.tensor_scalar(out=AI[:, 0, :], in0=lam8[:, 1, :], scalar1=-1.0, scalar2=None, op0=ALU.mult), [r_s], [r_s])
                EO(lambda e: e.tensor_copy(out=AI[:, 1, :], in_=lam8[:, 1, :]), [r_s], [r_s])
                r_cur = [Res(), Res()]; r_ta = Res(); r_tb0 = Res(); r_tb1 = Res(); r_ghw = Res()
                def scan(write_hist):
                    prev = None
                    for s_ in range(256):
                        cur, nxt = St[s_ % 2], St[(s_ + 1) % 2]
                        rc, rn = r_cur[s_ % 2], r_cur[(s_ + 1) % 2]
                        cf_, cb_ = s_ + 2, 255 - s_
                        EO(lambda e, cur=cur: e.tensor_tensor(out=ta[:], in0=cur[:], in1=AR[:], op=ALU.mult), [rc, r_s], [r_ta])
                        EO(lambda e, cur=cur: e.tensor_tensor(out=tb2[:, 0, :], in0=cur[:, 1, :], in1=AI[:, 0, :], op=ALU.mult), [rc, r_s], [r_tb0])
                        EO(lambda e, cur=cur: e.tensor_tensor(out=tb2[:, 1, :], in0=cur[:, 0, :], in1=AI[:, 1, :], op=ALU.mult), [rc, r_s], [r_tb1])
                        if write_hist and prev is not None:
                            pcf, pcb = prev
                            EO(lambda e, cur=cur, pcf=pcf: e.tensor_copy(out=GH[0:64, :, :, pcf], in_=cur[0:64]), [rc], [r_ghw])
                            EO(lambda e, cur=cur, pcb=pcb: e.tensor_copy(out=GH[64:128, :, :, pcb], in_=cur[64:128]), [rc], [r_ghw])
                        EO(lambda e: e.tensor_tensor(out=ta[:], in0=ta[:], in1=tb2[:], op=ALU.add), [r_ta, r_tb0, r_tb1], [r_ta])
                        EO(lambda e, nxt=nxt, cf_=cf_: e.tensor_tensor(out=nxt[0:64], in0=ta[0:64], in1=GH[0:64, :, :, cf_], op=ALU.add), [r_ta], [rn])
                        EO(lambda e, nxt=nxt, cb_=cb_: e.tensor_tensor(out=nxt[64:128], in0=ta[64:128], in1=GH[64:128, :, :, cb_], op=ALU.add), [r_ta], [rn])
                        prev = (cf_, cb_)
                    if write_hist:
                        fin = St[0]; pcf, pcb = prev
                        EO(lambda e: e.tensor_copy(out=GH[0:64, :, :, pcf], in_=fin[0:64]), [r_cur[0]], [r_ghw])
                        EO(lambda e: e.tensor_copy(out=GH[64:128, :, :, pcb], in_=fin[64:128]), [r_cur[0]], [r_ghw])
                    EO(lambda e: e.tensor_copy(out=ta[:], in_=St[0][:]), [r_cur[0], r_cur[1], r_ta, r_tb0, r_tb1, r_ghw, r_s], [r_s, r_ta])
                EO(lambda e: e.memset(St[0][:], 0.0), [r_s], [r_s, r_cur[0]])
                scan(False)
                d_st = S.dma_sem(); d_st2 = S.dma_sem(); r_stl = Res(); r_sta = Res()
                S.dma("sync", d_st, st_loc, St[0][:].rearrange("p r g -> p (r g)"), reads=[r_s], writes=[r_stl])
                S.coll(d_st2, lambda e: e.collective_compute("AllGather", ALU.bypass, replica_groups=[[0, 1, 2, 3], [4, 5, 6, 7]],
                                                             ins=[st_loc.opt()], outs=[st_all.opt()]), reads=[r_stl], writes=[r_sta])
                Fall = sbt(sc_, "Fall", [128, 4, 2, 32], F32)
                S.dma("sync", d_st, Fall[:].rearrange("p r a g -> p r (a g)"), st_all.rearrange("(r p) f -> p r f", p=128), reads=[r_sta], writes=[r_s])
                Pw = sbt(sc_, "Pw", [128, 3, 2, 32], F32); tq2 = sbt(sc_, "tq2", [128, 32], F32); tq3 = sbt(sc_, "tq3", [128, 32], F32)
                EO(lambda e: e.tensor_copy(out=Pw[:, 1], in_=lam8[:]), [r_s], [r_s])
                def csq(dst, src):
                    GG(tq2[:], src[:, 0, :], src[:, 0, :], ALU.mult); GG(tq3[:], src[:, 1, :], src[:, 1, :], ALU.mult)
                    GG(tq3[:], tq2[:], tq3[:], ALU.subtract)
                    GG(tq2[:], src[:, 0, :], src[:, 1, :], ALU.mult)
                    EO(lambda e: e.tensor_scalar(out=dst[:, 1, :], in0=tq2[:], scalar1=2.0, scalar2=None, op0=ALU.mult), [r_s], [r_s])
                    EO(lambda e: e.tensor_copy(out=dst[:, 0, :], in_=tq3[:]), [r_s], [r_s])
                for _ in range(8):
                    csq(Pw[:, 1], Pw[:, 1])
                csq(Pw[:, 2], Pw[:, 1])
                EO(lambda e: e.memset(Pw[:, 0, 0, :], 1.0), [r_s], [r_s]); EO(lambda e: e.memset(Pw[:, 0, 1, :], 0.0), [r_s], [r_s])
                Sin_ = St[0]
                EO(lambda e: e.memset(Sin_[:], 0.0), [r_s], [r_s])
                for rr in range(4):
                    for k in range(3):
                        GG(ta[:, 0, :], Pw[:, k, 0, :], Fall[:, rr, 0, :], ALU.mult); GG(tb2[:, 0, :], Pw[:, k, 1, :], Fall[:, rr, 1, :], ALU.mult)
                        GG(ta[:, 0, :], ta[:, 0, :], tb2[:, 0, :], ALU.subtract)
                        GG(ta[:, 1, :], Pw[:, k, 0, :], Fall[:, rr, 1, :], ALU.mult); GG(tb2[:, 1, :], Pw[:, k, 1, :], Fall[:, rr, 0, :], ALU.mult)
                        GG(ta[:, 1, :], ta[:, 1, :], tb2[:, 1, :], ALU.add)
                        EO(lambda e, rr=rr, k=k: e.tensor_scalar(out=ta[:], in0=ta[:], scalar1=selc[:, rr * 3 + k:rr * 3 + k + 1], scalar2=None, op0=ALU.mult), [r_s, r_cf], [r_s])
                        GG(Sin_[:], Sin_[:], ta[:], ALU.add)
                EO(lambda e: e.tensor_copy(out=GH[0:64, :, :, 1], in_=Sin_[0:64]), [r_s], [r_s])
                EO(lambda e: e.tensor_copy(out=GH[64:128, :, :, 256], in_=Sin_[64:128]), [r_s], [r_s])
                ENG[0] = 'gpsimd'
                scan(True)
            S.enabled = bool(PH['attn'])
            with ExitStack() as ph:
                Pb = [(sbt(ph, "Pb%d" % i, [128, 1024], BF16), Res()) for i in range(3)]
                den = sbt(ph, "den", [128, 512], F32); r_den = Res()
                rec = sbt(ph, "rec", [128, 512], F32); r_rec = Res()
                qz = [[(sbt(ph, "qz%d%d" % (hh, b), [128, 512], BF16), Res()) for b in range(2)] for hh in range(2)]
                for hh in range(2):
                    for b in range(2):
                        V(lambda e, t_=qz[hh][b][0]: e.memset(t_[:], 0.0), [], [qz[hh][b][1]])
                pk = 0; hn = 0
                heads = [(qt, c4, hh) for qt in range(4) for c4 in range(4) for hh in range(2)]
                def qzcopy(n):
                    qt_, c4_, hh_ = heads[n]
                    qzt_, r_qz_ = qz[hh_][(n // 2) % 2]
                    lo_ = 64 * hh_
                    V(lambda e: e.tensor_copy(out=qzt_[lo_:lo_ + 64, :], in_=qT[lo_:lo_ + 64, c4_, qt_ * 512:(qt_ + 1) * 512]), [r_q], [r_qz_])
                qzcopy(0)
                for (qt, c4, hh) in heads:
                    if True:
                        if True:
                            q0 = qt * 512
                            lo = 64 * hh
                            qzt, r_qz = qz[hh][(hn // 2) % 2]
                            if hn + 1 < len(heads):
                                qzcopy(hn + 1)
                            po, r_po = pbanks[4 + (hn % 2)]; hn += 1
                            vcols = slice(0, 65) if hh == 0 else slice(66, 194)
                            M_ = 65 if hh == 0 else 128
                            def qkpair(j):
                                sc, r_sc = pbig[j % 2]
                                for u_ in range(2):
                                    kc = 2 * j + u_
                                    mm(sc[:, u_ * 512:(u_ + 1) * 512], Kall[:, kc * 128:(kc + 1) * 128], qzt[:], True, True, [r_K, r_qz], [r_sc])
                                return sc, r_sc
                            cur = qkpair(0)
                            for j in range(32):
                                nxt = qkpair(j + 1) if j < 31 else None
                                sc, r_sc = cur
                                pb, r_pb = Pb[pk % 3]; pk += 1
                                A(lambda e, pb=pb, sc=sc: e.activation(out=pb[:], in_=sc[:, :], func=AF.Exp, scale=0.125), [r_sc], [r_pb])
                                for u_ in range(2):
                                    kc = 2 * j + u_
                                    mm(po[0:M_, :], Vall[:, kc, vcols], pb[:, u_ * 512:(u_ + 1) * 512], kc == 0, kc == 63, [r_V, r_pb], [r_po])
                                cur = nxt
                            dp = 64 if hh == 0 else 0
                            V(lambda e, po=po, dp=dp: e.tensor_copy(out=den[dp:dp + 1, :], in_=po[dp:dp + 1, :]), [r_po], [r_den])
                            pbc, r_pbc = pbanks[6]
                            mm(pbc[:, :], onesf[dp:dp + 1, :], den[dp:dp + 1, :], True, True, [r_den, r_cf], [r_pbc])
                            V(lambda e, pbc=pbc: e.reciprocal(out=rec[:], in_=pbc[:, :]), [r_pbc], [r_rec])
                            V(lambda e, po=po, lo=lo, c4=c4, q0=q0: e.tensor_tensor(out=qT[lo:lo + 64, c4, q0:q0 + 512], in0=po[lo:lo + 64, :],
                                                                                    in1=rec[lo:lo + 64, :], op=ALU.mult), [r_po, r_rec], [r_q])
                S.barrier()

            S.enabled = bool(PH['ssm'])
            sc_.close()
            kvst.close()
            S.enabled = bool(PH['ssm'])
            dump('GH1', GH, [128, 2, 32, 258], [r_s])
            S.enabled = bool(PH['ssm'] and SS['D'])
            with ExitStack() as so:
                for g in range(32):
                    py_, r_py_ = nps()
                    mm(py_[:, 0:256], Mop[:, g, :], U[:, g, :], True, False, [r_s], [r_py_])
                    for ri in range(2):
                        mm(py_[:, 0:256], Cst[:, g, ri, :], GH[:, ri, g, 1:257], False, ri == 1, [r_s], [r_py_])
                    A(lambda e, py_=py_, g=g: e.activation(out=U[:, g, :], in_=py_[:, 0:256], func=AF.Gelu_apprx_tanh), [r_py_, r_s], [r_s])
                selTb = sbt(so, "selTb", [128, 64, 128], BF16); d_selT = S.dma_sem()
                for q4 in range(4):
                    S.dma("gpsimd", d_selT, selTb[:, q4 * 16:(q4 + 1) * 16, :].rearrange("p a b -> p (a b)"), selT_d[:, q4 * 2048:(q4 + 1) * 2048], writes=[r_s])
                Z1 = sbt(so, "Z1", [128, 4, NT], BF16)
                for kt in range(4):
                    zv = Z1[:, kt, :].rearrange("p (c j) -> p j c", j=8)
                    for j in range(8):
                        pz, r_pz = nps()
                        for gl in range(8):
                            mm(pz[:, 0:256], selTb[:, gl * 8 + j, :], U[:, kt * 8 + gl, :], gl == 0, gl == 7, [r_s], [r_pz])
                        V(lambda e, pz=pz, zv=zv, j=j: e.tensor_copy(out=zv[:, j, :], in_=pz[:, 0:256]), [r_pz, r_s], [r_s])
                wglu_t = sbt(so, "wglu_t", [128, 4, 512], BF16); d_wgl = S.dma_sem()
                S.dma("gpsimd", d_wgl, wglu_t[:], wglu.rearrange("(c p) n -> p c n", p=128), writes=[r_s])
                sgl = sbt(so, "sgl", [128, 512], F32)
                for ot in range(4):
                    for s4 in range(4):
                        t0 = s4 * 512
                        pg2, r_pg2 = nps()
                        for kt in range(4):
                            mm(pg2[:, :], wglu_t[:, kt, ot * 128:(ot + 1) * 128], Z1[:, kt, t0:t0 + 512], kt == 0, kt == 3, [r_s], [r_pg2])
                        A(lambda e, pg2=pg2, ot=ot: e.activation(out=sgl[:], in_=pg2[:, :], func=AF.Sigmoid, bias=vecs[:, 34 + ot:35 + ot]), [r_pg2, r_s, r_vecs], [r_s])
                        V(lambda e, ot=ot, t0=t0: e.tensor_tensor(out=zbuf[:, ot, t0:t0 + 512], in0=Z1[:, ot, t0:t0 + 512], in1=sgl[:], op=ALU.mult), [r_s, r_z], [r_z, r_s])
                S.barrier()
            S.enabled = bool(PH['ssm'])
            S.barrier()

        dump("attn", qT[:], [128, 4, NT], [r_q])
        dump("z", zbuf[:], [128, 4, NT], [r_z])
        d_rl = S.dma_sem()
        S.dma("sync", d_rl, hT[:], h_spill.rearrange("(c p) t -> p c t", p=128), reads=[r_hsp], writes=[r_h])
        S.enabled = bool(PH['merge'])
        with ExitStack() as ph:
            tmpn = mk_norm_tmp(ph)
            xn = sbt(ph, "xn3", [128, 8, 512], BF16); r_xn = Res()
            wab_t = sbt(ph, "wab_t", [128, 4, DM], BF16); r_wab = Res(); d_w3 = S.dma_sem()
            for c4 in range(4):
                S.dma("gpsimd", d_w3, wab_t[0:64, c4, :], wab[64 * c4:64 * c4 + 64, :], writes=[r_wab])
                S.dma("gpsimd", d_w3, wab_t[64:128, c4, :], wab[256 + 64 * c4:256 + 64 * c4 + 64, :], writes=[r_wab])
            wsb_t = sbt(ph, "wsb_t", [128, 4, DM], BF16); r_wsb = Res()
            S.dma("gpsimd", d_w3, wsb_t[:], wsb.rearrange("(c p) n -> p c n", p=128), writes=[r_wsb])
            wout_t = sbt(ph, "wout_t", [128, 8, DM], BF16); r_wout = Res()
            wout_v = wout.rearrange("(c p) n -> p c n", p=128)
            S.dma("gpsimd", d_w3, wout_t[:, 0:4, :], wout_v[:, 0:4, :], writes=[r_wout])
            S.dma("gpsimd", d_w3, wout_t[:, 4:8, :], wout_v[:, 4:8, :], writes=[r_wout])
            wgs = WStream(ph, "wgate", [128, 8, 2, 128])
            mrg = sbt(ph, "mrg", [128, 8, 512], BF16); r_mrg = Res()
            sa = sbt(ph, "sa", [128, 512], F32); r_sa = Res(); ss = sbt(ph, "ss", [128, 512], F32); r_ss = Res()
            for s4 in range(4):
                t0 = s4 * 512
                rmsnorm(ph, xn, r_xn, t0, 8, tmpn)
                for o in range(8):
                    wg_ = wgs.next()
                    wload(wg_, wg_[0][:, :, 0, :], win_v[:, :, 1280 + o * 128:1280 + (o + 1) * 128])
                    wload(wg_, wg_[0][:, :, 1, :], win_v[:, :, 2304 + o * 128:2304 + (o + 1) * 128])
                    pga, r_pga = nps(); pya, r_pya = nps(); pgs, r_pgs = nps(); pys, r_pys = nps()
                    for c in range(8):
                        mm(pga[:, :], wg_[0][:, c, 0, :], xn[:, c, :], c == 0, c == 7, [wg_[1], r_xn], [r_pga])
                    for c in range(4):
                        mm(pya[:, :], wab_t[:, c, o * 128:(o + 1) * 128], qT[:, c, t0:t0 + 512], c == 0, c == 3, [r_wab, r_q], [r_pya])
                    for c in range(8):
                        mm(pgs[:, :], wg_[0][:, c, 1, :], xn[:, c, :], c == 0, c == 7, [wg_[1], r_xn], [r_pgs])
                    for c in range(4):
                        mm(pys[:, :], wsb_t[:, c, o * 128:(o + 1) * 128], zbuf[:, c, t0:t0 + 512], c == 0, c == 3, [r_wsb, r_z], [r_pys])
                    A(lambda e, pga=pga: e.activation(out=sa[:], in_=pga[:, :], func=AF.Sigmoid), [r_pga], [r_sa])
                    A(lambda e, pgs=pgs: e.activation(out=ss[:], in_=pgs[:, :], func=AF.Sigmoid), [r_pgs], [r_ss])
                    V(lambda e, pya=pya: e.tensor_tensor(out=sa[:], in0=sa[:], in1=pya[:, :], op=ALU.mult), [r_sa, r_pya], [r_sa])
                    V(lambda e, pys=pys: e.tensor_tensor(out=ss[:], in0=ss[:], in1=pys[:, :], op=ALU.mult), [r_ss, r_pys], [r_ss])
                    V(lambda e, o=o: e.tensor_tensor(out=mrg[:, o, :], in0=sa[:], in1=ss[:], op=ALU.add), [r_sa, r_ss], [r_mrg])
                for o2 in range(8):
                    py, r_py = nps()
                    for o in range(8):
                        mm(py[:, :], wout_t[:, o, o2 * 128:(o2 + 1) * 128], mrg[:, o, :], o == 0, o == 7, [r_wout, r_mrg], [r_py])
                    V(lambda e, py=py, o2=o2, t0=t0: e.tensor_tensor(out=hT[:, o2, t0:t0 + 512], in0=py[:, :], in1=hT[:, o2, t0:t0 + 512], op=ALU.add),
                      [r_py, r_h], [r_h])
            S.barrier()

        mix.close()
        S.enabled = bool(PH['ffn2'])
        ffn(w2g, w2u, w2d, 16, "b")

        S.enabled = bool(PH['ple'])
        with ExitStack() as ph:
            tmpn = mk_norm_tmp(ph)
            xn = sbt(ph, "xn4", [128, 8, 512], BF16); r_xn = Res()
            wpg_t = sbt(ph, "wpg_t", [128, 8, DM], BF16); r_wpg = Res(); d_w4 = S.dma_sem()
            wpg_v = wpg.rearrange("(c p) n -> p c n", p=128)
            S.dma("gpsimd", d_w4, wpg_t[:, 0:4, :], wpg_v[:, 0:4, :], writes=[r_wpg])
            S.dma("gpsimd", d_w4, wpg_t[:, 4:8, :], wpg_v[:, 4:8, :], writes=[r_wpg])
            wpp_t = sbt(ph, "wpp_t", [128, 2, DM], BF16); r_wpp = Res()
            S.dma("gpsimd", d_w4, wpp_t[:], wpp.rearrange("(c p) n -> p c n", p=128), writes=[r_wpp])
            pb16 = sbt(ph, "pb16", [128, 2, NT], BF16); r_p16 = Res()
            pT_v = pT.rearrange("(c p) t -> p c t", p=128)
            for c in range(2):
                S.dma("gpsimd", d_w4, pb16[:, c, :], pT_v[:, c, :], writes=[r_p16])
            sgp = sbt(ph, "sgp", [128, 512], F32); r_sgp = Res()
            d_out = S.dma_sem()
            for s4 in range(4):
                t0 = s4 * 512
                rmsnorm(ph, xn, r_xn, t0, 24, tmpn)
                for o in range(8):
                    pg, r_pg = nps(); pl, r_pl = nps()
                    for c in range(8):
                        mm(pg[:, :], wpg_t[:, c, o * 128:(o + 1) * 128], xn[:, c, :], c == 0, c == 7, [r_wpg, r_xn], [r_pg])
                    for c in range(2):
                        mm(pl[:, :], wpp_t[:, c, o * 128:(o + 1) * 128], pb16[:, c, t0:t0 + 512], c == 0, c == 1, [r_wpp, r_p16], [r_pl])
                    A(lambda e, pg=pg: e.activation(out=sgp[:], in_=pg[:, :], func=AF.Sigmoid), [r_pg], [r_sgp])
                    V(lambda e, pl=pl: e.tensor_tensor(out=sgp[:], in0=sgp[:], in1=pl[:, :], op=ALU.mult), [r_sgp, r_pl], [r_sgp])
                    V(lambda e, o=o, t0=t0: e.tensor_tensor(out=hT[:, o, t0:t0 + 512], in0=sgp[:], in1=hT[:, o, t0:t0 + 512], op=ALU.add),
                      [r_sgp, r_h], [r_h])
                S.dma("sync", d_out, outT.rearrange("(c p) t -> p c t", p=128)[:, :, t0:t0 + 512], hT[:, :, t0:t0 + 512], reads=[r_h])
            S.barrier()
        S.enabled = True
        if not PH['ple']:
            d_o2 = S.dma_sem()
            S.dma("sync", d_o2, outT.rearrange("(c p) t -> p c t", p=128), hT[:], reads=[r_h])
        for e_ in S.ENGS:
            S.wait_all(e_)
        S.emit()
    return nc


_NC_CACHE = {}


def _consts():
    f = np.float32
    ident = np.eye(128, dtype=f)
    rot = np.zeros((128, 128), f)
    for m in range(128):
        d = m % 64
        if d < 32:
            rot[m + 32, m] = -1.0
        else:
            rot[m - 32, m] = 1.0
    ones = np.ones((128, 128), f)
    ii = np.arange(128) // 16
    maskf = (ii[None, :] >= ii[:, None]).astype(f)
    maskb = (ii[:, None] >= ii[None, :]).astype(f)
    E = np.zeros((128, 26), f)
    s = np.arange(8)
    E[:64, 0:8] = 7 - s; E[64:, 0:8] = s
    E[:64, 8:16] = s - 7; E[64:, 8:16] = -s
    E[:64, 16:24] = s + 1; E[64:, 16:24] = 8 - s
    E[:, 24] = 8; E[:, 25] = 1
    sel = np.zeros((64, 128, 128), f); selT = np.zeros((64, 128, 128), f)
    for gl in range(8):
        for i in range(8):
            for h in range(16):
                sel[gl * 8 + i, gl * 16 + h, i * 16 + h] = 1.0
                selT[gl * 8 + i, i * 16 + h, gl * 16 + h] = 1.0
    sel = np.ascontiguousarray(sel.transpose(1, 0, 2)).reshape(128, 64 * 128)
    selT = np.ascontiguousarray(selT.transpose(1, 0, 2)).reshape(128, 64 * 128)
    return ident, rot, ones, maskf, maskb, E, sel, selT


def kernel(x, p, ffn1_norm, ffn1_w_gate, ffn1_w_up, ffn1_w_down, mix_norm, w_in,
           q_norm, k_norm, ssm_lambda_re, ssm_lambda_im, ssm_log_step, ssm_b_re,
           ssm_b_im, ssm_c_re, ssm_c_im, ssm_d, ssm_glu_w, ssm_glu_b, w_attn_branch,
           w_ssm_branch, w_out, ffn2_norm, ffn2_w_gate, ffn2_w_up, ffn2_w_down,
           ple_norm, ple_w_gate, ple_w_proj):
    f = np.float32
    A_ = lambda a: np.ascontiguousarray(np.asarray(a, dtype=f))
    x = A_(x); p = A_(p)
    if "nc" not in _NC_CACHE:
        _NC_CACHE["nc"] = build()
    nc = _NC_CACHE["nc"]
    ident, rot, ones, maskf, maskb, E, sel, selT = _consts()
    vecs = np.zeros((128, 72), f)
    pc = lambda v: A_(v).reshape(8, 128).T
    vecs[:, 0:8] = pc(ffn1_norm[0]); vecs[:, 8:16] = pc(mix_norm[0]); vecs[:, 16:24] = pc(ffn2_norm[0]); vecs[:, 24:32] = pc(ple_norm[0])
    vecs[:, 32] = np.tile(A_(q_norm[0]), 2); vecs[:, 33] = np.tile(A_(k_norm[0]), 2)
    vecs[:, 34:38] = A_(ssm_glu_b[0]).reshape(4, 128).T
    freqs = (10000.0 ** (-np.arange(0, 32, 2, dtype=np.float32) / 32)).astype(f)
    for pp in range(128):
        dd = (pp % 64) % 32
        if dd < 16:
            vecs[pp, 38] = freqs[dd]
        else:
            vecs[pp, 39] = freqs[dd - 16]
    vecs[:, 40:72] = np.tile(A_(ssm_d[0]).reshape(32, 16).T, (8, 1))
    def dn(a):
        return A_(a).transpose(0, 2, 1).reshape(128, 32)
    lamre = dn(ssm_lambda_re[0]); lamim = dn(ssm_lambda_im[0])
    lstep = np.repeat(A_(ssm_log_step[0])[:, None, :], 64, axis=1).reshape(128, 32)
    bre = A_(ssm_b_re[0]).transpose(0, 2, 1, 3).reshape(128, 512)
    bim = A_(ssm_b_im[0]).transpose(0, 2, 1, 3).reshape(128, 512)
    cre = A_(ssm_c_re[0]).transpose(0, 3, 1, 2).reshape(128, 512)
    cim = A_(ssm_c_im[0]).transpose(0, 3, 1, 2).reshape(128, 512)
    ssmp = np.ascontiguousarray(np.concatenate([lamre, lamim, lstep, bre, bim, cre, cim], axis=1))
    shared = dict(vecs=vecs, sel=sel, selT=selT, ssmp=ssmp,
                  w1g=A_(ffn1_w_gate[0]), w1u=A_(ffn1_w_up[0]), w1d=A_(ffn1_w_down[0]),
                  w2g=A_(ffn2_w_gate[0]), w2u=A_(ffn2_w_up[0]), w2d=A_(ffn2_w_down[0]),
                  w_in=A_(w_in[0]), wglu=A_(ssm_glu_w[0]), wab=A_(w_attn_branch[0]), wsb=A_(w_ssm_branch[0]),
                  wout=A_(w_out[0]), wpg=A_(ple_w_gate[0]), wpp=A_(ple_w_proj[0]))
    in_maps = []
    for r in range(8):
        b, q = r // 4, r % 4
        t0 = q * NT
        tok = np.arange(t0, t0 + NT)
        pos = np.stack([(tok // 64).astype(f), (tok % 64).astype(f)], axis=0)
        selc = np.zeros((128, 12), f)
        for rr in range(4):
            if rr < q:
                selc[:64, rr * 3 + (q - 1 - rr)] = 1.0
            if rr > q:
                selc[64:, rr * 3 + (rr - q - 1)] = 1.0
        cf32 = np.ascontiguousarray(np.concatenate([ident, rot, ones, maskf, maskb, E, selc], axis=1))
        m = dict(shared)
        m["xT"] = np.ascontiguousarray(x[b, t0:t0 + NT, :].T)
        m["pT"] = np.ascontiguousarray(p[0, b, t0:t0 + NT, :].T)
        m["pos"] = np.ascontiguousarray(pos)
        m["cf32"] = cf32
        in_maps.append(m)
    res = run_bass_kernel_spmd(nc, in_maps, core_ids=list(range(8)), **({'trace': True} if DEBUG.get('trace') else {}))
    DEBUG['last'] = res
    out = np.zeros((2, 8192, DM), f)
    for r in range(8):
        b, q = r // 4, r % 4
        out[b, q * NT:(q + 1) * NT, :] = np.asarray(res.results[r]["outT"]).T
    return out
```

```python
import numpy as np
import ml_dtypes
from contextlib import ExitStack
import concourse.bass as bass
import concourse.mybir as mybir
from concourse.bass_utils import run_bass_kernel_spmd

F32 = mybir.dt.float32
BF16 = mybir.dt.bfloat16
I32 = mybir.dt.int32
AF = mybir.ActivationFunctionType
ALU = mybir.AluOpType


class Res:
    __slots__ = ("name", "last_w", "readers")

    def __init__(self, name=""):
        self.name = name
        self.last_w = None
        self.readers = {}


class Sched:
    ENGS = ("tensor", "vector", "scalar", "gpsimd", "sync")

    def __init__(self, nc, stack):
        self.nc = nc
        self.stack = stack
        self.sems = {}
        self.count = {}
        self.prog = {e: [] for e in self.ENGS}
        self.seen = {e: {} for e in self.ENGS}
        for e in self.ENGS:
            self.sems[e] = stack.enter_context(nc.semaphore("s_" + e))
            self.count[e] = 0
        self.ndma = 0
        self.enabled = True

    def dma_sem(self, name=None):
        key = "dma%d" % self.ndma
        self.ndma += 1
        self.sems[key] = self.stack.enter_context(self.nc.semaphore("s_" + key))
        self.count[key] = 0
        return key

    def _wait(self, eng, key, val):
        if val <= 0:
            return
        if self.seen[eng].get(key, 0) >= val:
            return
        self.seen[eng][key] = val
        sem = self.sems[key]
        self.prog[eng].append(lambda e, sem=sem, val=val: e.wait_ge(sem, val))

    def _deps(self, eng, reads, writes):
        deps = []
        for r in reads:
            if r.last_w is not None:
                deps.append(r.last_w)
        for w in writes:
            if w.last_w is not None:
                deps.append(w.last_w)
            deps.extend(w.readers.items())
        for key, val in deps:
            if key == eng and eng == "tensor":
                continue
            self._wait(eng, key, val)

    def op(self, eng, fn, reads=(), writes=()):
        if not self.enabled:
            return 0
        self._deps(eng, reads, writes)
        self.count[eng] += 1
        v = self.count[eng]
        sem = self.sems[eng]
        self.prog[eng].append(lambda e, fn=fn, sem=sem: fn(e).then_inc(sem, 1))
        for r in reads:
            r.readers[eng] = v
        for w in writes:
            w.last_w = (eng, v)
            w.readers = {}
        return v

    def dma(self, eng, dkey, out, in_, reads=(), writes=(), **kw):
        if not self.enabled:
            return 0
        self._deps(eng, reads, writes)
        self.count[dkey] += 16
        v = self.count[dkey]
        sem = self.sems[dkey]
        self.prog[eng].append(
            lambda e, out=out, in_=in_, sem=sem, kw=kw: e.dma_start(out=out, in_=in_, **kw).then_inc(sem, 16))
        for r in reads:
            r.readers[dkey] = v
        for w in writes:
            w.last_w = (dkey, v)
            w.readers = {}
        return v

    def coll(self, dkey, fn, reads=(), writes=()):
        eng = "gpsimd"
        if not self.enabled:
            return 0
        self._deps(eng, reads, writes)
        self.count[dkey] += 1
        v = self.count[dkey]
        sem = self.sems[dkey]
        self.prog[eng].append(lambda e, fn=fn, sem=sem: fn(e).then_inc(sem, 1))
        for r in reads:
            r.readers[dkey] = v
        for w in writes:
            w.last_w = (dkey, v)
            w.readers = {}
        return v

    def raw(self, eng, fn):
        self.prog[eng].append(fn)

    def wait_all(self, eng):
        for key, val in self.count.items():
            if key != eng or True:
                self._wait(eng, key, val)

    def barrier(self):
        snap = dict(self.count)
        for e in self.ENGS:
            for key, val in snap.items():
                self._wait(e, key, val)

    def emit(self):
        nc = self.nc
        with nc.Block() as block:
            @block.tensor
            def _(eng):
                for f in self.prog["tensor"]:
                    f(eng)

            @block.vector
            def _(eng):
                for f in self.prog["vector"]:
                    f(eng)

            @block.scalar
            def _(eng):
                for f in self.prog["scalar"]:
                    f(eng)

            @block.gpsimd
            def _(eng):
                for f in self.prog["gpsimd"]:
                    f(eng)

            @block.sync
            def _(eng):
                for f in self.prog["sync"]:
                    f(eng)

NT = 2048
DM = 1024
DFF = 2816
NJ = 22
TWO_PI = 2.0 * np.pi
DEBUG = {}
SS = dict(A=1, B=1, C=1, D=1)
PH = dict(ffn1=1, mix1=1, mix2=1, ssm=1, attn=1, merge=1, ffn2=1, ple=1)


def build(dbg=False):
    nc = bass.Bass("TRN2", target_bir_lowering=False)
    di = lambda name, shape, dt=F32: nc.dram_tensor(name, shape, dt, kind="ExternalInput").ap()
    xT = di("xT", [DM, NT]); pT = di("pT", [256, NT]); pos = di("pos", [2, NT])
    vecs_d = di("vecs", [128, 72]); cf32_d = di("cf32", [128, 5 * 128 + 26 + 12])
    sel_d = di("sel", [128, 64 * 128]); selT_d = di("selT", [128, 64 * 128])
    ssmp_d = di("ssmp", [128, 96 + 2048])
    w1g = di("w1g", [DM, DFF]); w1u = di("w1u", [DM, DFF]); w1d = di("w1d", [DFF, DM])
    w2g = di("w2g", [DM, DFF]); w2u = di("w2u", [DM, DFF]); w2d = di("w2d", [DFF, DM])
    w_in = di("w_in", [DM, 3328]); wglu = di("wglu", [512, 512]); wab = di("wab", [512, DM])
    wsb = di("wsb", [512, DM]); wout = di("wout", [DM, DM]); wpg = di("wpg", [DM, DM]); wpp = di("wpp", [256, DM])
    outT = nc.dram_tensor("outT", [DM, NT], F32, kind="ExternalOutput").ap()
    k_loc = nc.dram_tensor("k_loc", [128, 2048], BF16).ap()
    k_all = nc.dram_tensor("k_all", [512, 2048], BF16).ap()
    v_loc = nc.dram_tensor("v_loc", [128, 16 * 194], BF16).ap()
    v_all = nc.dram_tensor("v_all", [512, 16 * 194], BF16).ap()
    st_loc = nc.dram_tensor("st_loc", [128, 64], F32).ap()
    st_all = nc.dram_tensor("st_all", [512, 64], F32).ap()
    h_spill = nc.dram_tensor("h_spill", [DM, NT], F32).ap()
    dbg_out = {}

    with ExitStack() as st:
        S = Sched(nc, st)
        _uid = [0]
        def sbt(stack, name, shape, dt):
            _uid[0] += 1
            return stack.enter_context(nc.sbuf_tensor("sb%d_%s" % (_uid[0], name), shape, dt))
        V = lambda fn, r=(), w=(): S.op("vector", fn, r, w)
        A = lambda fn, r=(), w=(): S.op("scalar", fn, r, w)
        G = lambda fn, r=(), w=(): S.op("gpsimd", fn, r, w)
        T = lambda fn, r=(), w=(): S.op("tensor", fn, r, w)
        def mm(ps, lhsT, rhs, start, stop, r=(), w=()):
            T(lambda e: e.matmul(ps, lhsT=lhsT, rhs=rhs, start=start, stop=stop), r, w)

        d_dbg = S.dma_sem()
        def dump(name, src, shape, reads):
            if not DEBUG.get(name):
                return
            o = nc.dram_tensor("dbg_" + name, list(shape), F32, kind="ExternalOutput").ap()
            en = S.enabled; S.enabled = True
            S.dma("gpsimd", d_dbg, o, src, reads=reads)
            S.enabled = en
        hT = sbt(st, "hT", [128, 8, NT], F32); r_h = Res()
        vecs = sbt(st, "vecs", [128, 72], F32); r_vecs = Res()
        cf32 = sbt(st, "cf32", [128, 5 * 128 + 38], F32); r_cf = Res()
        cbf = sbt(st, "cbf", [128, 3 * 128], BF16); r_cbf = Res()
        identf = cf32[:, 0:128]; rotm = cf32[:, 128:256]; onesf = cf32[:, 256:384]
        maskf = cf32[:, 384:512]; maskb = cf32[:, 512:640]; Etab = cf32[:, 640:666]; selc = cf32[:, 666:678]
        identb = cbf[:, 0:128]; onesb = cbf[:, 128:256]; blkones = cbf[:, 256:384]
        d_in = S.dma_sem()
        S.dma("sync", d_in, hT[:], xT.rearrange("(c p) t -> p c t", p=128), writes=[r_h])
        S.dma("sync", d_in, vecs[:], vecs_d, writes=[r_vecs])
        S.dma("sync", d_in, cf32[:], cf32_d, writes=[r_cf])
        V(lambda e: e.tensor_copy(out=cbf[:, 0:128], in_=cf32[:, 0:128]), [r_cf], [r_cbf])
        V(lambda e: e.tensor_copy(out=cbf[:, 128:256], in_=cf32[:, 256:384]), [r_cf], [r_cbf])
        V(lambda e: e.memset(cbf[:, 256:384], 0.0), [], [r_cbf])
        V(lambda e: e.memset(cbf[0:64, 256:320], 1.0), [], [r_cbf])
        V(lambda e: e.memset(cbf[64:128, 320:384], 1.0), [], [r_cbf])

        pbig = []
        pbanks = []
        for i in range(3):
            t_ = st.enter_context(nc.psum_tensor("pbig%d" % i, [128, 1024], F32))
            pbig.append((t_, Res()))
            pbanks.append((t_[:, 0:512], Res())); pbanks.append((t_[:, 512:1024], Res()))
        pbanks.append((st.enter_context(nc.psum_tensor("pb6", [128, 512], F32)), Res()))
        pbf = st.enter_context(nc.psum_tensor("pbf", [128, 1024], BF16)); r_pbf = Res()
        pctr = [0]
        def nps():
            p = pbanks[pctr[0] % 7]; pctr[0] += 1
            return p

        class WStream:
            def __init__(self, stack, name, shape, n=2, dt=BF16):
                self.slots = [(sbt(stack, "%s%d" % (name, i), shape, dt), Res(), S.dma_sem()) for i in range(n)]
                self.i = 0
            def next(self):
                s = self.slots[self.i % len(self.slots)]; self.i += 1
                return s
        def wload(slot, dst, src):
            tile, res, dk = slot
            S.dma("gpsimd", dk, dst, src, writes=[res])

        def rmsnorm(stack_tmp, xn, r_xn, t0, gcol, tmp):
            sq, r_sq, rs, r_rs = tmp["sq"], tmp["r_sq"], tmp["rs"], tmp["r_rs"]
            A(lambda e: e.activation(out=sq[:], in_=hT[:, :, t0:t0 + 512], func=AF.Square), [r_h], [r_sq])
            ps, r_ps = nps()
            for c in range(8):
                mm(ps[:, :], onesb, sq[:, c, :], c == 0, c == 7, [r_sq, r_cbf], [r_ps])
            A(lambda e: e.activation(out=rs[:], in_=ps[:, :], func=AF.Sqrt, bias=1e-6, scale=1.0 / DM), [r_ps], [r_rs])
            V(lambda e: e.reciprocal(out=rs[:], in_=rs[:]), [r_rs], [r_rs])
            for c in range(8):
                V(lambda e, c=c: e.scalar_tensor_tensor(out=xn[:, c, :], in0=hT[:, c, t0:t0 + 512],
                                                       scalar=vecs[:, gcol + c:gcol + c + 1], in1=rs[:],
                                                       op0=ALU.mult, op1=ALU.mult), [r_h, r_rs, r_vecs], [r_xn])

        def mk_norm_tmp(stack):
            return dict(sq=sbt(stack, "sq", [128, 8, 512], BF16), r_sq=Res(),
                        rs=sbt(stack, "rs", [128, 512], F32), r_rs=Res())

        def ffn(wg, wu, wd, gcol, tag):
            with ExitStack() as ph:
                tmp = mk_norm_tmp(ph)
                xn = sbt(ph, "xn" + tag, [128, 8, 1024], BF16); r_xn = Res()
                hid = sbt(ph, "hid" + tag, [128, NJ, 1024], BF16); r_hid = Res()
                sg = [(sbt(ph, "sg%s%d" % (tag, i), [128, 512], F32), Res()) for i in range(2)]
                wgs = WStream(ph, "wg" + tag, [128, 8, 256]); wus = WStream(ph, "wu" + tag, [128, 8, 256])
                wds = WStream(ph, "wd" + tag, [128, NJ, 256])
                wg_v = wg.rearrange("(c p) n -> p c n", p=128); wu_v = wu.rearrange("(c p) n -> p c n", p=128)
                wd_v = wd.rearrange("(j p) n -> p j n", p=128)
                for tt in range(2):
                    for sub in range(2):
                        rmsnorm(ph, xn[:, :, sub * 512:(sub + 1) * 512], r_xn, tt * 1024 + sub * 512, gcol, tmp)
                    k = 0
                    for jb in range(11):
                        sg_, su_ = wgs.next(), wus.next()
                        wload(sg_, sg_[0][:], wg_v[:, :, jb * 256:(jb + 1) * 256])
                        wload(su_, su_[0][:], wu_v[:, :, jb * 256:(jb + 1) * 256])
                        for jj in range(2):
                            j = jb * 2 + jj
                            for sub in range(2):
                                pg, r_pg = nps(); pu, r_pu = nps()
                                xs = xn[:, :, sub * 512:(sub + 1) * 512]
                                for c in range(8):
                                    mm(pg[:, :], sg_[0][:, c, jj * 128:(jj + 1) * 128], xs[:, c, :], c == 0, c == 7, [sg_[1], r_xn], [r_pg])
                                for c in range(8):
                                    mm(pu[:, :], su_[0][:, c, jj * 128:(jj + 1) * 128], xs[:, c, :], c == 0, c == 7, [su_[1], r_xn], [r_pu])
                                sgt, r_sgt = sg[k % 2]; k += 1
                                A(lambda e, sgt=sgt, pg=pg: e.activation(out=sgt[:], in_=pg[:, :], func=AF.Silu), [r_pg], [r_sgt])
                                V(lambda e, sgt=sgt, pu=pu, j=j, sub=sub: e.tensor_tensor(
                                    out=hid[:, j, sub * 512:(sub + 1) * 512], in0=sgt[:], in1=pu[:, :], op=ALU.mult),
                                    [r_sgt, r_pu], [r_hid])
                    for ob in range(4):
                        sd_ = wds.next()
                        wload(sd_, sd_[0][:], wd_v[:, :, ob * 256:(ob + 1) * 256])
                        for oo in range(2):
                            o = ob * 2 + oo
                            for sub in range(2):
                                py, r_py = nps()
                                for j in range(NJ):
                                    mm(py[:, :], sd_[0][:, j, oo * 128:(oo + 1) * 128], hid[:, j, sub * 512:(sub + 1) * 512],
                                       j == 0, j == NJ - 1, [sd_[1], r_hid], [r_py])
                                t0 = tt * 1024 + sub * 512
                                V(lambda e, py=py, o=o, t0=t0: e.scalar_tensor_tensor(
                                    out=hT[:, o, t0:t0 + 512], in0=py[:, :], scalar=0.5, in1=hT[:, o, t0:t0 + 512],
                                    op0=ALU.mult, op1=ALU.add), [r_py, r_h], [r_h])
                S.barrier()

        def sin_of(stack, out, ang, shape, r_in, r_out, shift, tag):
            u = sbt(stack, "rr_u" + tag, shape, F32); ki = sbt(stack, "rr_k" + tag, shape, I32)
            kf = sbt(stack, "rr_f" + tag, shape, F32); r_t = Res()
            V(lambda e: e.tensor_scalar(out=u[:], in0=ang, scalar1=1.0 / TWO_PI, scalar2=shift / TWO_PI,
                                        op0=ALU.mult, op1=ALU.add), r_in, [r_t])
            V(lambda e: e.tensor_copy(out=ki[:], in_=u[:]), [r_t], [r_t])
            V(lambda e: e.tensor_copy(out=kf[:], in_=ki[:]), [r_t], [r_t])
            V(lambda e: e.tensor_tensor(out=u[:], in0=u[:], in1=kf[:], op=ALU.subtract), [r_t], [r_t])
            V(lambda e: e.tensor_scalar(out=kf[:], in0=u[:], scalar1=0.5, scalar2=-1.0, op0=ALU.is_gt, op1=ALU.mult), [r_t], [r_t])
            V(lambda e: e.tensor_tensor(out=u[:], in0=u[:], in1=kf[:], op=ALU.add), [r_t], [r_t])
            V(lambda e: e.tensor_scalar(out=kf[:], in0=u[:], scalar1=-0.5, scalar2=1.0, op0=ALU.is_lt, op1=ALU.mult), [r_t], [r_t])
            V(lambda e: e.tensor_tensor(out=u[:], in0=u[:], in1=kf[:], op=ALU.add), [r_t], [r_t])
            V(lambda e: e.tensor_scalar(out=u[:], in0=u[:], scalar1=0.5, scalar2=-0.5, op0=ALU.min, op1=ALU.max), [r_t], [r_t])
            A(lambda e: e.activation(out=out, in_=u[:], func=AF.Sin, scale=TWO_PI), [r_t], r_out)

        def qk_finish(ps, r_ps, gcol, cosT, sinT, r_cs, out_bf, r_out, tmp):
            sq, r_sq, rs, r_rs, qn, r_qn, t1, r_t1 = tmp
            A(lambda e: e.activation(out=sq[:], in_=ps[:, :], func=AF.Square), [r_ps], [r_sq])
            p2, r_p2 = nps()
            mm(p2[:, :], blkones, sq[:], True, True, [r_sq, r_cbf], [r_p2])
            A(lambda e: e.activation(out=rs[:], in_=p2[:, :], func=AF.Sqrt, bias=1e-6, scale=1.0 / 64), [r_p2], [r_rs])
            V(lambda e: e.reciprocal(out=rs[:], in_=rs[:]), [r_rs], [r_rs])
            V(lambda e: e.scalar_tensor_tensor(out=qn[:], in0=ps[:, :], scalar=vecs[:, gcol:gcol + 1], in1=rs[:],
                                               op0=ALU.mult, op1=ALU.mult), [r_ps, r_rs, r_vecs], [r_qn])
            p3, r_p3 = nps()
            mm(p3[:, :], rotm, qn[:], True, True, [r_qn, r_cf], [r_p3])
            V(lambda e: e.tensor_tensor(out=t1[:], in0=p3[:, :], in1=sinT, op=ALU.mult), [r_p3, r_cs], [r_t1])
            V(lambda e: e.tensor_tensor(out=qn[:], in0=qn[:], in1=cosT, op=ALU.mult), [r_qn, r_cs], [r_qn])
            V(lambda e: e.tensor_tensor(out=out_bf, in0=qn[:], in1=t1[:], op=ALU.add), [r_qn, r_t1], [r_out])

        def rope_tables(stack, t0, cosT, sinT, r_cs, posb, r_posb, ang, r_ang, d_pos, tag):
            S.dma("sync", d_pos, posb[:, 0, :], pos[0:1, t0:t0 + 512].partition_broadcast(128)[:, 0, :], writes=[r_posb])
            S.dma("sync", d_pos, posb[:, 1, :], pos[1:2, t0:t0 + 512].partition_broadcast(128)[:, 0, :], writes=[r_posb])
            V(lambda e: e.tensor_scalar(out=ang[:], in0=posb[:, 0, :], scalar1=vecs[:, 38:39], scalar2=None, op0=ALU.mult),
              [r_posb, r_vecs], [r_ang])
            V(lambda e: e.scalar_tensor_tensor(out=ang[:], in0=posb[:, 1, :], scalar=vecs[:, 39:40], in1=ang[:],
                                               op0=ALU.mult, op1=ALU.add), [r_posb, r_vecs, r_ang], [r_ang])
            with ExitStack() as tmp:
                sin_of(tmp, sinT[:], ang[:], [128, 512], [r_ang], [r_cs], 0.0, tag + "s")
                sin_of(tmp, cosT[:], ang[:], [128, 512], [r_ang], [r_cs], np.pi / 2, tag + "c")
                S.barrier()

        win_v = w_in.rearrange("(c p) n -> p c n", p=128)

        S.enabled = bool(PH['ffn1'])
        ffn(w1g, w1u, w1d, 0, "a")

        mix = ExitStack()
        zbuf = sbt(mix, "zbuf", [128, 4, NT], BF16); r_z = Res()
        S.enabled = bool(PH['mix1'])
        qT = sbt(mix, "qT", [128, 4, NT], BF16); r_q = Res()
        with ExitStack() as ph:
            cosT = sbt(ph, "cosT", [128, NT], F32); sinT = sbt(ph, "sinT", [128, NT], F32); r_cs = Res()
            with ExitStack() as tb0:
                posb = sbt(tb0, "posb", [128, 2, NT], F32); r_posb = Res(); ang = sbt(tb0, "ang", [128, NT], F32); r_ang = Res()
                d_pos = S.dma_sem()
                S.dma("sync", d_pos, posb[:, 0, :], pos[0:1, :].partition_broadcast(128)[:, 0, :], writes=[r_posb])
                S.dma("sync", d_pos, posb[:, 1, :], pos[1:2, :].partition_broadcast(128)[:, 0, :], writes=[r_posb])
                V(lambda e: e.tensor_scalar(out=ang[:], in0=posb[:, 0, :], scalar1=vecs[:, 38:39], scalar2=None, op0=ALU.mult),
                  [r_posb, r_vecs], [r_ang])
                V(lambda e: e.scalar_tensor_tensor(out=ang[:], in0=posb[:, 1, :], scalar=vecs[:, 39:40], in1=ang[:],
                                                   op0=ALU.mult, op1=ALU.add), [r_posb, r_vecs, r_ang], [r_ang])
                with ExitStack() as tb_:
                    sin_of(tb_, sinT[:], ang[:], [128, NT], [r_ang], [r_cs], 0.0, "rs")
                    S.barrier()
                with ExitStack() as tb_:
                    sin_of(tb_, cosT[:], ang[:], [128, NT], [r_ang], [r_cs], np.pi / 2, "rc")
                    S.barrier()
            tmpn = mk_norm_tmp(ph)
            xn = sbt(ph, "xn1", [128, 8, 512], BF16); r_xn = Res()
            wk = sbt(ph, "wkvx", [128, 8, 768], BF16); r_wk = Res(); d_wk = S.dma_sem()
            S.dma("gpsimd", d_wk, wk[:, :, 0:384], win_v[:, :, 512:896], writes=[r_wk])
            S.dma("gpsimd", d_wk, wk[:, :, 384:768], win_v[:, :, 896:1280], writes=[r_wk])
            wq = sbt(ph, "wq", [128, 8, 4, 128], BF16); r_wq = Res(); d_wq = S.dma_sem()
            for c in range(4):
                for hh in range(2):
                    col = (hh * 4 + c) * 64
                    S.dma("gpsimd", d_wq, wq[:, :, c, hh * 64:(hh + 1) * 64], win_v[:, :, col:col + 64], writes=[r_wq])
            kloc = sbt(ph, "kloc", [128, NT], BF16); r_kloc = Res()
            vloc = sbt(ph, "vloc", [128, 16, 194], BF16); r_vloc = Res()
            V(lambda e: e.memset(vloc[:], 0.0), [], [r_vloc])
            V(lambda e: e.memset(vloc[:, :, 64:65], 1.0), [], [r_vloc])
            V(lambda e: e.memset(vloc[:, :, 66:67], 1.0), [], [r_vloc])
            qkts = [(sbt(ph, "qk_sq%d" % i, [128, 512], BF16), Res(), sbt(ph, "qk_rs%d" % i, [128, 512], F32), Res(),
                     sbt(ph, "qk_qn%d" % i, [128, 512], F32), Res(), sbt(ph, "qk_t1%d" % i, [128, 512], F32), Res()) for i in range(2)]
            nq = 0
            for s4 in range(4):
                t0 = s4 * 512
                rmsnorm(ph, xn, r_xn, t0, 8, tmpn)
                ps, r_ps = nps()
                for c in range(8):
                    mm(ps[:, :], wk[:, c, 0:128], xn[:, c, :], c == 0, c == 7, [r_wk, r_xn], [r_ps])
                qk_finish(ps, r_ps, 33, cosT[:, t0:t0 + 512], sinT[:, t0:t0 + 512], r_cs, kloc[:, t0:t0 + 512], r_kloc, qkts[nq % 2]); nq += 1
                for tc in range(4):
                    pv, r_pv = nps()
                    for c in range(8):
                        mm(pv[:, 0:128], xn[:, c, tc * 128:(tc + 1) * 128], wk[:, c, 128:256], c == 0, c == 7, [r_wk, r_xn], [r_pv])
                    tcg = s4 * 4 + tc
                    V(lambda e, pv=pv, tcg=tcg: e.tensor_copy(out=vloc[:, tcg, 0:64], in_=pv[:, 0:64]), [r_pv], [r_vloc])
                    A(lambda e, pv=pv, tcg=tcg: e.activation(out=vloc[:, tcg, 130:194], in_=pv[:, 64:128], func=AF.Copy), [r_pv], [r_vloc])
                for cj in range(4):
                    px, r_px = nps()
                    for c in range(8):
                        mm(px[:, :], wk[:, c, 256 + cj * 128:256 + (cj + 1) * 128], xn[:, c, :], c == 0, c == 7, [r_wk, r_xn], [r_px])
                    A(lambda e, px=px, cj=cj, t0=t0: e.activation(out=zbuf[:, cj, t0:t0 + 512], in_=px[:, :], func=AF.Copy), [r_px], [r_z])
                for c4 in range(4):
                    ps, r_ps = nps()
                    for c in range(8):
                        mm(ps[:, :], wq[:, c, c4, :], xn[:, c, :], c == 0, c == 7, [r_wq, r_xn], [r_ps])
                    qk_finish(ps, r_ps, 32, cosT[:, t0:t0 + 512], sinT[:, t0:t0 + 512], r_cs, qT[:, c4, t0:t0 + 512], r_q, qkts[nq % 2]); nq += 1
                if s4 == 3:
                    pass
            d_kv = S.dma_sem(); d_kv2 = S.dma_sem(); r_kloc2 = Res(); r_vloc2 = Res(); r_kall = Res(); r_vall = Res()
            S.dma("sync", d_kv, k_loc, kloc[:], reads=[r_kloc], writes=[r_kloc2])
            S.dma("sync", d_kv2, v_loc, vloc[:].rearrange("p a b -> p (a b)"), reads=[r_vloc], writes=[r_vloc2])
            d_ag = S.dma_sem(); d_ag2 = S.dma_sem()
            S.coll(d_ag, lambda e: e.collective_compute("AllGather", ALU.bypass, replica_groups=[[0, 1, 2, 3], [4, 5, 6, 7]],
                                                        ins=[k_loc.opt()], outs=[k_all.opt()]), reads=[r_kloc2], writes=[r_kall])
            S.coll(d_ag2, lambda e: e.collective_compute("AllGather", ALU.bypass, replica_groups=[[0, 1, 2, 3], [4, 5, 6, 7]],
                                                         ins=[v_loc.opt()], outs=[v_all.opt()]), reads=[r_vloc2], writes=[r_vall])
            S.barrier()

        dump("kall", k_all, [512, 2048], [r_kall])
        dump("vall", v_all, [512, 16 * 194], [r_vall])
        dump("xssm", zbuf[:], [128, 4, NT], [r_z])
        dump("q", qT[:], [128, 4, NT], [r_q])
        d_sp0 = S.dma_sem(); r_hsp = Res()
        S.dma("sync", d_sp0, h_spill.rearrange("(c p) t -> p c t", p=128), hT[:], reads=[r_h], writes=[r_hsp])
        S.barrier()
        hflat = hT[:].rearrange("p c t -> p (c t)")
        with ExitStack() as ssm:
            r_s = Res()
            Cst = sbt(ssm, "Cst", [128, 32, 2, 128], BF16)
            Mop = sbt(ssm, "Mop", [128, 32, 128], BF16)
            lam8 = sbt(ssm, "lam8", [128, 2, 32], F32)
            ab = ExitStack()
            Bst = sbt(ab, "Bst", [128, 32, 2, 128], BF16)
            Bf = hflat[:, 0:4096].bitcast(BF16).rearrange("p (g r k) -> p g r k", g=32, r=2)
            CM = hflat[:, 4096:8192].bitcast(BF16).rearrange("p (g r k) -> p g r k", g=32, r=2)
            GH = hflat[:, 0:8256].bitcast(BF16).rearrange("p (r g c) -> p r g c", r=2, g=32)
            U = hflat[:, 8256:12352].bitcast(BF16).rearrange("p (g c) -> p g c", g=32)
            NS = 26
            S.enabled = bool(PH['ssm'] and SS['A'])
            with ExitStack() as sa:
                sp = sbt(sa, "ssmp", [128, 2144], F32); d_sp = S.dma_sem()
                S.dma("sync", d_sp, sp[:], ssmp_d, writes=[r_s])
                lamre = sp[:, 0:32]; lamim = sp[:, 32:64]; lstep = sp[:, 64:96]
                b3 = lambda a: a.rearrange("p (g h) -> p g h", h=16)
                bre = b3(sp[:, 96:608]); bim = b3(sp[:, 608:1120]); cre = b3(sp[:, 1120:1632]); cim = b3(sp[:, 1632:2144])
                def sm(name, n):
                    return sbt(sa, name, [128, n], F32)
                dl = sm("dl", 32); are = sm("are", 32); aim = sm("aim", 32)
                magE = sm("magE", 32 * NS); angE = sm("angE", 32 * NS); sinE = sm("sinE", 32 * NS); cosE = sm("cosE", 32 * NS)
                g3 = lambda a: a.rearrange("p (g s) -> p g s", s=NS)
                VV = lambda out, a, b, op: V(lambda e: e.tensor_tensor(out=out, in0=a, in1=b, op=op), [r_s, r_cf], [r_s])
                A(lambda e: e.activation(out=dl[:], in_=lstep, func=AF.Exp), [r_s], [r_s])
                VV(are[:], dl[:], lamre, ALU.mult); VV(aim[:], dl[:], lamim, ALU.mult)
                Eb = Etab.unsqueeze(1).broadcast_to([128, 32, NS])
                VV(g3(magE[:]), are[:].unsqueeze(2).broadcast_to([128, 32, NS]), Eb, ALU.mult)
                VV(g3(angE[:]), aim[:].unsqueeze(2).broadcast_to([128, 32, NS]), Eb, ALU.mult)
                A(lambda e: e.activation(out=magE[:], in_=magE[:], func=AF.Exp), [r_s], [r_s])
                with ExitStack() as tmp:
                    sin_of(tmp, sinE[:], angE[:], [128, 32 * NS], [r_s], [r_s], 0.0, "ss")
                    S.barrier()
                with ExitStack() as tmp:
                    sin_of(tmp, cosE[:], angE[:], [128, 32 * NS], [r_s], [r_s], np.pi / 2, "sc")
                    S.barrier()
                VV(cosE[:], cosE[:], magE[:], ALU.mult); VV(sinE[:], sinE[:], magE[:], ALU.mult)
                PWr = g3(cosE[:]); PWi = g3(sinE[:])
                V(lambda e: e.tensor_copy(out=lam8[:, 0, :], in_=PWr[:, :, 24]), [r_s], [r_s])
                V(lambda e: e.tensor_copy(out=lam8[:, 1, :], in_=PWi[:, :, 24]), [r_s], [r_s])
                nr = sm("nr", 32); ni = sm("ni", 32); den_ = sm("den_", 32); cr = sm("cr", 32); ci = sm("ci", 32); tq = sm("tq", 32)
                V(lambda e: e.tensor_scalar(out=nr[:], in0=PWr[:, :, 25], scalar1=-1.0, scalar2=None, op0=ALU.add), [r_s], [r_s])
                V(lambda e: e.tensor_copy(out=ni[:], in_=PWi[:, :, 25]), [r_s], [r_s])
                VV(den_[:], lamre, lamre, ALU.mult); VV(tq[:], lamim, lamim, ALU.mult); VV(den_[:], den_[:], tq[:], ALU.add)
                V(lambda e: e.reciprocal(out=den_[:], in_=den_[:]), [r_s], [r_s])
                VV(cr[:], nr[:], lamre, ALU.mult); VV(tq[:], ni[:], lamim, ALU.mult); VV(cr[:], cr[:], tq[:], ALU.add); VV(cr[:], cr[:], den_[:], ALU.mult)
                VV(ci[:], ni[:], lamre, ALU.mult); VV(tq[:], nr[:], lamim, ALU.mult); VV(ci[:], ci[:], tq[:], ALU.subtract); VV(ci[:], ci[:], den_[:], ALU.mult)
                bbr = sbt(sa, "bbr", [128, 32, 16], F32); bbi = sbt(sa, "bbi", [128, 32, 16], F32); tb = sbt(sa, "tb", [128, 32, 16], F32)
                crb = cr[:].unsqueeze(2).broadcast_to([128, 32, 16]); cib = ci[:].unsqueeze(2).broadcast_to([128, 32, 16])
                VV(bbr[:], bre, crb, ALU.mult); VV(tb[:], bim, cib, ALU.mult); VV(bbr[:], bbr[:], tb[:], ALU.subtract)
                VV(bbi[:], bim, crb, ALU.mult); VV(tb[:], bre, cib, ALU.mult); VV(bbi[:], bbi[:], tb[:], ALU.add)
                t1 = sbt(sa, "t1", [128, 8, 8, 16], F32); t2 = sbt(sa, "t2", [128, 8, 8, 16], F32)
                def table(dst, xr, xi, s0, negi):
                    for gq in range(4):
                        gs = slice(gq * 8, gq * 8 + 8)
                        xrb = xr[:, gs, :].unsqueeze(2).broadcast_to([128, 8, 8, 16]); xib = xi[:, gs, :].unsqueeze(2).broadcast_to([128, 8, 8, 16])
                        prb = PWr[:, gs, s0:s0 + 8].unsqueeze(3).broadcast_to([128, 8, 8, 16]); pib = PWi[:, gs, s0:s0 + 8].unsqueeze(3).broadcast_to([128, 8, 8, 16])
                        dr = dst[:, gs, 0, :].rearrange("p g (s h) -> p g s h", h=16); dim_ = dst[:, gs, 1, :].rearrange("p g (s h) -> p g s h", h=16)
                        VV(t1[:], xrb, prb, ALU.mult); VV(t2[:], xib, pib, ALU.mult); VV(dr, t1[:], t2[:], ALU.subtract)
                        VV(t1[:], xrb, pib, ALU.mult); VV(t2[:], xib, prb, ALU.mult)
                        if negi:
                            VV(t1[:], t1[:], t2[:], ALU.add)
                            V(lambda e, dim_=dim_: e.tensor_scalar(out=dim_, in0=t1[:], scalar1=-1.0, scalar2=None, op0=ALU.mult), [r_s], [r_s])
                        else:
                            VV(dim_, t1[:], t2[:], ALU.add)
                table(Bf, bbr[:], bbi[:], 0, False)
                table(CM, cre, cim, 8, True)
                table(Cst, cre, cim, 16, True)
                mt = sbt(sa, "mt", [128, 128], F32); mt2 = sbt(sa, "mt2", [128, 128], F32)
                for g in range(32):
                    pa, r_pa = nps(); pb_, r_pb_ = nps()
                    for ri in range(2):
                        mm(pa[:, 0:128], Bf[0:64, g, ri, :], CM[0:64, g, ri, :], ri == 0, ri == 1, [r_s], [r_pa])
                    for ri in range(2):
                        mm(pb_[:, 0:128], Bf[64:128, g, ri, :], CM[64:128, g, ri, :], ri == 0, ri == 1, [r_s], [r_pb_])
                    V(lambda e, pa=pa: e.tensor_tensor(out=mt[:], in0=pa[:, 0:128], in1=maskf, op=ALU.mult), [r_pa, r_cf, r_s], [r_s])
                    V(lambda e, pb_=pb_: e.tensor_tensor(out=mt2[:], in0=pb_[:, 0:128], in1=maskb, op=ALU.mult), [r_pb_, r_cf, r_s], [r_s])
                    VV(mt[:], mt[:], mt2[:], ALU.add)
                    V(lambda e, g=g: e.scalar_tensor_tensor(out=Mop[:, g, :], in0=identf, scalar=vecs[:, 40 + g:41 + g], in1=mt[:],
                                                          op0=ALU.mult, op1=ALU.add), [r_s, r_cf, r_vecs], [r_s])
                    for ri in range(2):
                        T(lambda e, g=g, ri=ri: e.transpose(out=pbf[:, ri * 128:(ri + 1) * 128], in_=Bf[:, g, ri, :], identity=identb), [r_s, r_cbf], [r_pbf])
                    A(lambda e, g=g: e.activation(out=Bst[:, g, :, :].rearrange("p r k -> p (r k)"), in_=pbf[:, 0:256], func=AF.Copy), [r_pbf], [r_s])
                S.barrier()
            S.enabled = bool(PH['ssm'])
            dump('Mop', Mop[:], [128, 32, 128], [r_s]); dump('Bst', Bst[:], [128, 32, 2, 128], [r_s]); dump('Cst', Cst[:], [128, 32, 2, 128], [r_s]); dump('lam8', lam8[:], [128, 2, 32], [r_s])
            S.enabled = bool(PH['ssm'] and SS['B'])
            with ExitStack() as sb_:
                selb = sbt(sb_, "selb", [128, 64, 128], BF16); d_sel = S.dma_sem(); r_selb = Res()
                r_Ug = [Res() for _ in range(32)]; r_ghv = Res(); r_gha = Res()
                for q4 in range(4):
                    S.dma("gpsimd", d_sel, selb[:, q4 * 16:(q4 + 1) * 16, :].rearrange("p a b -> p (a b)"), sel_d[:, q4 * 2048:(q4 + 1) * 2048], writes=[r_selb])
                for g in range(32):
                    kt, gl = g // 8, g % 8
                    pu_, r_pu_ = nps()
                    xv = zbuf[:, kt, :].rearrange("p (c i) -> p i c", i=8)
                    for i in range(8):
                        mm(pu_[:, 0:256], selb[:, gl * 8 + i, :], xv[:, i, :], i == 0, i == 7, [r_selb, r_z], [r_pu_])
                    V(lambda e, pu_=pu_, g=g: e.tensor_copy(out=U[:, g, :], in_=pu_[:, 0:256]), [r_pu_], [r_Ug[g]])
                for g in range(32):
                    for ri in range(2):
                        pg_, r_pg_ = nps()
                        mm(pg_[:, 0:256], Bst[:, g, ri, :], U[:, g, :], True, True, [r_Ug[g]], [r_pg_])
                        V(lambda e, pg_=pg_, g=g, ri=ri: e.tensor_copy(out=GH[0:64, ri, g, 2:258], in_=pg_[0:64, 0:256]), [r_pg_], [r_ghv])
                        A(lambda e, pg_=pg_, g=g, ri=ri: e.activation(out=GH[64:128, ri, g, 0:256], in_=pg_[64:128, 0:256], func=AF.Copy), [r_pg_], [r_gha])
                S.barrier()
            ab.close()
            S.enabled = bool(PH['ssm'])
            dump('U', U, [128, 32, 256], [r_s]); dump('GH0', GH, [128, 2, 32, 258], [r_s])
            S.enabled = bool(PH['ssm'] and SS['C'])
            kvst = ExitStack()
            S.enabled = bool(PH['attn'])
            Kall = sbt(kvst, "Kall", [128, 4 * NT], BF16); r_K = Res()
            Vall = sbt(kvst, "Vall", [128, 64, 194], BF16); r_V = Res()
            d_kl = S.dma_sem()
            for r in range(4):
                S.dma("sync", d_kl, Kall[:, r * NT:(r + 1) * NT], k_all[r * 128:(r + 1) * 128, :], reads=[r_kall], writes=[r_K])
                S.dma("sync", d_kl, Vall[:, r * 16:(r + 1) * 16, :].rearrange("p a b -> p (a b)"),
                      v_all[r * 128:(r + 1) * 128, :], reads=[r_vall], writes=[r_V])
            S.enabled = bool(PH['ssm'] and SS['C'])
            sc_ = ExitStack()
            if True:
                ENG = ['gpsimd']
                EO = lambda fn, r=(), w=(): S.op(ENG[0], fn, r, w)
                GG = lambda out, a, b, op: EO(lambda e: e.tensor_tensor(out=out, in0=a, in1=b, op=op), [r_s, r_cf], [r_s])
                St = [sbt(sc_, "St%d" % i, [128, 2, 32], F32) for i in range(2)]
                AR = sbt(sc_, "AR", [128, 2, 32], F32); AI = sbt(sc_, "AI", [128, 2, 32], F32)
                ta = sbt(sc_, "ta", [128, 2, 32], F32); tb2 = sbt(sc_, "tb2", [128, 2, 32], F32)
                EO(lambda e: e.tensor_copy(out=AR[:, 0, :], in_=lam8[:, 0, :]), [r_s], [r_s])
                EO(lambda e: e.tensor_copy(out=AR[:, 1, :], in_=lam8[:, 0, :]), [r_s], [r_s])
                EO(lambda e: e.tensor_scalar(out=AI[:, 0, :], in0=lam8[:, 1, :], scalar1=-1.0, scalar2=None, op0=ALU.mult), [r_s], [r_s])
                EO(lambda e: e.tensor_copy(out=AI[:, 1, :], in_=lam8[:, 1, :]), [r_s], [r_s])
                r_cur = [Res(), Res()]; r_ta = Res(); r_tb0 = Res(); r_tb1 = Res(); r_ghw = Res()
                def scan(write_hist):
                    prev = None
                    for s_ in range(256):
                        cur, nxt = St[s_ % 2], St[(s_ + 1) % 2]
                        rc, rn = r_cur[s_ % 2], r_cur[(s_ + 1) % 2]
                        cf_, cb_ = s_ + 2, 255 - s_
                        EO(lambda e, cur=cur: e.tensor_tensor(out=ta[:], in0=cur[:], in1=AR[:], op=ALU.mult), [rc, r_s], [r_ta])
                        EO(lambda e, cur=cur: e.tensor_tensor(out=tb2[:, 0, :], in0=cur[:, 1, :], in1=AI[:, 0, :], op=ALU.mult), [rc, r_s], [r_tb0])
                        EO(lambda e, cur=cur: e.tensor_tensor(out=tb2[:, 1, :], in0=cur[:, 0, :], in1=AI[:, 1, :], op=ALU.mult), [rc, r_s], [r_tb1])
                        if write_hist and prev is not None:
                            pcf, pcb = prev
                            EO(lambda e, cur=cur, pcf=pcf: e.tensor_copy(out=GH[0:64, :, :, pcf], in_=cur[0:64]), [rc], [r_ghw])
                            EO(lambda e, cur=cur, pcb=pcb: e.tensor_copy(out=GH[64:128, :, :, pcb], in_=cur[64:128]), [rc], [r_ghw])
                        EO(lambda e: e.tensor_tensor(out=ta[:], in0=ta[:], in1=tb2[:], op=ALU.add), [r_ta, r_tb0, r_tb1], [r_ta])
                        EO(lambda e, nxt=nxt, cf_=cf_: e.tensor_tensor(out=nxt[0:64], in0=ta[0:64], in1=GH[0:64, :, :, cf_], op=ALU.add), [r_ta], [rn])
                        EO(lambda e, nxt=nxt, cb_=cb_: e.tensor_tensor(out=nxt[64:128], in0=ta[64:128], in1=GH[64:128, :, :, cb_], op=ALU.add), [r_ta], [rn])
                        prev = (cf_, cb_)
                    if write_hist:
                        fin = St[0]; pcf, pcb = prev
                        EO(lambda e: e.tensor_copy(out=GH[0:64, :, :, pcf], in_=fin[0:64]), [r_cur[0]], [r_ghw])
                        EO(lambda e: e.tensor_copy(out=GH[64:128, :, :, pcb], in_=fin[64:128]), [r_cur[0]], [r_ghw])
                    EO(lambda e: e.tensor_copy(out=ta[:], in_=St[0][:]), [r_cur[0], r_cur[1], r_ta, r_tb0, r_tb1, r_ghw, r_s], [r_s, r_ta])
                EO(lambda e: e.memset(St[0][:], 0.0), [r_s], [r_s, r_cur[0]])
                scan(False)
                d_st = S.dma_sem(); d_st2 = S.dma_sem(); r_stl = Res(); r_sta = Res()
                S.dma("sync", d_st, st_loc, St[0][:].rearrange("p r g -> p (r g)"), reads=[r_s], writes=[r_stl])
                S.coll(d_st2, lambda e: e.collective_compute("AllGather", ALU.bypass, replica_groups=[[0, 1, 2, 3], [4, 5, 6, 7]],
                                                             ins=[st_loc.opt()], outs=[st_all.opt()]), reads=[r_stl], writes=[r_sta])
                Fall = sbt(sc_, "Fall", [128, 4, 2, 32], F32)
                S.dma("sync", d_st, Fall[:].rearrange("p r a g -> p r (a g)"), st_all.rearrange("(r p) f -> p r f", p=128), reads=[r_sta], writes=[r_s])
                Pw = sbt(sc_, "Pw", [128, 3, 2, 32], F32); tq2 = sbt(sc_, "tq2", [128, 32], F32); tq3 = sbt(sc_, "tq3", [128, 32], F32)
                EO(lambda e: e.tensor_copy(out=Pw[:, 1], in_=lam8[:]), [r_s], [r_s])
                def csq(dst, src):
                    GG(tq2[:], src[:, 0, :], src[:, 0, :], ALU.mult); GG(tq3[:], src[:, 1, :], src[:, 1, :], ALU.mult)
                    GG(tq3[:], tq2[:], tq3[:], ALU.subtract)
                    GG(tq2[:], src[:, 0, :], src[:, 1, :], ALU.mult)
                    EO(lambda e: e.tensor_scalar(out=dst[:, 1, :], in0=tq2[:], scalar1=2.0, scalar2=None, op0=ALU.mult), [r_s], [r_s])
                    EO(lambda e: e.tensor_copy(out=dst[:, 0, :], in_=tq3[:]), [r_s], [r_s])
                for _ in range(8):
                    csq(Pw[:, 1], Pw[:, 1])
                csq(Pw[:, 2], Pw[:, 1])
                EO(lambda e: e.memset(Pw[:, 0, 0, :], 1.0), [r_s], [r_s]); EO(lambda e: e.memset(Pw[:, 0, 1, :], 0.0), [r_s], [r_s])
                Sin_ = St[0]
                EO(lambda e: e.memset(Sin_[:], 0.0), [r_s], [r_s])
                for rr in range(4):
                    for k in range(3):
                        GG(ta[:, 0, :], Pw[:, k, 0, :], Fall[:, rr, 0, :], ALU.mult); GG(tb2[:, 0, :], Pw[:, k, 1, :], Fall[:, rr, 1, :], ALU.mult)
                        GG(ta[:, 0, :], ta[:, 0, :], tb2[:, 0, :], ALU.subtract)
                        GG(ta[:, 1, :], Pw[:, k, 0, :], Fall[:, rr, 1, :], ALU.mult); GG(tb2[:, 1, :], Pw[:, k, 1, :], Fall[:, rr, 0, :], ALU.mult)
                        GG(ta[:, 1, :], ta[:, 1, :], tb2[:, 1, :], ALU.add)
                        EO(lambda e, rr=rr, k=k: e.tensor_scalar(out=ta[:], in0=ta[:], scalar1=selc[:, rr * 3 + k:rr * 3 + k + 1], scalar2=None, op0=ALU.mult), [r_s, r_cf], [r_s])
                        GG(Sin_[:], Sin_[:], ta[:], ALU.add)
                EO(lambda e: e.tensor_copy(out=GH[0:64, :, :, 1], in_=Sin_[0:64]), [r_s], [r_s])
                EO(lambda e: e.tensor_copy(out=GH[64:128, :, :, 256], in_=Sin_[64:128]), [r_s], [r_s])
                ENG[0] = 'gpsimd'
                scan(True)
            S.enabled = bool(PH['attn'])
            with ExitStack() as ph:
                Pb = [(sbt(ph, "Pb%d" % i, [128, 1024], BF16), Res()) for i in range(3)]
                den = sbt(ph, "den", [128, 512], F32); r_den = Res()
                rec = sbt(ph, "rec", [128, 512], F32); r_rec = Res()
                qz = [[(sbt(ph, "qz%d%d" % (hh, b), [128, 512], BF16), Res()) for b in range(2)] for hh in range(2)]
                for hh in range(2):
                    for b in range(2):
                        V(lambda e, t_=qz[hh][b][0]: e.memset(t_[:], 0.0), [], [qz[hh][b][1]])
                pk = 0; hn = 0
                heads = [(qt, c4, hh) for qt in range(4) for c4 in range(4) for hh in range(2)]
                def qzcopy(n):
                    qt_, c4_, hh_ = heads[n]
                    qzt_, r_qz_ = qz[hh_][(n // 2) % 2]
                    lo_ = 64 * hh_
                    V(lambda e: e.tensor_copy(out=qzt_[lo_:lo_ + 64, :], in_=qT[lo_:lo_ + 64, c4_, qt_ * 512:(qt_ + 1) * 512]), [r_q], [r_qz_])
                qzcopy(0)
                for (qt, c4, hh) in heads:
                    if True:
                        if True:
                            q0 = qt * 512
                            lo = 64 * hh
                            qzt, r_qz = qz[hh][(hn // 2) % 2]
                            if hn + 1 < len(heads):
                                qzcopy(hn + 1)
                            po, r_po = pbanks[4 + (hn % 2)]; hn += 1
                            vcols = slice(0, 65) if hh == 0 else slice(66, 194)
                            M_ = 65 if hh == 0 else 128
                            def qkpair(j):
                                sc, r_sc = pbig[j % 2]
                                for u_ in range(2):
                                    kc = 2 * j + u_
                                    mm(sc[:, u_ * 512:(u_ + 1) * 512], Kall[:, kc * 128:(kc + 1) * 128], qzt[:], True, True, [r_K, r_qz], [r_sc])
                                return sc, r_sc
                            cur = qkpair(0)
                            for j in range(32):
                                nxt = qkpair(j + 1) if j < 31 else None
                                sc, r_sc = cur
                                pb, r_pb = Pb[pk % 3]; pk += 1
                                A(lambda e, pb=pb, sc=sc: e.activation(out=pb[:], in_=sc[:, :], func=AF.Exp, scale=0.125), [r_sc], [r_pb])
                                for u_ in range(2):
                                    kc = 2 * j + u_
                                    mm(po[0:M_, :], Vall[:, kc, vcols], pb[:, u_ * 512:(u_ + 1) * 512], kc == 0, kc == 63, [r_V, r_pb], [r_po])
                                cur = nxt
                            dp = 64 if hh == 0 else 0
                            V(lambda e, po=po, dp=dp: e.tensor_copy(out=den[dp:dp + 1, :], in_=po[dp:dp + 1, :]), [r_po], [r_den])
                            pbc, r_pbc = pbanks[6]
                            mm(pbc[:, :], onesf[dp:dp + 1, :], den[dp:dp + 1, :], True, True, [r_den, r_cf], [r_pbc])
                            V(lambda e, pbc=pbc: e.reciprocal(out=rec[:], in_=pbc[:, :]), [r_pbc], [r_rec])
                            V(lambda e, po=po, lo=lo, c4=c4, q0=q0: e.tensor_tensor(out=qT[lo:lo + 64, c4, q0:q0 + 512], in0=po[lo:lo + 64, :],
                                                                                    in1=rec[lo:lo + 64, :], op=ALU.mult), [r_po, r_rec], [r_q])
                S.barrier()

            S.enabled = bool(PH['ssm'])
            sc_.close()
            kvst.close()
            S.enabled = bool(PH['ssm'])
            dump('GH1', GH, [128, 2, 32, 258], [r_s])
            S.enabled = bool(PH['ssm'] and SS['D'])
            with ExitStack() as so:
                for g in range(32):
                    py_, r_py_ = nps()
                    mm(py_[:, 0:256], Mop[:, g, :], U[:, g, :], True, False, [r_Ug[g]], [r_py_])
                    for ri in range(2):
                        mm(py_[:, 0:256], Cst[:, g, ri, :], GH[:, ri, g, 1:257], False, ri == 1, [r_Ug[g]], [r_py_])
                    A(lambda e, py_=py_, g=g: e.activation(out=U[:, g, :], in_=py_[:, 0:256], func=AF.Gelu_apprx_tanh), [r_py_], [r_Ug[g]])
                selTb = sbt(so, "selTb", [128, 64, 128], BF16); d_selT = S.dma_sem(); r_selT = Res()
                for q4 in range(4):
                    S.dma("gpsimd", d_selT, selTb[:, q4 * 16:(q4 + 1) * 16, :].rearrange("p a b -> p (a b)"), selT_d[:, q4 * 2048:(q4 + 1) * 2048], writes=[r_selT])
                Z1 = sbt(so, "Z1", [128, 4, NT], BF16); r_Z1 = Res()
                for kt in range(4):
                    zv = Z1[:, kt, :].rearrange("p (c j) -> p j c", j=8)
                    for j in range(8):
                        pz, r_pz = nps()
                        for gl in range(8):
                            mm(pz[:, 0:256], selTb[:, gl * 8 + j, :], U[:, kt * 8 + gl, :], gl == 0, gl == 7, [r_selT, r_Ug[kt * 8 + gl]], [r_pz])
                        V(lambda e, pz=pz, zv=zv, j=j: e.tensor_copy(out=zv[:, j, :], in_=pz[:, 0:256]), [r_pz], [r_Z1])
                wglu_t = sbt(so, "wglu_t", [128, 4, 512], BF16); d_wgl = S.dma_sem(); r_wgl = Res()
                S.dma("gpsimd", d_wgl, wglu_t[:], wglu.rearrange("(c p) n -> p c n", p=128), writes=[r_wgl])
                sgls = [(sbt(so, "sgl%d" % i, [128, 512], F32), Res()) for i in range(2)]
                ng = 0
                for ot in range(4):
                    for s4 in range(4):
                        t0 = s4 * 512
                        pg2, r_pg2 = nps()
                        for kt in range(4):
                            mm(pg2[:, :], wglu_t[:, kt, ot * 128:(ot + 1) * 128], Z1[:, kt, t0:t0 + 512], kt == 0, kt == 3, [r_wgl, r_Z1], [r_pg2])
                        sgl, r_sgl = sgls[ng % 2]; ng += 1
                        A(lambda e, pg2=pg2, ot=ot, sgl=sgl: e.activation(out=sgl[:], in_=pg2[:, :], func=AF.Sigmoid, bias=vecs[:, 34 + ot:35 + ot]), [r_pg2, r_vecs], [r_sgl])
                        V(lambda e, ot=ot, t0=t0, sgl=sgl: e.tensor_tensor(out=zbuf[:, ot, t0:t0 + 512], in0=Z1[:, ot, t0:t0 + 512], in1=sgl[:], op=ALU.mult), [r_sgl, r_Z1], [r_z])
                S.barrier()
            S.enabled = bool(PH['ssm'])
            S.barrier()

        dump("attn", qT[:], [128, 4, NT], [r_q])
        dump("z", zbuf[:], [128, 4, NT], [r_z])
        d_rl = S.dma_sem()
        S.dma("sync", d_rl, hT[:], h_spill.rearrange("(c p) t -> p c t", p=128), reads=[r_hsp], writes=[r_h])
        S.enabled = bool(PH['merge'])
        with ExitStack() as ph:
            tmpn = mk_norm_tmp(ph)
            xn = sbt(ph, "xn3", [128, 8, 512], BF16); r_xn = Res()
            wab_t = sbt(ph, "wab_t", [128, 4, DM], BF16); r_wab = Res(); d_w3 = S.dma_sem()
            for c4 in range(4):
                S.dma("gpsimd", d_w3, wab_t[0:64, c4, :], wab[64 * c4:64 * c4 + 64, :], writes=[r_wab])
                S.dma("gpsimd", d_w3, wab_t[64:128, c4, :], wab[256 + 64 * c4:256 + 64 * c4 + 64, :], writes=[r_wab])
            wsb_t = sbt(ph, "wsb_t", [128, 4, DM], BF16); r_wsb = Res()
            S.dma("gpsimd", d_w3, wsb_t[:], wsb.rearrange("(c p) n -> p c n", p=128), writes=[r_wsb])
            wout_t = sbt(ph, "wout_t", [128, 8, DM], BF16); r_wout = Res()
            wout_v = wout.rearrange("(c p) n -> p c n", p=128)
            S.dma("gpsimd", d_w3, wout_t[:, 0:4, :], wout_v[:, 0:4, :], writes=[r_wout])
            S.dma("gpsimd", d_w3, wout_t[:, 4:8, :], wout_v[:, 4:8, :], writes=[r_wout])
            wgs = WStream(ph, "wgate", [128, 8, 2, 128])
            mrg = sbt(ph, "mrg", [128, 8, 512], BF16); r_mrg = Res()
            sa = sbt(ph, "sa", [128, 512], F32); r_sa = Res(); ss = sbt(ph, "ss", [128, 512], F32); r_ss = Res()
            for s4 in range(4):
                t0 = s4 * 512
                rmsnorm(ph, xn, r_xn, t0, 8, tmpn)
                for o in range(8):
                    wg_ = wgs.next()
                    wload(wg_, wg_[0][:, :, 0, :], win_v[:, :, 1280 + o * 128:1280 + (o + 1) * 128])
                    wload(wg_, wg_[0][:, :, 1, :], win_v[:, :, 2304 + o * 128:2304 + (o + 1) * 128])
                    pga, r_pga = nps(); pya, r_pya = nps(); pgs, r_pgs = nps(); pys, r_pys = nps()
                    for c in range(8):
                        mm(pga[:, :], wg_[0][:, c, 0, :], xn[:, c, :], c == 0, c == 7, [wg_[1], r_xn], [r_pga])
                    for c in range(4):
                        mm(pya[:, :], wab_t[:, c, o * 128:(o + 1) * 128], qT[:, c, t0:t0 + 512], c == 0, c == 3, [r_wab, r_q], [r_pya])
                    for c in range(8):
                        mm(pgs[:, :], wg_[0][:, c, 1, :], xn[:, c, :], c == 0, c == 7, [wg_[1], r_xn], [r_pgs])
                    for c in range(4):
                        mm(pys[:, :], wsb_t[:, c, o * 128:(o + 1) * 128], zbuf[:, c, t0:t0 + 512], c == 0, c == 3, [r_wsb, r_z], [r_pys])
                    A(lambda e, pga=pga: e.activation(out=sa[:], in_=pga[:, :], func=AF.Sigmoid), [r_pga], [r_sa])
                    A(lambda e, pgs=pgs: e.activation(out=ss[:], in_=pgs[:, :], func=AF.Sigmoid), [r_pgs], [r_ss])
                    V(lambda e, pya=pya: e.tensor_tensor(out=sa[:], in0=sa[:], in1=pya[:, :], op=ALU.mult), [r_sa, r_pya], [r_sa])
                    V(lambda e, pys=pys: e.tensor_tensor(out=ss[:], in0=ss[:], in1=pys[:, :], op=ALU.mult), [r_ss, r_pys], [r_ss])
                    V(lambda e, o=o: e.tensor_tensor(out=mrg[:, o, :], in0=sa[:], in1=ss[:], op=ALU.add), [r_sa, r_ss], [r_mrg])
                for o2 in range(8):
                    py, r_py = nps()
                    for o in range(8):
                        mm(py[:, :], wout_t[:, o, o2 * 128:(o2 + 1) * 128], mrg[:, o, :], o == 0, o == 7, [r_wout, r_mrg], [r_py])
                    V(lambda e, py=py, o2=o2, t0=t0: e.tensor_tensor(out=hT[:, o2, t0:t0 + 512], in0=py[:, :], in1=hT[:, o2, t0:t0 + 512], op=ALU.add),
                      [r_py, r_h], [r_h])
            S.barrier()

        mix.close()
        S.enabled = bool(PH['ffn2'])
        ffn(w2g, w2u, w2d, 16, "b")

        S.enabled = bool(PH['ple'])
        with ExitStack() as ph:
            tmpn = mk_norm_tmp(ph)
            xn = sbt(ph, "xn4", [128, 8, 512], BF16); r_xn = Res()
            wpg_t = sbt(ph, "wpg_t", [128, 8, DM], BF16); r_wpg = Res(); d_w4 = S.dma_sem()
            wpg_v = wpg.rearrange("(c p) n -> p c n", p=128)
            S.dma("gpsimd", d_w4, wpg_t[:, 0:4, :], wpg_v[:, 0:4, :], writes=[r_wpg])
            S.dma("gpsimd", d_w4, wpg_t[:, 4:8, :], wpg_v[:, 4:8, :], writes=[r_wpg])
            wpp_t = sbt(ph, "wpp_t", [128, 2, DM], BF16); r_wpp = Res()
            S.dma("gpsimd", d_w4, wpp_t[:], wpp.rearrange("(c p) n -> p c n", p=128), writes=[r_wpp])
            pb16 = sbt(ph, "pb16", [128, 2, NT], BF16); r_p16 = Res()
            pT_v = pT.rearrange("(c p) t -> p c t", p=128)
            for c in range(2):
                S.dma("gpsimd", d_w4, pb16[:, c, :], pT_v[:, c, :], writes=[r_p16])
            sgp = sbt(ph, "sgp", [128, 512], F32); r_sgp = Res()
            d_out = S.dma_sem()
            for s4 in range(4):
                t0 = s4 * 512
                rmsnorm(ph, xn, r_xn, t0, 24, tmpn)
                for o in range(8):
                    pg, r_pg = nps(); pl, r_pl = nps()
                    for c in range(8):
                        mm(pg[:, :], wpg_t[:, c, o * 128:(o + 1) * 128], xn[:, c, :], c == 0, c == 7, [r_wpg, r_xn], [r_pg])
                    for c in range(2):
                        mm(pl[:, :], wpp_t[:, c, o * 128:(o + 1) * 128], pb16[:, c, t0:t0 + 512], c == 0, c == 1, [r_wpp, r_p16], [r_pl])
                    A(lambda e, pg=pg: e.activation(out=sgp[:], in_=pg[:, :], func=AF.Sigmoid), [r_pg], [r_sgp])
                    V(lambda e, pl=pl: e.tensor_tensor(out=sgp[:], in0=sgp[:], in1=pl[:, :], op=ALU.mult), [r_sgp, r_pl], [r_sgp])
                    V(lambda e, o=o, t0=t0: e.tensor_tensor(out=hT[:, o, t0:t0 + 512], in0=sgp[:], in1=hT[:, o, t0:t0 + 512], op=ALU.add),
                      [r_sgp, r_h], [r_h])
                S.dma("sync", d_out, outT.rearrange("(c p) t -> p c t", p=128)[:, :, t0:t0 + 512], hT[:, :, t0:t0 + 512], reads=[r_h])
            S.barrier()
        S.enabled = True
        if not PH['ple']:
            d_o2 = S.dma_sem()
            S.dma("sync", d_o2, outT.rearrange("(c p) t -> p c t", p=128), hT[:], reads=[r_h])
        for e_ in S.ENGS:
            S.wait_all(e_)
        S.emit()
    return nc


_NC_CACHE = {}


def _consts():
    f = np.float32
    ident = np.eye(128, dtype=f)
    rot = np.zeros((128, 128), f)
    for m in range(128):
        d = m % 64
        if d < 32:
            rot[m + 32, m] = -1.0
        else:
            rot[m - 32, m] = 1.0
    ones = np.ones((128, 128), f)
    ii = np.arange(128) // 16
    maskf = (ii[None, :] >= ii[:, None]).astype(f)
    maskb = (ii[:, None] >= ii[None, :]).astype(f)
    E = np.zeros((128, 26), f)
    s = np.arange(8)
    E[:64, 0:8] = 7 - s; E[64:, 0:8] = s
    E[:64, 8:16] = s - 7; E[64:, 8:16] = -s
    E[:64, 16:24] = s + 1; E[64:, 16:24] = 8 - s
    E[:, 24] = 8; E[:, 25] = 1
    sel = np.zeros((64, 128, 128), f); selT = np.zeros((64, 128, 128), f)
    for gl in range(8):
        for i in range(8):
            for h in range(16):
                sel[gl * 8 + i, gl * 16 + h, i * 16 + h] = 1.0
                selT[gl * 8 + i, i * 16 + h, gl * 16 + h] = 1.0
    sel = np.ascontiguousarray(sel.transpose(1, 0, 2)).reshape(128, 64 * 128)
    selT = np.ascontiguousarray(selT.transpose(1, 0, 2)).reshape(128, 64 * 128)
    return ident, rot, ones, maskf, maskb, E, sel, selT


def kernel(x, p, ffn1_norm, ffn1_w_gate, ffn1_w_up, ffn1_w_down, mix_norm, w_in,
           q_norm, k_norm, ssm_lambda_re, ssm_lambda_im, ssm_log_step, ssm_b_re,
           ssm_b_im, ssm_c_re, ssm_c_im, ssm_d, ssm_glu_w, ssm_glu_b, w_attn_branch,
           w_ssm_branch, w_out, ffn2_norm, ffn2_w_gate, ffn2_w_up, ffn2_w_down,
           ple_norm, ple_w_gate, ple_w_proj):
    f = np.float32
    A_ = lambda a: np.ascontiguousarray(np.asarray(a, dtype=f))
    x = A_(x); p = A_(p)
    if "nc" not in _NC_CACHE:
        _NC_CACHE["nc"] = build()
    nc = _NC_CACHE["nc"]
    ident, rot, ones, maskf, maskb, E, sel, selT = _consts()
    vecs = np.zeros((128, 72), f)
    pc = lambda v: A_(v).reshape(8, 128).T
    vecs[:, 0:8] = pc(ffn1_norm[0]); vecs[:, 8:16] = pc(mix_norm[0]); vecs[:, 16:24] = pc(ffn2_norm[0]); vecs[:, 24:32] = pc(ple_norm[0])
    vecs[:, 32] = np.tile(A_(q_norm[0]), 2); vecs[:, 33] = np.tile(A_(k_norm[0]), 2)
    vecs[:, 34:38] = A_(ssm_glu_b[0]).reshape(4, 128).T
    freqs = (10000.0 ** (-np.arange(0, 32, 2, dtype=np.float32) / 32)).astype(f)
    for pp in range(128):
        dd = (pp % 64) % 32
        if dd < 16:
            vecs[pp, 38] = freqs[dd]
        else:
            vecs[pp, 39] = freqs[dd - 16]
    vecs[:, 40:72] = np.tile(A_(ssm_d[0]).reshape(32, 16).T, (8, 1))
    def dn(a):
        return A_(a).transpose(0, 2, 1).reshape(128, 32)
    lamre = dn(ssm_lambda_re[0]); lamim = dn(ssm_lambda_im[0])
    lstep = np.repeat(A_(ssm_log_step[0])[:, None, :], 64, axis=1).reshape(128, 32)
    bre = A_(ssm_b_re[0]).transpose(0, 2, 1, 3).reshape(128, 512)
    bim = A_(ssm_b_im[0]).transpose(0, 2, 1, 3).reshape(128, 512)
    cre = A_(ssm_c_re[0]).transpose(0, 3, 1, 2).reshape(128, 512)
    cim = A_(ssm_c_im[0]).transpose(0, 3, 1, 2).reshape(128, 512)
    ssmp = np.ascontiguousarray(np.concatenate([lamre, lamim, lstep, bre, bim, cre, cim], axis=1))
    shared = dict(vecs=vecs, sel=sel, selT=selT, ssmp=ssmp,
                  w1g=A_(ffn1_w_gate[0]), w1u=A_(ffn1_w_up[0]), w1d=A_(ffn1_w_down[0]),
                  w2g=A_(ffn2_w_gate[0]), w2u=A_(ffn2_w_up[0]), w2d=A_(ffn2_w_down[0]),
                  w_in=A_(w_in[0]), wglu=A_(ssm_glu_w[0]), wab=A_(w_attn_branch[0]), wsb=A_(w_ssm_branch[0]),
                  wout=A_(w_out[0]), wpg=A_(ple_w_gate[0]), wpp=A_(ple_w_proj[0]))
    in_maps = []
    for r in range(8):
        b, q = r // 4, r % 4
        t0 = q * NT
        tok = np.arange(t0, t0 + NT)
        pos = np.stack([(tok // 64).astype(f), (tok % 64).astype(f)], axis=0)
        selc = np.zeros((128, 12), f)
        for rr in range(4):
            if rr < q:
                selc[:64, rr * 3 + (q - 1 - rr)] = 1.0
            if rr > q:
                selc[64:, rr * 3 + (rr - q - 1)] = 1.0
        cf32 = np.ascontiguousarray(np.concatenate([ident, rot, ones, maskf, maskb, E, selc], axis=1))
        m = dict(shared)
        m["xT"] = np.ascontiguousarray(x[b, t0:t0 + NT, :].T)
        m["pT"] = np.ascontiguousarray(p[0, b, t0:t0 + NT, :].T)
        m["pos"] = np.ascontiguousarray(pos)
        m["cf32"] = cf32
        in_maps.append(m)
    res = run_bass_kernel_spmd(nc, in_maps, core_ids=list(range(8)), **({'trace': True} if DEBUG.get('trace') else {}))
    DEBUG['last'] = res
    out = np.zeros((2, 8192, DM), f)
    for r in range(8):
        b, q = r // 4, r % 4
        out[b, q * NT:(q + 1) * NT, :] = np.asarray(res.results[r]["outT"]).T
    return out
```

```python
import numpy as np
import ml_dtypes
from contextlib import ExitStack
import concourse.bass as bass
import concourse.mybir as mybir
from concourse.bass_utils import run_bass_kernel_spmd

F32 = mybir.dt.float32
BF16 = mybir.dt.bfloat16
I32 = mybir.dt.int32
AF = mybir.ActivationFunctionType
ALU = mybir.AluOpType


class Res:
    __slots__ = ("name", "last_w", "readers")

    def __init__(self, name=""):
        self.name = name
        self.last_w = None
        self.readers = {}


class Sched:
    ENGS = ("tensor", "vector", "scalar", "gpsimd", "sync")

    def __init__(self, nc, stack):
        self.nc = nc
        self.stack = stack
        self.sems = {}
        self.count = {}
        self.prog = {e: [] for e in self.ENGS}
        self.seen = {e: {} for e in self.ENGS}
        for e in self.ENGS:
            self.sems[e] = stack.enter_context(nc.semaphore("s_" + e))
            self.count[e] = 0
        self.ndma = 0
        self.enabled = True

    def dma_sem(self, name=None):
        key = "dma%d" % self.ndma
        self.ndma += 1
        self.sems[key] = self.stack.enter_context(self.nc.semaphore("s_" + key))
        self.count[key] = 0
        return key

    def _wait(self, eng, key, val):
        if val <= 0:
            return
        if self.seen[eng].get(key, 0) >= val:
            return
        self.seen[eng][key] = val
        sem = self.sems[key]
        self.prog[eng].append(lambda e, sem=sem, val=val: e.wait_ge(sem, val))

    def _deps(self, eng, reads, writes):
        deps = []
        for r in reads:
            if r.last_w is not None:
                deps.append(r.last_w)
        for w in writes:
            if w.last_w is not None:
                deps.append(w.last_w)
            deps.extend(w.readers.items())
        for key, val in deps:
            if key == eng and eng == "tensor":
                continue
            self._wait(eng, key, val)

    def op(self, eng, fn, reads=(), writes=()):
        if not self.enabled:
            return 0
        self._deps(eng, reads, writes)
        self.count[eng] += 1
        v = self.count[eng]
        sem = self.sems[eng]
        self.prog[eng].append(lambda e, fn=fn, sem=sem: fn(e).then_inc(sem, 1))
        for r in reads:
            r.readers[eng] = v
        for w in writes:
            w.last_w = (eng, v)
            w.readers = {}
        return v

    def dma(self, eng, dkey, out, in_, reads=(), writes=(), **kw):
        if not self.enabled:
            return 0
        self._deps(eng, reads, writes)
        self.count[dkey] += 16
        v = self.count[dkey]
        sem = self.sems[dkey]
        self.prog[eng].append(
            lambda e, out=out, in_=in_, sem=sem, kw=kw: e.dma_start(out=out, in_=in_, **kw).then_inc(sem, 16))
        for r in reads:
            r.readers[dkey] = v
        for w in writes:
            w.last_w = (dkey, v)
            w.readers = {}
        return v

    def coll(self, dkey, fn, reads=(), writes=()):
        eng = "gpsimd"
        if not self.enabled:
            return 0
        self._deps(eng, reads, writes)
        self.count[dkey] += 1
        v = self.count[dkey]
        sem = self.sems[dkey]
        self.prog[eng].append(lambda e, fn=fn, sem=sem: fn(e).then_inc(sem, 1))
        for r in reads:
            r.readers[dkey] = v
        for w in writes:
            w.last_w = (dkey, v)
            w.readers = {}
        return v

    def raw(self, eng, fn):
        self.prog[eng].append(fn)

    def wait_all(self, eng):
        for key, val in self.count.items():
            if key != eng or True:
                self._wait(eng, key, val)

    def barrier(self):
        snap = dict(self.count)
        for e in self.ENGS:
            for key, val in snap.items():
                self._wait(e, key, val)

    def emit(self):
        nc = self.nc
        with nc.Block() as block:
            @block.tensor
            def _(eng):
                for f in self.prog["tensor"]:
                    f(eng)

            @block.vector
            def _(eng):
                for f in self.prog["vector"]:
                    f(eng)

            @block.scalar
            def _(eng):
                for f in self.prog["scalar"]:
                    f(eng)

            @block.gpsimd
            def _(eng):
                for f in self.prog["gpsimd"]:
                    f(eng)

            @block.sync
            def _(eng):
                for f in self.prog["sync"]:
                    f(eng)

NT = 2048
DM = 1024
DFF = 2816
NJ = 22
TWO_PI = 2.0 * np.pi
DEBUG = {}
SS = dict(A=1, B=1, C=1, D=1)
PH = dict(ffn1=1, mix1=1, mix2=1, ssm=1, attn=1, merge=1, ffn2=1, ple=1)


def build(dbg=False):
    nc = bass.Bass("TRN2", target_bir_lowering=False)
    di = lambda name, shape, dt=F32: nc.dram_tensor(name, shape, dt, kind="ExternalInput").ap()
    xT = di("xT", [DM, NT]); pT = di("pT", [256, NT]); pos = di("pos", [2, NT])
    vecs_d = di("vecs", [128, 72]); cf32_d = di("cf32", [128, 5 * 128 + 26 + 12])
    sel_d = di("sel", [128, 64 * 128]); selT_d = di("selT", [128, 64 * 128])
    ssmp_d = di("ssmp", [128, 96 + 2048])
    w1g = di("w1g", [DM, DFF]); w1u = di("w1u", [DM, DFF]); w1d = di("w1d", [DFF, DM])
    w2g = di("w2g", [DM, DFF]); w2u = di("w2u", [DM, DFF]); w2d = di("w2d", [DFF, DM])
    w_in = di("w_in", [DM, 3328]); wglu = di("wglu", [512, 512]); wab = di("wab", [512, DM])
    wsb = di("wsb", [512, DM]); wout = di("wout", [DM, DM]); wpg = di("wpg", [DM, DM]); wpp = di("wpp", [256, DM])
    outT = nc.dram_tensor("outT", [DM, NT], F32, kind="ExternalOutput").ap()
    k_loc = nc.dram_tensor("k_loc", [128, 2048], BF16).ap()
    k_all = nc.dram_tensor("k_all", [512, 2048], BF16).ap()
    v_loc = nc.dram_tensor("v_loc", [128, 16 * 194], BF16).ap()
    v_all = nc.dram_tensor("v_all", [512, 16 * 194], BF16).ap()
    st_loc = nc.dram_tensor("st_loc", [128, 64], F32).ap()
    st_all = nc.dram_tensor("st_all", [512, 64], F32).ap()
    h_spill = nc.dram_tensor("h_spill", [DM, NT], F32).ap()
    dbg_out = {}

    with ExitStack() as st:
        S = Sched(nc, st)
        _uid = [0]
        def sbt(stack, name, shape, dt):
            _uid[0] += 1
            return stack.enter_context(nc.sbuf_tensor("sb%d_%s" % (_uid[0], name), shape, dt))
        V = lambda fn, r=(), w=(): S.op("vector", fn, r, w)
        A = lambda fn, r=(), w=(): S.op("scalar", fn, r, w)
        G = lambda fn, r=(), w=(): S.op("gpsimd", fn, r, w)
        T = lambda fn, r=(), w=(): S.op("tensor", fn, r, w)
        def mm(ps, lhsT, rhs, start, stop, r=(), w=()):
            T(lambda e: e.matmul(ps, lhsT=lhsT, rhs=rhs, start=start, stop=stop), r, w)

        d_dbg = S.dma_sem()
        def dump(name, src, shape, reads):
            if not DEBUG.get(name):
                return
            o = nc.dram_tensor("dbg_" + name, list(shape), F32, kind="ExternalOutput").ap()
            en = S.enabled; S.enabled = True
            S.dma("gpsimd", d_dbg, o, src, reads=reads)
            S.enabled = en
        hT = sbt(st, "hT", [128, 8, NT], F32); r_h = Res()
        vecs = sbt(st, "vecs", [128, 72], F32); r_vecs = Res()
        cf32 = sbt(st, "cf32", [128, 5 * 128 + 38], F32); r_cf = Res()
        cbf = sbt(st, "cbf", [128, 3 * 128], BF16); r_cbf = Res()
        identf = cf32[:, 0:128]; rotm = cf32[:, 128:256]; onesf = cf32[:, 256:384]
        maskf = cf32[:, 384:512]; maskb = cf32[:, 512:640]; Etab = cf32[:, 640:666]; selc = cf32[:, 666:678]
        identb = cbf[:, 0:128]; onesb = cbf[:, 128:256]; blkones = cbf[:, 256:384]
        d_in = S.dma_sem()
        S.dma("sync", d_in, hT[:], xT.rearrange("(c p) t -> p c t", p=128), writes=[r_h])
        S.dma("sync", d_in, vecs[:], vecs_d, writes=[r_vecs])
        S.dma("sync", d_in, cf32[:], cf32_d, writes=[r_cf])
        V(lambda e: e.tensor_copy(out=cbf[:, 0:128], in_=cf32[:, 0:128]), [r_cf], [r_cbf])
        V(lambda e: e.tensor_copy(out=cbf[:, 128:256], in_=cf32[:, 256:384]), [r_cf], [r_cbf])
        V(lambda e: e.memset(cbf[:, 256:384], 0.0), [], [r_cbf])
        V(lambda e: e.memset(cbf[0:64, 256:320], 1.0), [], [r_cbf])
        V(lambda e: e.memset(cbf[64:128, 320:384], 1.0), [], [r_cbf])

        pbig = []
        pbanks = []
        for i in range(3):
            t_ = st.enter_context(nc.psum_tensor("pbig%d" % i, [128, 1024], F32))
            pbig.append((t_, Res()))
            pbanks.append((t_[:, 0:512], Res())); pbanks.append((t_[:, 512:1024], Res()))
        pbanks.append((st.enter_context(nc.psum_tensor("pb6", [128, 512], F32)), Res()))
        pbf = st.enter_context(nc.psum_tensor("pbf", [128, 1024], BF16)); r_pbf = Res()
        pctr = [0]
        def nps():
            p = pbanks[pctr[0] % 7]; pctr[0] += 1
            return p

        class WStream:
            def __init__(self, stack, name, shape, n=2, dt=BF16):
                self.slots = [(sbt(stack, "%s%d" % (name, i), shape, dt), Res(), S.dma_sem()) for i in range(n)]
                self.i = 0
            def next(self):
                s = self.slots[self.i % len(self.slots)]; self.i += 1
                return s
        def wload(slot, dst, src):
            tile, res, dk = slot
            S.dma("gpsimd", dk, dst, src, writes=[res])

        def rmsnorm(stack_tmp, xn, r_xn, t0, gcol, tmp):
            sq, r_sq, rs, r_rs = tmp["sq"], tmp["r_sq"], tmp["rs"], tmp["r_rs"]
            A(lambda e: e.activation(out=sq[:], in_=hT[:, :, t0:t0 + 512], func=AF.Square), [r_h], [r_sq])
            ps, r_ps = nps()
            for c in range(8):
                mm(ps[:, :], onesb, sq[:, c, :], c == 0, c == 7, [r_sq, r_cbf], [r_ps])
            A(lambda e: e.activation(out=rs[:], in_=ps[:, :], func=AF.Sqrt, bias=1e-6, scale=1.0 / DM), [r_ps], [r_rs])
            V(lambda e: e.reciprocal(out=rs[:], in_=rs[:]), [r_rs], [r_rs])
            for c in range(8):
                V(lambda e, c=c: e.scalar_tensor_tensor(out=xn[:, c, :], in0=hT[:, c, t0:t0 + 512],
                                                       scalar=vecs[:, gcol + c:gcol + c + 1], in1=rs[:],
                                                       op0=ALU.mult, op1=ALU.mult), [r_h, r_rs, r_vecs], [r_xn])

        def mk_norm_tmp(stack):
            return dict(sq=sbt(stack, "sq", [128, 8, 512], BF16), r_sq=Res(),
                        rs=sbt(stack, "rs", [128, 512], F32), r_rs=Res())

        def ffn(wg, wu, wd, gcol, tag, hook=None):
            with ExitStack() as ph:
                tmp = mk_norm_tmp(ph)
                xn = sbt(ph, "xn" + tag, [128, 8, 1024], BF16); r_xn = Res()
                hid = sbt(ph, "hid" + tag, [128, NJ, 1024], BF16); r_hid = Res()
                sg = [(sbt(ph, "sg%s%d" % (tag, i), [128, 512], F32), Res()) for i in range(2)]
                wgs = WStream(ph, "wg" + tag, [128, 8, 256]); wus = WStream(ph, "wu" + tag, [128, 8, 256])
                wds = WStream(ph, "wd" + tag, [128, NJ, 256])
                wg_v = wg.rearrange("(c p) n -> p c n", p=128); wu_v = wu.rearrange("(c p) n -> p c n", p=128)
                wd_v = wd.rearrange("(j p) n -> p j n", p=128)
                for tt in range(2):
                    if tt == 1 and hook is not None:
                        hook()
                    for sub in range(2):
                        rmsnorm(ph, xn[:, :, sub * 512:(sub + 1) * 512], r_xn, tt * 1024 + sub * 512, gcol, tmp)
                    k = 0
                    for jb in range(11):
                        sg_, su_ = wgs.next(), wus.next()
                        wload(sg_, sg_[0][:], wg_v[:, :, jb * 256:(jb + 1) * 256])
                        wload(su_, su_[0][:], wu_v[:, :, jb * 256:(jb + 1) * 256])
                        for jj in range(2):
                            j = jb * 2 + jj
                            for sub in range(2):
                                pg, r_pg = nps(); pu, r_pu = nps()
                                xs = xn[:, :, sub * 512:(sub + 1) * 512]
                                for c in range(8):
                                    mm(pg[:, :], sg_[0][:, c, jj * 128:(jj + 1) * 128], xs[:, c, :], c == 0, c == 7, [sg_[1], r_xn], [r_pg])
                                for c in range(8):
                                    mm(pu[:, :], su_[0][:, c, jj * 128:(jj + 1) * 128], xs[:, c, :], c == 0, c == 7, [su_[1], r_xn], [r_pu])
                                sgt, r_sgt = sg[k % 2]; k += 1
                                A(lambda e, sgt=sgt, pg=pg: e.activation(out=sgt[:], in_=pg[:, :], func=AF.Silu), [r_pg], [r_sgt])
                                V(lambda e, sgt=sgt, pu=pu, j=j, sub=sub: e.tensor_tensor(
                                    out=hid[:, j, sub * 512:(sub + 1) * 512], in0=sgt[:], in1=pu[:, :], op=ALU.mult),
                                    [r_sgt, r_pu], [r_hid])
                    for ob in range(4):
                        sd_ = wds.next()
                        wload(sd_, sd_[0][:], wd_v[:, :, ob * 256:(ob + 1) * 256])
                        for oo in range(2):
                            o = ob * 2 + oo
                            for sub in range(2):
                                py, r_py = nps()
                                for j in range(NJ):
                                    mm(py[:, :], sd_[0][:, j, oo * 128:(oo + 1) * 128], hid[:, j, sub * 512:(sub + 1) * 512],
                                       j == 0, j == NJ - 1, [sd_[1], r_hid], [r_py])
                                t0 = tt * 1024 + sub * 512
                                V(lambda e, py=py, o=o, t0=t0: e.scalar_tensor_tensor(
                                    out=hT[:, o, t0:t0 + 512], in0=py[:, :], scalar=0.5, in1=hT[:, o, t0:t0 + 512],
                                    op0=ALU.mult, op1=ALU.add), [r_py, r_h], [r_h])
                S.barrier()

        def sin_of(stack, out, ang, shape, r_in, r_out, shift, tag):
            u = sbt(stack, "rr_u" + tag, shape, F32); ki = sbt(stack, "rr_k" + tag, shape, I32)
            kf = sbt(stack, "rr_f" + tag, shape, F32); r_t = Res()
            V(lambda e: e.tensor_scalar(out=u[:], in0=ang, scalar1=1.0 / TWO_PI, scalar2=shift / TWO_PI,
                                        op0=ALU.mult, op1=ALU.add), r_in, [r_t])
            V(lambda e: e.tensor_copy(out=ki[:], in_=u[:]), [r_t], [r_t])
            V(lambda e: e.tensor_copy(out=kf[:], in_=ki[:]), [r_t], [r_t])
            V(lambda e: e.tensor_tensor(out=u[:], in0=u[:], in1=kf[:], op=ALU.subtract), [r_t], [r_t])
            V(lambda e: e.tensor_scalar(out=kf[:], in0=u[:], scalar1=0.5, scalar2=-1.0, op0=ALU.is_gt, op1=ALU.mult), [r_t], [r_t])
            V(lambda e: e.tensor_tensor(out=u[:], in0=u[:], in1=kf[:], op=ALU.add), [r_t], [r_t])
            V(lambda e: e.tensor_scalar(out=kf[:], in0=u[:], scalar1=-0.5, scalar2=1.0, op0=ALU.is_lt, op1=ALU.mult), [r_t], [r_t])
            V(lambda e: e.tensor_tensor(out=u[:], in0=u[:], in1=kf[:], op=ALU.add), [r_t], [r_t])
            V(lambda e: e.tensor_scalar(out=u[:], in0=u[:], scalar1=0.5, scalar2=-0.5, op0=ALU.min, op1=ALU.max), [r_t], [r_t])
            A(lambda e: e.activation(out=out, in_=u[:], func=AF.Sin, scale=TWO_PI), [r_t], r_out)

        def qk_finish(ps, r_ps, gcol, cosT, sinT, r_cs, out_bf, r_out, tmp):
            sq, r_sq, rs, r_rs, qn, r_qn, t1, r_t1 = tmp
            A(lambda e: e.activation(out=sq[:], in_=ps[:, :], func=AF.Square), [r_ps], [r_sq])
            p2, r_p2 = nps()
            mm(p2[:, :], blkones, sq[:], True, True, [r_sq, r_cbf], [r_p2])
            A(lambda e: e.activation(out=rs[:], in_=p2[:, :], func=AF.Sqrt, bias=1e-6, scale=1.0 / 64), [r_p2], [r_rs])
            V(lambda e: e.reciprocal(out=rs[:], in_=rs[:]), [r_rs], [r_rs])
            V(lambda e: e.scalar_tensor_tensor(out=qn[:], in0=ps[:, :], scalar=vecs[:, gcol:gcol + 1], in1=rs[:],
                                               op0=ALU.mult, op1=ALU.mult), [r_ps, r_rs, r_vecs], [r_qn])
            p3, r_p3 = nps()
            mm(p3[:, :], rotm, qn[:], True, True, [r_qn, r_cf], [r_p3])
            V(lambda e: e.tensor_tensor(out=t1[:], in0=p3[:, :], in1=sinT, op=ALU.mult), [r_p3, r_cs], [r_t1])
            V(lambda e: e.tensor_tensor(out=qn[:], in0=qn[:], in1=cosT, op=ALU.mult), [r_qn, r_cs], [r_qn])
            V(lambda e: e.tensor_tensor(out=out_bf, in0=qn[:], in1=t1[:], op=ALU.add), [r_qn, r_t1], [r_out])

        def rope_tables(stack, t0, cosT, sinT, r_cs, posb, r_posb, ang, r_ang, d_pos, tag):
            S.dma("sync", d_pos, posb[:, 0, :], pos[0:1, t0:t0 + 512].partition_broadcast(128)[:, 0, :], writes=[r_posb])
            S.dma("sync", d_pos, posb[:, 1, :], pos[1:2, t0:t0 + 512].partition_broadcast(128)[:, 0, :], writes=[r_posb])
            V(lambda e: e.tensor_scalar(out=ang[:], in0=posb[:, 0, :], scalar1=vecs[:, 38:39], scalar2=None, op0=ALU.mult),
              [r_posb, r_vecs], [r_ang])
            V(lambda e: e.scalar_tensor_tensor(out=ang[:], in0=posb[:, 1, :], scalar=vecs[:, 39:40], in1=ang[:],
                                               op0=ALU.mult, op1=ALU.add), [r_posb, r_vecs, r_ang], [r_ang])
            with ExitStack() as tmp:
                sin_of(tmp, sinT[:], ang[:], [128, 512], [r_ang], [r_cs], 0.0, tag + "s")
                sin_of(tmp, cosT[:], ang[:], [128, 512], [r_ang], [r_cs], np.pi / 2, tag + "c")
                S.barrier()

        win_v = w_in.rearrange("(c p) n -> p c n", p=128)

        S.enabled = bool(PH['ffn1'])
        ffn(w1g, w1u, w1d, 0, "a")

        mix = ExitStack()
        zbuf = sbt(mix, "zbuf", [128, 4, NT], BF16); r_z = Res()
        S.enabled = bool(PH['mix1'])
        qT = sbt(mix, "qT", [128, 4, NT], BF16); r_q = Res()
        with ExitStack() as ph:
            cosT = sbt(ph, "cosT", [128, NT], F32); sinT = sbt(ph, "sinT", [128, NT], F32); r_cs = Res()
            with ExitStack() as tb0:
                posb = sbt(tb0, "posb", [128, 2, NT], F32); r_posb = Res(); ang = sbt(tb0, "ang", [128, NT], F32); r_ang = Res()
                d_pos = S.dma_sem()
                S.dma("sync", d_pos, posb[:, 0, :], pos[0:1, :].partition_broadcast(128)[:, 0, :], writes=[r_posb])
                S.dma("sync", d_pos, posb[:, 1, :], pos[1:2, :].partition_broadcast(128)[:, 0, :], writes=[r_posb])
                V(lambda e: e.tensor_scalar(out=ang[:], in0=posb[:, 0, :], scalar1=vecs[:, 38:39], scalar2=None, op0=ALU.mult),
                  [r_posb, r_vecs], [r_ang])
                V(lambda e: e.scalar_tensor_tensor(out=ang[:], in0=posb[:, 1, :], scalar=vecs[:, 39:40], in1=ang[:],
                                                   op0=ALU.mult, op1=ALU.add), [r_posb, r_vecs, r_ang], [r_ang])
                with ExitStack() as tb_:
                    sin_of(tb_, sinT[:], ang[:], [128, NT], [r_ang], [r_cs], 0.0, "rs")
                    S.barrier()
                with ExitStack() as tb_:
                    sin_of(tb_, cosT[:], ang[:], [128, NT], [r_ang], [r_cs], np.pi / 2, "rc")
                    S.barrier()
            tmpn = mk_norm_tmp(ph)
            xn = sbt(ph, "xn1", [128, 8, 512], BF16); r_xn = Res()
            wk = sbt(ph, "wkvx", [128, 8, 768], BF16); r_wk = Res(); d_wk = S.dma_sem()
            S.dma("gpsimd", d_wk, wk[:, :, 0:384], win_v[:, :, 512:896], writes=[r_wk])
            S.dma("gpsimd", d_wk, wk[:, :, 384:768], win_v[:, :, 896:1280], writes=[r_wk])
            wq = sbt(ph, "wq", [128, 8, 4, 128], BF16); r_wq = Res(); d_wq = S.dma_sem()
            for c in range(4):
                for hh in range(2):
                    col = (hh * 4 + c) * 64
                    S.dma("gpsimd", d_wq, wq[:, :, c, hh * 64:(hh + 1) * 64], win_v[:, :, col:col + 64], writes=[r_wq])
            kloc = sbt(ph, "kloc", [128, NT], BF16); r_kloc = Res()
            vloc = sbt(ph, "vloc", [128, 16, 194], BF16); r_vloc = Res()
            V(lambda e: e.memset(vloc[:], 0.0), [], [r_vloc])
            V(lambda e: e.memset(vloc[:, :, 64:65], 1.0), [], [r_vloc])
            V(lambda e: e.memset(vloc[:, :, 66:67], 1.0), [], [r_vloc])
            qkts = [(sbt(ph, "qk_sq%d" % i, [128, 512], BF16), Res(), sbt(ph, "qk_rs%d" % i, [128, 512], F32), Res(),
                     sbt(ph, "qk_qn%d" % i, [128, 512], F32), Res(), sbt(ph, "qk_t1%d" % i, [128, 512], F32), Res()) for i in range(2)]
            nq = 0
            for s4 in range(4):
                t0 = s4 * 512
                rmsnorm(ph, xn, r_xn, t0, 8, tmpn)
                ps, r_ps = nps()
                for c in range(8):
                    mm(ps[:, :], wk[:, c, 0:128], xn[:, c, :], c == 0, c == 7, [r_wk, r_xn], [r_ps])
                qk_finish(ps, r_ps, 33, cosT[:, t0:t0 + 512], sinT[:, t0:t0 + 512], r_cs, kloc[:, t0:t0 + 512], r_kloc, qkts[nq % 2]); nq += 1
                for tc in range(4):
                    pv, r_pv = nps()
                    for c in range(8):
                        mm(pv[:, 0:128], xn[:, c, tc * 128:(tc + 1) * 128], wk[:, c, 128:256], c == 0, c == 7, [r_wk, r_xn], [r_pv])
                    tcg = s4 * 4 + tc
                    V(lambda e, pv=pv, tcg=tcg: e.tensor_copy(out=vloc[:, tcg, 0:64], in_=pv[:, 0:64]), [r_pv], [r_vloc])
                    A(lambda e, pv=pv, tcg=tcg: e.activation(out=vloc[:, tcg, 130:194], in_=pv[:, 64:128], func=AF.Copy), [r_pv], [r_vloc])
                for cj in range(4):
                    px, r_px = nps()
                    for c in range(8):
                        mm(px[:, :], wk[:, c, 256 + cj * 128:256 + (cj + 1) * 128], xn[:, c, :], c == 0, c == 7, [r_wk, r_xn], [r_px])
                    A(lambda e, px=px, cj=cj, t0=t0: e.activation(out=zbuf[:, cj, t0:t0 + 512], in_=px[:, :], func=AF.Copy), [r_px], [r_z])
                for c4 in range(4):
                    ps, r_ps = nps()
                    for c in range(8):
                        mm(ps[:, :], wq[:, c, c4, :], xn[:, c, :], c == 0, c == 7, [r_wq, r_xn], [r_ps])
                    qk_finish(ps, r_ps, 32, cosT[:, t0:t0 + 512], sinT[:, t0:t0 + 512], r_cs, qT[:, c4, t0:t0 + 512], r_q, qkts[nq % 2]); nq += 1
                if s4 == 3:
                    pass
            d_kv = S.dma_sem(); d_kv2 = S.dma_sem(); r_kloc2 = Res(); r_vloc2 = Res(); r_kall = Res(); r_vall = Res()
            S.dma("sync", d_kv, k_loc, kloc[:], reads=[r_kloc], writes=[r_kloc2])
            S.dma("sync", d_kv2, v_loc, vloc[:].rearrange("p a b -> p (a b)"), reads=[r_vloc], writes=[r_vloc2])
            d_ag = S.dma_sem(); d_ag2 = S.dma_sem()
            S.coll(d_ag, lambda e: e.collective_compute("AllGather", ALU.bypass, replica_groups=[[0, 1, 2, 3], [4, 5, 6, 7]],
                                                        ins=[k_loc.opt()], outs=[k_all.opt()]), reads=[r_kloc2], writes=[r_kall])
            S.coll(d_ag2, lambda e: e.collective_compute("AllGather", ALU.bypass, replica_groups=[[0, 1, 2, 3], [4, 5, 6, 7]],
                                                         ins=[v_loc.opt()], outs=[v_all.opt()]), reads=[r_vloc2], writes=[r_vall])
            S.barrier()

        dump("kall", k_all, [512, 2048], [r_kall])
        dump("vall", v_all, [512, 16 * 194], [r_vall])
        dump("xssm", zbuf[:], [128, 4, NT], [r_z])
        dump("q", qT[:], [128, 4, NT], [r_q])
        d_sp0 = S.dma_sem(); r_hsp = Res()
        S.dma("sync", d_sp0, h_spill.rearrange("(c p) t -> p c t", p=128), hT[:], reads=[r_h], writes=[r_hsp])
        S.barrier()
        hflat = hT[:].rearrange("p c t -> p (c t)")
        with ExitStack() as ssm:
            r_s = Res()
            Cst = sbt(ssm, "Cst", [128, 32, 2, 128], BF16)
            Mop = sbt(ssm, "Mop", [128, 32, 128], BF16)
            lam8 = sbt(ssm, "lam8", [128, 2, 32], F32)
            ab = ExitStack()
            Bst = sbt(ab, "Bst", [128, 32, 2, 128], BF16)
            Bf = hflat[:, 0:4096].bitcast(BF16).rearrange("p (g r k) -> p g r k", g=32, r=2)
            CM = hflat[:, 4096:8192].bitcast(BF16).rearrange("p (g r k) -> p g r k", g=32, r=2)
            GH = hflat[:, 0:8256].bitcast(BF16).rearrange("p (r g c) -> p r g c", r=2, g=32)
            U = hflat[:, 8256:12352].bitcast(BF16).rearrange("p (g c) -> p g c", g=32)
            NS = 26
            S.enabled = bool(PH['ssm'] and SS['A'])
            with ExitStack() as sa:
                sp = sbt(sa, "ssmp", [128, 2144], F32); d_sp = S.dma_sem()
                S.dma("sync", d_sp, sp[:], ssmp_d, writes=[r_s])
                lamre = sp[:, 0:32]; lamim = sp[:, 32:64]; lstep = sp[:, 64:96]
                b3 = lambda a: a.rearrange("p (g h) -> p g h", h=16)
                bre = b3(sp[:, 96:608]); bim = b3(sp[:, 608:1120]); cre = b3(sp[:, 1120:1632]); cim = b3(sp[:, 1632:2144])
                def sm(name, n):
                    return sbt(sa, name, [128, n], F32)
                dl = sm("dl", 32); are = sm("are", 32); aim = sm("aim", 32)
                magE = sm("magE", 32 * NS); angE = sm("angE", 32 * NS); sinE = sm("sinE", 32 * NS); cosE = sm("cosE", 32 * NS)
                g3 = lambda a: a.rearrange("p (g s) -> p g s", s=NS)
                VV = lambda out, a, b, op: V(lambda e: e.tensor_tensor(out=out, in0=a, in1=b, op=op), [r_s, r_cf], [r_s])
                A(lambda e: e.activation(out=dl[:], in_=lstep, func=AF.Exp), [r_s], [r_s])
                VV(are[:], dl[:], lamre, ALU.mult); VV(aim[:], dl[:], lamim, ALU.mult)
                Eb = Etab.unsqueeze(1).broadcast_to([128, 32, NS])
                VV(g3(magE[:]), are[:].unsqueeze(2).broadcast_to([128, 32, NS]), Eb, ALU.mult)
                VV(g3(angE[:]), aim[:].unsqueeze(2).broadcast_to([128, 32, NS]), Eb, ALU.mult)
                A(lambda e: e.activation(out=magE[:], in_=magE[:], func=AF.Exp), [r_s], [r_s])
                with ExitStack() as tmp:
                    sin_of(tmp, sinE[:], angE[:], [128, 32 * NS], [r_s], [r_s], 0.0, "ss")
                    S.barrier()
                with ExitStack() as tmp:
                    sin_of(tmp, cosE[:], angE[:], [128, 32 * NS], [r_s], [r_s], np.pi / 2, "sc")
                    S.barrier()
                VV(cosE[:], cosE[:], magE[:], ALU.mult); VV(sinE[:], sinE[:], magE[:], ALU.mult)
                PWr = g3(cosE[:]); PWi = g3(sinE[:])
                V(lambda e: e.tensor_copy(out=lam8[:, 0, :], in_=PWr[:, :, 24]), [r_s], [r_s])
                V(lambda e: e.tensor_copy(out=lam8[:, 1, :], in_=PWi[:, :, 24]), [r_s], [r_s])
                nr = sm("nr", 32); ni = sm("ni", 32); den_ = sm("den_", 32); cr = sm("cr", 32); ci = sm("ci", 32); tq = sm("tq", 32)
                V(lambda e: e.tensor_scalar(out=nr[:], in0=PWr[:, :, 25], scalar1=-1.0, scalar2=None, op0=ALU.add), [r_s], [r_s])
                V(lambda e: e.tensor_copy(out=ni[:], in_=PWi[:, :, 25]), [r_s], [r_s])
                VV(den_[:], lamre, lamre, ALU.mult); VV(tq[:], lamim, lamim, ALU.mult); VV(den_[:], den_[:], tq[:], ALU.add)
                V(lambda e: e.reciprocal(out=den_[:], in_=den_[:]), [r_s], [r_s])
                VV(cr[:], nr[:], lamre, ALU.mult); VV(tq[:], ni[:], lamim, ALU.mult); VV(cr[:], cr[:], tq[:], ALU.add); VV(cr[:], cr[:], den_[:], ALU.mult)
                VV(ci[:], ni[:], lamre, ALU.mult); VV(tq[:], nr[:], lamim, ALU.mult); VV(ci[:], ci[:], tq[:], ALU.subtract); VV(ci[:], ci[:], den_[:], ALU.mult)
                bbr = sbt(sa, "bbr", [128, 32, 16], F32); bbi = sbt(sa, "bbi", [128, 32, 16], F32); tb = sbt(sa, "tb", [128, 32, 16], F32)
                crb = cr[:].unsqueeze(2).broadcast_to([128, 32, 16]); cib = ci[:].unsqueeze(2).broadcast_to([128, 32, 16])
                VV(bbr[:], bre, crb, ALU.mult); VV(tb[:], bim, cib, ALU.mult); VV(bbr[:], bbr[:], tb[:], ALU.subtract)
                VV(bbi[:], bim, crb, ALU.mult); VV(tb[:], bre, cib, ALU.mult); VV(bbi[:], bbi[:], tb[:], ALU.add)
                t1 = sbt(sa, "t1", [128, 8, 8, 16], F32); t2 = sbt(sa, "t2", [128, 8, 8, 16], F32)
                def table(dst, xr, xi, s0, negi):
                    for gq in range(4):
                        gs = slice(gq * 8, gq * 8 + 8)
                        xrb = xr[:, gs, :].unsqueeze(2).broadcast_to([128, 8, 8, 16]); xib = xi[:, gs, :].unsqueeze(2).broadcast_to([128, 8, 8, 16])
                        prb = PWr[:, gs, s0:s0 + 8].unsqueeze(3).broadcast_to([128, 8, 8, 16]); pib = PWi[:, gs, s0:s0 + 8].unsqueeze(3).broadcast_to([128, 8, 8, 16])
                        dr = dst[:, gs, 0, :].rearrange("p g (s h) -> p g s h", h=16); dim_ = dst[:, gs, 1, :].rearrange("p g (s h) -> p g s h", h=16)
                        VV(t1[:], xrb, prb, ALU.mult); VV(t2[:], xib, pib, ALU.mult); VV(dr, t1[:], t2[:], ALU.subtract)
                        VV(t1[:], xrb, pib, ALU.mult); VV(t2[:], xib, prb, ALU.mult)
                        if negi:
                            VV(t1[:], t1[:], t2[:], ALU.add)
                            V(lambda e, dim_=dim_: e.tensor_scalar(out=dim_, in0=t1[:], scalar1=-1.0, scalar2=None, op0=ALU.mult), [r_s], [r_s])
                        else:
                            VV(dim_, t1[:], t2[:], ALU.add)
                table(Bf, bbr[:], bbi[:], 0, False)
                table(CM, cre, cim, 8, True)
                table(Cst, cre, cim, 16, True)
                mt = sbt(sa, "mt", [128, 128], F32); mt2 = sbt(sa, "mt2", [128, 128], F32)
                for g in range(32):
                    pa, r_pa = nps(); pb_, r_pb_ = nps()
                    for ri in range(2):
                        mm(pa[:, 0:128], Bf[0:64, g, ri, :], CM[0:64, g, ri, :], ri == 0, ri == 1, [r_s], [r_pa])
                    for ri in range(2):
                        mm(pb_[:, 0:128], Bf[64:128, g, ri, :], CM[64:128, g, ri, :], ri == 0, ri == 1, [r_s], [r_pb_])
                    V(lambda e, pa=pa: e.tensor_tensor(out=mt[:], in0=pa[:, 0:128], in1=maskf, op=ALU.mult), [r_pa, r_cf, r_s], [r_s])
                    V(lambda e, pb_=pb_: e.tensor_tensor(out=mt2[:], in0=pb_[:, 0:128], in1=maskb, op=ALU.mult), [r_pb_, r_cf, r_s], [r_s])
                    VV(mt[:], mt[:], mt2[:], ALU.add)
                    V(lambda e, g=g: e.scalar_tensor_tensor(out=Mop[:, g, :], in0=identf, scalar=vecs[:, 40 + g:41 + g], in1=mt[:],
                                                          op0=ALU.mult, op1=ALU.add), [r_s, r_cf, r_vecs], [r_s])
                    for ri in range(2):
                        T(lambda e, g=g, ri=ri: e.transpose(out=pbf[:, ri * 128:(ri + 1) * 128], in_=Bf[:, g, ri, :], identity=identb), [r_s, r_cbf], [r_pbf])
                    A(lambda e, g=g: e.activation(out=Bst[:, g, :, :].rearrange("p r k -> p (r k)"), in_=pbf[:, 0:256], func=AF.Copy), [r_pbf], [r_s])
                S.barrier()
            S.enabled = bool(PH['ssm'])
            dump('Mop', Mop[:], [128, 32, 128], [r_s]); dump('Bst', Bst[:], [128, 32, 2, 128], [r_s]); dump('Cst', Cst[:], [128, 32, 2, 128], [r_s]); dump('lam8', lam8[:], [128, 2, 32], [r_s])
            S.enabled = bool(PH['ssm'] and SS['B'])
            with ExitStack() as sb_:
                selb = sbt(sb_, "selb", [128, 64, 128], BF16); d_sel = S.dma_sem(); r_selb = Res()
                r_Ug = [Res() for _ in range(32)]; r_ghv = Res(); r_gha = Res()
                for q4 in range(4):
                    S.dma("gpsimd", d_sel, selb[:, q4 * 16:(q4 + 1) * 16, :].rearrange("p a b -> p (a b)"), sel_d[:, q4 * 2048:(q4 + 1) * 2048], writes=[r_selb])
                for g in range(32):
                    kt, gl = g // 8, g % 8
                    pu_, r_pu_ = nps()
                    xv = zbuf[:, kt, :].rearrange("p (c i) -> p i c", i=8)
                    for i in range(8):
                        mm(pu_[:, 0:256], selb[:, gl * 8 + i, :], xv[:, i, :], i == 0, i == 7, [r_selb, r_z], [r_pu_])
                    V(lambda e, pu_=pu_, g=g: e.tensor_copy(out=U[:, g, :], in_=pu_[:, 0:256]), [r_pu_], [r_Ug[g]])
                for g in range(32):
                    for ri in range(2):
                        pg_, r_pg_ = nps()
                        mm(pg_[:, 0:256], Bst[:, g, ri, :], U[:, g, :], True, True, [r_Ug[g]], [r_pg_])
                        V(lambda e, pg_=pg_, g=g, ri=ri: e.tensor_copy(out=GH[0:64, ri, g, 2:258], in_=pg_[0:64, 0:256]), [r_pg_], [r_ghv])
                        A(lambda e, pg_=pg_, g=g, ri=ri: e.activation(out=GH[64:128, ri, g, 0:256], in_=pg_[64:128, 0:256], func=AF.Copy), [r_pg_], [r_gha])
                S.barrier()
            ab.close()
            S.enabled = bool(PH['ssm'])
            dump('U', U, [128, 32, 256], [r_s]); dump('GH0', GH, [128, 2, 32, 258], [r_s])
            S.enabled = bool(PH['ssm'] and SS['C'])
            kvst = ExitStack()
            S.enabled = bool(PH['attn'])
            Kall = sbt(kvst, "Kall", [128, 4 * NT], BF16); r_K = Res()
            Vall = sbt(kvst, "Vall", [128, 64, 194], BF16); r_V = Res()
            d_kl = S.dma_sem()
            for r in range(4):
                S.dma("sync", d_kl, Kall[:, r * NT:(r + 1) * NT], k_all[r * 128:(r + 1) * 128, :], reads=[r_kall], writes=[r_K])
                S.dma("sync", d_kl, Vall[:, r * 16:(r + 1) * 16, :].rearrange("p a b -> p (a b)"),
                      v_all[r * 128:(r + 1) * 128, :], reads=[r_vall], writes=[r_V])
            S.enabled = bool(PH['ssm'] and SS['C'])
            sc_ = ExitStack()
            if True:
                ENG = ['gpsimd']
                EO = lambda fn, r=(), w=(): S.op(ENG[0], fn, r, w)
                GG = lambda out, a, b, op: EO(lambda e: e.tensor_tensor(out=out, in0=a, in1=b, op=op), [r_s, r_cf], [r_s])
                St = [sbt(sc_, "St%d" % i, [128, 2, 32], F32) for i in range(2)]
                AR = sbt(sc_, "AR", [128, 2, 32], F32); AI = sbt(sc_, "AI", [128, 2, 32], F32)
                ta = sbt(sc_, "ta", [128, 2, 32], F32); tb2 = sbt(sc_, "tb2", [128, 2, 32], F32)
                EO(lambda e: e.tensor_copy(out=AR[:, 0, :], in_=lam8[:, 0, :]), [r_s], [r_s])
                EO(lambda e: e.tensor_copy(out=AR[:, 1, :], in_=lam8[:, 0, :]), [r_s], [r_s])
                EO(lambda e: e.tensor_scalar(out=AI[:, 0, :], in0=lam8[:, 1, :], scalar1=-1.0, scalar2=None, op0=ALU.mult), [r_s], [r_s])
                EO(lambda e: e.tensor_copy(out=AI[:, 1, :], in_=lam8[:, 1, :]), [r_s], [r_s])
                r_cur = [Res(), Res()]; r_ta = Res(); r_tb0 = Res(); r_tb1 = Res(); r_ghw = Res()
                def scan(write_hist):
                    prev = None
                    for s_ in range(256):
                        cur, nxt = St[s_ % 2], St[(s_ + 1) % 2]
                        rc, rn = r_cur[s_ % 2], r_cur[(s_ + 1) % 2]
                        cf_, cb_ = s_ + 2, 255 - s_
                        EO(lambda e, cur=cur: e.tensor_tensor(out=ta[:], in0=cur[:], in1=AR[:], op=ALU.mult), [rc, r_s], [r_ta])
                        EO(lambda e, cur=cur: e.tensor_tensor(out=tb2[:, 0, :], in0=cur[:, 1, :], in1=AI[:, 0, :], op=ALU.mult), [rc, r_s], [r_tb0])
                        EO(lambda e, cur=cur: e.tensor_tensor(out=tb2[:, 1, :], in0=cur[:, 0, :], in1=AI[:, 1, :], op=ALU.mult), [rc, r_s], [r_tb1])
                        if write_hist and prev is not None:
                            pcf, pcb = prev
                            EO(lambda e, cur=cur, pcf=pcf: e.tensor_copy(out=GH[0:64, :, :, pcf], in_=cur[0:64]), [rc], [r_ghw])
                            EO(lambda e, cur=cur, pcb=pcb: e.tensor_copy(out=GH[64:128, :, :, pcb], in_=cur[64:128]), [rc], [r_ghw])
                        EO(lambda e: e.tensor_tensor(out=ta[:], in0=ta[:], in1=tb2[:], op=ALU.add), [r_ta, r_tb0, r_tb1], [r_ta])
                        EO(lambda e, nxt=nxt, cf_=cf_: e.tensor_tensor(out=nxt[0:64], in0=ta[0:64], in1=GH[0:64, :, :, cf_], op=ALU.add), [r_ta], [rn])
                        EO(lambda e, nxt=nxt, cb_=cb_: e.tensor_tensor(out=nxt[64:128], in0=ta[64:128], in1=GH[64:128, :, :, cb_], op=ALU.add), [r_ta], [rn])
                        prev = (cf_, cb_)
                    if write_hist:
                        fin = St[0]; pcf, pcb = prev
                        EO(lambda e: e.tensor_copy(out=GH[0:64, :, :, pcf], in_=fin[0:64]), [r_cur[0]], [r_ghw])
                        EO(lambda e: e.tensor_copy(out=GH[64:128, :, :, pcb], in_=fin[64:128]), [r_cur[0]], [r_ghw])
                    EO(lambda e: e.tensor_copy(out=ta[:], in_=St[0][:]), [r_cur[0], r_cur[1], r_ta, r_tb0, r_tb1, r_ghw, r_s], [r_s, r_ta])
                EO(lambda e: e.memset(St[0][:], 0.0), [r_s], [r_s, r_cur[0]])
                scan(False)
                d_st = S.dma_sem(); d_st2 = S.dma_sem(); r_stl = Res(); r_sta = Res()
                S.dma("sync", d_st, st_loc, St[0][:].rearrange("p r g -> p (r g)"), reads=[r_s], writes=[r_stl])
                S.coll(d_st2, lambda e: e.collective_compute("AllGather", ALU.bypass, replica_groups=[[0, 1, 2, 3], [4, 5, 6, 7]],
                                                             ins=[st_loc.opt()], outs=[st_all.opt()]), reads=[r_stl], writes=[r_sta])
                Fall = sbt(sc_, "Fall", [128, 4, 2, 32], F32)
                S.dma("sync", d_st, Fall[:].rearrange("p r a g -> p r (a g)"), st_all.rearrange("(r p) f -> p r f", p=128), reads=[r_sta], writes=[r_s])
                Pw = sbt(sc_, "Pw", [128, 3, 2, 32], F32); tq2 = sbt(sc_, "tq2", [128, 32], F32); tq3 = sbt(sc_, "tq3", [128, 32], F32)
                EO(lambda e: e.tensor_copy(out=Pw[:, 1], in_=lam8[:]), [r_s], [r_s])
                def csq(dst, src):
                    GG(tq2[:], src[:, 0, :], src[:, 0, :], ALU.mult); GG(tq3[:], src[:, 1, :], src[:, 1, :], ALU.mult)
                    GG(tq3[:], tq2[:], tq3[:], ALU.subtract)
                    GG(tq2[:], src[:, 0, :], src[:, 1, :], ALU.mult)
                    EO(lambda e: e.tensor_scalar(out=dst[:, 1, :], in0=tq2[:], scalar1=2.0, scalar2=None, op0=ALU.mult), [r_s], [r_s])
                    EO(lambda e: e.tensor_copy(out=dst[:, 0, :], in_=tq3[:]), [r_s], [r_s])
                for _ in range(8):
                    csq(Pw[:, 1], Pw[:, 1])
                csq(Pw[:, 2], Pw[:, 1])
                EO(lambda e: e.memset(Pw[:, 0, 0, :], 1.0), [r_s], [r_s]); EO(lambda e: e.memset(Pw[:, 0, 1, :], 0.0), [r_s], [r_s])
                Sin_ = St[0]
                EO(lambda e: e.memset(Sin_[:], 0.0), [r_s], [r_s])
                for rr in range(4):
                    for k in range(3):
                        GG(ta[:, 0, :], Pw[:, k, 0, :], Fall[:, rr, 0, :], ALU.mult); GG(tb2[:, 0, :], Pw[:, k, 1, :], Fall[:, rr, 1, :], ALU.mult)
                        GG(ta[:, 0, :], ta[:, 0, :], tb2[:, 0, :], ALU.subtract)
                        GG(ta[:, 1, :], Pw[:, k, 0, :], Fall[:, rr, 1, :], ALU.mult); GG(tb2[:, 1, :], Pw[:, k, 1, :], Fall[:, rr, 0, :], ALU.mult)
                        GG(ta[:, 1, :], ta[:, 1, :], tb2[:, 1, :], ALU.add)
                        EO(lambda e, rr=rr, k=k: e.tensor_scalar(out=ta[:], in0=ta[:], scalar1=selc[:, rr * 3 + k:rr * 3 + k + 1], scalar2=None, op0=ALU.mult), [r_s, r_cf], [r_s])
                        GG(Sin_[:], Sin_[:], ta[:], ALU.add)
                EO(lambda e: e.tensor_copy(out=GH[0:64, :, :, 1], in_=Sin_[0:64]), [r_s], [r_s])
                EO(lambda e: e.tensor_copy(out=GH[64:128, :, :, 256], in_=Sin_[64:128]), [r_s], [r_s])
                ENG[0] = 'gpsimd'
                scan(True)
            S.enabled = bool(PH['attn'])
            with ExitStack() as ph:
                Pb = [(sbt(ph, "Pb%d" % i, [128, 1024], BF16), Res()) for i in range(3)]
                den = sbt(ph, "den", [128, 512], F32); r_den = Res()
                rec = sbt(ph, "rec", [128, 512], F32); r_rec = Res()
                qz = [[(sbt(ph, "qz%d%d" % (hh, b), [128, 512], BF16), Res()) for b in range(2)] for hh in range(2)]
                for hh in range(2):
                    for b in range(2):
                        V(lambda e, t_=qz[hh][b][0]: e.memset(t_[:], 0.0), [], [qz[hh][b][1]])
                pk = 0; hn = 0
                heads = [(qt, c4, hh) for qt in range(4) for c4 in range(4) for hh in range(2)]
                def qzcopy(n):
                    qt_, c4_, hh_ = heads[n]
                    qzt_, r_qz_ = qz[hh_][(n // 2) % 2]
                    lo_ = 64 * hh_
                    V(lambda e: e.tensor_copy(out=qzt_[lo_:lo_ + 64, :], in_=qT[lo_:lo_ + 64, c4_, qt_ * 512:(qt_ + 1) * 512]), [r_q], [r_qz_])
                qzcopy(0)
                for (qt, c4, hh) in heads:
                    if True:
                        if True:
                            q0 = qt * 512
                            lo = 64 * hh
                            qzt, r_qz = qz[hh][(hn // 2) % 2]
                            if hn + 1 < len(heads):
                                qzcopy(hn + 1)
                            po, r_po = pbanks[4 + (hn % 2)]; hn += 1
                            vcols = slice(0, 65) if hh == 0 else slice(66, 194)
                            M_ = 65 if hh == 0 else 128
                            def qkpair(j):
                                sc, r_sc = pbig[j % 2]
                                for u_ in range(2):
                                    kc = 2 * j + u_
                                    mm(sc[:, u_ * 512:(u_ + 1) * 512], Kall[:, kc * 128:(kc + 1) * 128], qzt[:], True, True, [r_K, r_qz], [r_sc])
                                return sc, r_sc
                            cur = qkpair(0)
                            for j in range(32):
                                nxt = qkpair(j + 1) if j < 31 else None
                                sc, r_sc = cur
                                pb, r_pb = Pb[pk % 3]; pk += 1
                                A(lambda e, pb=pb, sc=sc: e.activation(out=pb[:], in_=sc[:, :], func=AF.Exp, scale=0.125), [r_sc], [r_pb])
                                for u_ in range(2):
                                    kc = 2 * j + u_
                                    mm(po[0:M_, :], Vall[:, kc, vcols], pb[:, u_ * 512:(u_ + 1) * 512], kc == 0, kc == 63, [r_V, r_pb], [r_po])
                                cur = nxt
                            dp = 64 if hh == 0 else 0
                            V(lambda e, po=po, dp=dp: e.tensor_copy(out=den[dp:dp + 1, :], in_=po[dp:dp + 1, :]), [r_po], [r_den])
                            pbc, r_pbc = pbanks[6]
                            mm(pbc[:, :], onesf[dp:dp + 1, :], den[dp:dp + 1, :], True, True, [r_den, r_cf], [r_pbc])
                            V(lambda e, pbc=pbc: e.reciprocal(out=rec[:], in_=pbc[:, :]), [r_pbc], [r_rec])
                            V(lambda e, po=po, lo=lo, c4=c4, q0=q0: e.tensor_tensor(out=qT[lo:lo + 64, c4, q0:q0 + 512], in0=po[lo:lo + 64, :],
                                                                                    in1=rec[lo:lo + 64, :], op=ALU.mult), [r_po, r_rec], [r_q])
                S.barrier()

            S.enabled = bool(PH['ssm'])
            sc_.close()
            kvst.close()
            S.enabled = bool(PH['ssm'])
            dump('GH1', GH, [128, 2, 32, 258], [r_s])
            S.enabled = bool(PH['ssm'] and SS['D'])
            with ExitStack() as so:
                for g in range(32):
                    py_, r_py_ = nps()
                    mm(py_[:, 0:256], Mop[:, g, :], U[:, g, :], True, False, [r_Ug[g]], [r_py_])
                    for ri in range(2):
                        mm(py_[:, 0:256], Cst[:, g, ri, :], GH[:, ri, g, 1:257], False, ri == 1, [r_Ug[g]], [r_py_])
                    A(lambda e, py_=py_, g=g: e.activation(out=U[:, g, :], in_=py_[:, 0:256], func=AF.Gelu_apprx_tanh), [r_py_], [r_Ug[g]])
                selTb = sbt(so, "selTb", [128, 64, 128], BF16); d_selT = S.dma_sem(); r_selT = Res()
                for q4 in range(4):
                    S.dma("gpsimd", d_selT, selTb[:, q4 * 16:(q4 + 1) * 16, :].rearrange("p a b -> p (a b)"), selT_d[:, q4 * 2048:(q4 + 1) * 2048], writes=[r_selT])
                Z1 = sbt(so, "Z1", [128, 4, NT], BF16); r_Z1 = Res()
                for kt in range(4):
                    zv = Z1[:, kt, :].rearrange("p (c j) -> p j c", j=8)
                    for j in range(8):
                        pz, r_pz = nps()
                        for gl in range(8):
                            mm(pz[:, 0:256], selTb[:, gl * 8 + j, :], U[:, kt * 8 + gl, :], gl == 0, gl == 7, [r_selT, r_Ug[kt * 8 + gl]], [r_pz])
                        V(lambda e, pz=pz, zv=zv, j=j: e.tensor_copy(out=zv[:, j, :], in_=pz[:, 0:256]), [r_pz], [r_Z1])
                wglu_t = sbt(so, "wglu_t", [128, 4, 512], BF16); d_wgl = S.dma_sem(); r_wgl = Res()
                S.dma("gpsimd", d_wgl, wglu_t[:], wglu.rearrange("(c p) n -> p c n", p=128), writes=[r_wgl])
                sgls = [(sbt(so, "sgl%d" % i, [128, 512], F32), Res()) for i in range(2)]
                ng = 0
                for ot in range(4):
                    for s4 in range(4):
                        t0 = s4 * 512
                        pg2, r_pg2 = nps()
                        for kt in range(4):
                            mm(pg2[:, :], wglu_t[:, kt, ot * 128:(ot + 1) * 128], Z1[:, kt, t0:t0 + 512], kt == 0, kt == 3, [r_wgl, r_Z1], [r_pg2])
                        sgl, r_sgl = sgls[ng % 2]; ng += 1
                        A(lambda e, pg2=pg2, ot=ot, sgl=sgl: e.activation(out=sgl[:], in_=pg2[:, :], func=AF.Sigmoid, bias=vecs[:, 34 + ot:35 + ot]), [r_pg2, r_vecs], [r_sgl])
                        V(lambda e, ot=ot, t0=t0, sgl=sgl: e.tensor_tensor(out=zbuf[:, ot, t0:t0 + 512], in0=Z1[:, ot, t0:t0 + 512], in1=sgl[:], op=ALU.mult), [r_sgl, r_Z1], [r_z])
                S.barrier()
            S.enabled = bool(PH['ssm'])
            S.barrier()

        dump("attn", qT[:], [128, 4, NT], [r_q])
        dump("z", zbuf[:], [128, 4, NT], [r_z])
        d_rl = S.dma_sem()
        S.dma("sync", d_rl, hT[:], h_spill.rearrange("(c p) t -> p c t", p=128), reads=[r_hsp], writes=[r_h])
        S.enabled = bool(PH['merge'])
        with ExitStack() as ph:
            tmpn = mk_norm_tmp(ph)
            xn = sbt(ph, "xn3", [128, 8, 512], BF16); r_xn = Res()
            wab_t = sbt(ph, "wab_t", [128, 4, DM], BF16); r_wab = Res(); d_w3 = S.dma_sem()
            for c4 in range(4):
                S.dma("gpsimd", d_w3, wab_t[0:64, c4, :], wab[64 * c4:64 * c4 + 64, :], writes=[r_wab])
                S.dma("gpsimd", d_w3, wab_t[64:128, c4, :], wab[256 + 64 * c4:256 + 64 * c4 + 64, :], writes=[r_wab])
            wsb_t = sbt(ph, "wsb_t", [128, 4, DM], BF16); r_wsb = Res()
            S.dma("gpsimd", d_w3, wsb_t[:], wsb.rearrange("(c p) n -> p c n", p=128), writes=[r_wsb])
            wout_t = sbt(ph, "wout_t", [128, 8, DM], BF16); r_wout = Res()
            wout_v = wout.rearrange("(c p) n -> p c n", p=128)
            S.dma("gpsimd", d_w3, wout_t[:, 0:4, :], wout_v[:, 0:4, :], writes=[r_wout])
            S.dma("gpsimd", d_w3, wout_t[:, 4:8, :], wout_v[:, 4:8, :], writes=[r_wout])
            wgs = WStream(ph, "wgate", [128, 8, 2, 128])
            mrg = sbt(ph, "mrg", [128, 8, 512], BF16); r_mrg = Res()
            sa = sbt(ph, "sa", [128, 512], F32); r_sa = Res(); ss = sbt(ph, "ss", [128, 512], F32); r_ss = Res()
            for s4 in range(4):
                t0 = s4 * 512
                rmsnorm(ph, xn, r_xn, t0, 8, tmpn)
                for o in range(8):
                    wg_ = wgs.next()
                    wload(wg_, wg_[0][:, :, 0, :], win_v[:, :, 1280 + o * 128:1280 + (o + 1) * 128])
                    wload(wg_, wg_[0][:, :, 1, :], win_v[:, :, 2304 + o * 128:2304 + (o + 1) * 128])
                    pga, r_pga = nps(); pya, r_pya = nps(); pgs, r_pgs = nps(); pys, r_pys = nps()
                    for c in range(8):
                        mm(pga[:, :], wg_[0][:, c, 0, :], xn[:, c, :], c == 0, c == 7, [wg_[1], r_xn], [r_pga])
                    for c in range(4):
                        mm(pya[:, :], wab_t[:, c, o * 128:(o + 1) * 128], qT[:, c, t0:t0 + 512], c == 0, c == 3, [r_wab, r_q], [r_pya])
                    for c in range(8):
                        mm(pgs[:, :], wg_[0][:, c, 1, :], xn[:, c, :], c == 0, c == 7, [wg_[1], r_xn], [r_pgs])
                    for c in range(4):
                        mm(pys[:, :], wsb_t[:, c, o * 128:(o + 1) * 128], zbuf[:, c, t0:t0 + 512], c == 0, c == 3, [r_wsb, r_z], [r_pys])
                    A(lambda e, pga=pga: e.activation(out=sa[:], in_=pga[:, :], func=AF.Sigmoid), [r_pga], [r_sa])
                    A(lambda e, pgs=pgs: e.activation(out=ss[:], in_=pgs[:, :], func=AF.Sigmoid), [r_pgs], [r_ss])
                    V(lambda e, pya=pya: e.tensor_tensor(out=sa[:], in0=sa[:], in1=pya[:, :], op=ALU.mult), [r_sa, r_pya], [r_sa])
                    V(lambda e, pys=pys: e.tensor_tensor(out=ss[:], in0=ss[:], in1=pys[:, :], op=ALU.mult), [r_ss, r_pys], [r_ss])
                    V(lambda e, o=o: e.tensor_tensor(out=mrg[:, o, :], in0=sa[:], in1=ss[:], op=ALU.add), [r_sa, r_ss], [r_mrg])
                for o2 in range(8):
                    py, r_py = nps()
                    for o in range(8):
                        mm(py[:, :], wout_t[:, o, o2 * 128:(o2 + 1) * 128], mrg[:, o, :], o == 0, o == 7, [r_wout, r_mrg], [r_py])
                    V(lambda e, py=py, o2=o2, t0=t0: e.tensor_tensor(out=hT[:, o2, t0:t0 + 512], in0=py[:, :], in1=hT[:, o2, t0:t0 + 512], op=ALU.add),
                      [r_py, r_h], [r_h])
            S.barrier()

        mix.close()
        plst = ExitStack()
        wpg_t = sbt(plst, "wpg_t", [128, 8, DM], BF16); r_wpg = Res(); d_w4 = S.dma_sem()
        wpp_t = sbt(plst, "wpp_t", [128, 2, DM], BF16); r_wpp = Res()
        pb16 = sbt(plst, "pb16", [128, 2, NT], BF16); r_p16 = Res()
        def ple_loads():
            en = S.enabled; S.enabled = bool(PH['ple'])
            wpg_v = wpg.rearrange("(c p) n -> p c n", p=128)
            S.dma("gpsimd", d_w4, wpg_t[:, 0:4, :], wpg_v[:, 0:4, :], writes=[r_wpg])
            S.dma("gpsimd", d_w4, wpg_t[:, 4:8, :], wpg_v[:, 4:8, :], writes=[r_wpg])
            S.dma("gpsimd", d_w4, wpp_t[:], wpp.rearrange("(c p) n -> p c n", p=128), writes=[r_wpp])
            pT_v = pT.rearrange("(c p) t -> p c t", p=128)
            for c in range(2):
                S.dma("gpsimd", d_w4, pb16[:, c, :], pT_v[:, c, :], writes=[r_p16])
            S.enabled = en
        S.enabled = bool(PH['ffn2'])
        ffn(w2g, w2u, w2d, 16, "b", hook=ple_loads)
        if not PH['ffn2']:
            ple_loads()

        S.enabled = bool(PH['ple'])
        with ExitStack() as ph:
            tmpn = mk_norm_tmp(ph)
            xn = sbt(ph, "xn4", [128, 8, 512], BF16); r_xn = Res()
            sgp = sbt(ph, "sgp", [128, 512], F32); r_sgp = Res()
            d_out = S.dma_sem()
            for s4 in range(4):
                t0 = s4 * 512
                rmsnorm(ph, xn, r_xn, t0, 24, tmpn)
                for o in range(8):
                    pg, r_pg = nps(); pl, r_pl = nps()
                    for c in range(8):
                        mm(pg[:, :], wpg_t[:, c, o * 128:(o + 1) * 128], xn[:, c, :], c == 0, c == 7, [r_wpg, r_xn], [r_pg])
                    for c in range(2):
                        mm(pl[:, :], wpp_t[:, c, o * 128:(o + 1) * 128], pb16[:, c, t0:t0 + 512], c == 0, c == 1, [r_wpp, r_p16], [r_pl])
                    A(lambda e, pg=pg: e.activation(out=sgp[:], in_=pg[:, :], func=AF.Sigmoid), [r_pg], [r_sgp])
                    V(lambda e, pl=pl: e.tensor_tensor(out=sgp[:], in0=sgp[:], in1=pl[:, :], op=ALU.mult), [r_sgp, r_pl], [r_sgp])
                    V(lambda e, o=o, t0=t0: e.tensor_tensor(out=hT[:, o, t0:t0 + 512], in0=sgp[:], in1=hT[:, o, t0:t0 + 512], op=ALU.add),
                      [r_sgp, r_h], [r_h])
                S.dma("sync", d_out, outT.rearrange("(c p) t -> p c t", p=128)[:, :, t0:t0 + 512], hT[:, :, t0:t0 + 512], reads=[r_h])
            S.barrier()
        plst.close()
        S.enabled = True
        if not PH['ple']:
            d_o2 = S.dma_sem()
            S.dma("sync", d_o2, outT.rearrange("(c p) t -> p c t", p=128), hT[:], reads=[r_h])
        for e_ in S.ENGS:
            S.wait_all(e_)
        S.emit()
    return nc


_NC_CACHE = {}


def _consts():
    f = np.float32
    ident = np.eye(128, dtype=f)
    rot = np.zeros((128, 128), f)
    for m in range(128):
        d = m % 64
        if d < 32:
            rot[m + 32, m] = -1.0
        else:
            rot[m - 32, m] = 1.0
    ones = np.ones((128, 128), f)
    ii = np.arange(128) // 16
    maskf = (ii[None, :] >= ii[:, None]).astype(f)
    maskb = (ii[:, None] >= ii[None, :]).astype(f)
    E = np.zeros((128, 26), f)
    s = np.arange(8)
    E[:64, 0:8] = 7 - s; E[64:, 0:8] = s
    E[:64, 8:16] = s - 7; E[64:, 8:16] = -s
    E[:64, 16:24] = s + 1; E[64:, 16:24] = 8 - s
    E[:, 24] = 8; E[:, 25] = 1
    sel = np.zeros((64, 128, 128), f); selT = np.zeros((64, 128, 128), f)
    for gl in range(8):
        for i in range(8):
            for h in range(16):
                sel[gl * 8 + i, gl * 16 + h, i * 16 + h] = 1.0
                selT[gl * 8 + i, i * 16 + h, gl * 16 + h] = 1.0
    sel = np.ascontiguousarray(sel.transpose(1, 0, 2)).reshape(128, 64 * 128)
    selT = np.ascontiguousarray(selT.transpose(1, 0, 2)).reshape(128, 64 * 128)
    return ident, rot, ones, maskf, maskb, E, sel, selT


def kernel(x, p, ffn1_norm, ffn1_w_gate, ffn1_w_up, ffn1_w_down, mix_norm, w_in,
           q_norm, k_norm, ssm_lambda_re, ssm_lambda_im, ssm_log_step, ssm_b_re,
           ssm_b_im, ssm_c_re, ssm_c_im, ssm_d, ssm_glu_w, ssm_glu_b, w_attn_branch,
           w_ssm_branch, w_out, ffn2_norm, ffn2_w_gate, ffn2_w_up, ffn2_w_down,
           ple_norm, ple_w_gate, ple_w_proj):
    f = np.float32
    A_ = lambda a: np.ascontiguousarray(np.asarray(a, dtype=f))
    x = A_(x); p = A_(p)
    if "nc" not in _NC_CACHE:
        _NC_CACHE["nc"] = build()
    nc = _NC_CACHE["nc"]
    ident, rot, ones, maskf, maskb, E, sel, selT = _consts()
    vecs = np.zeros((128, 72), f)
    pc = lambda v: A_(v).reshape(8, 128).T
    vecs[:, 0:8] = pc(ffn1_norm[0]); vecs[:, 8:16] = pc(mix_norm[0]); vecs[:, 16:24] = pc(ffn2_norm[0]); vecs[:, 24:32] = pc(ple_norm[0])
    vecs[:, 32] = np.tile(A_(q_norm[0]), 2); vecs[:, 33] = np.tile(A_(k_norm[0]), 2)
    vecs[:, 34:38] = A_(ssm_glu_b[0]).reshape(4, 128).T
    freqs = (10000.0 ** (-np.arange(0, 32, 2, dtype=np.float32) / 32)).astype(f)
    for pp in range(128):
        dd = (pp % 64) % 32
        if dd < 16:
            vecs[pp, 38] = freqs[dd]
        else:
            vecs[pp, 39] = freqs[dd - 16]
    vecs[:, 40:72] = np.tile(A_(ssm_d[0]).reshape(32, 16).T, (8, 1))
    def dn(a):
        return A_(a).transpose(0, 2, 1).reshape(128, 32)
    lamre = dn(ssm_lambda_re[0]); lamim = dn(ssm_lambda_im[0])
    lstep = np.repeat(A_(ssm_log_step[0])[:, None, :], 64, axis=1).reshape(128, 32)
    bre = A_(ssm_b_re[0]).transpose(0, 2, 1, 3).reshape(128, 512)
    bim = A_(ssm_b_im[0]).transpose(0, 2, 1, 3).reshape(128, 512)
    cre = A_(ssm_c_re[0]).transpose(0, 3, 1, 2).reshape(128, 512)
    cim = A_(ssm_c_im[0]).transpose(0, 3, 1, 2).reshape(128, 512)
    ssmp = np.ascontiguousarray(np.concatenate([lamre, lamim, lstep, bre, bim, cre, cim], axis=1))
    shared = dict(vecs=vecs, sel=sel, selT=selT, ssmp=ssmp,
                  w1g=A_(ffn1_w_gate[0]), w1u=A_(ffn1_w_up[0]), w1d=A_(ffn1_w_down[0]),
                  w2g=A_(ffn2_w_gate[0]), w2u=A_(ffn2_w_up[0]), w2d=A_(ffn2_w_down[0]),
                  w_in=A_(w_in[0]), wglu=A_(ssm_glu_w[0]), wab=A_(w_attn_branch[0]), wsb=A_(w_ssm_branch[0]),
                  wout=A_(w_out[0]), wpg=A_(ple_w_gate[0]), wpp=A_(ple_w_proj[0]))
    in_maps = []
    for r in range(8):
        b, q = r // 4, r % 4
        t0 = q * NT
        tok = np.arange(t0, t0 + NT)
        pos = np.stack([(tok // 64).astype(f), (tok % 64).astype(f)], axis=0)
        selc = np.zeros((128, 12), f)
        for rr in range(4):
            if rr < q:
                selc[:64, rr * 3 + (q - 1 - rr)] = 1.0
            if rr > q:
                selc[64:, rr * 3 + (rr - q - 1)] = 1.0
        cf32 = np.ascontiguousarray(np.concatenate([ident, rot, ones, maskf, maskb, E, selc], axis=1))
        m = dict(shared)
        m["xT"] = np.ascontiguousarray(x[b, t0:t0 + NT, :].T)
        m["pT"] = np.ascontiguousarray(p[0, b, t0:t0 + NT, :].T)
        m["pos"] = np.ascontiguousarray(pos)
        m["cf32"] = cf32
        in_maps.append(m)
    res = run_bass_kernel_spmd(nc, in_maps, core_ids=list(range(8)), **({'trace': True} if DEBUG.get('trace') else {}))
    DEBUG['last'] = res
    out = np.zeros((2, 8192, DM), f)
    for r in range(8):
        b, q = r // 4, r % 4
        out[b, q * NT:(q + 1) * NT, :] = np.asarray(res.results[r]["outT"]).T
    return out
```
